# Optimizing a Trainium2 kernel written in Bass

```python
import math
import jax
import jax.numpy as jnp
from jax import lax
import numpy as np

D_MODEL = 1024
BATCH = 8
SEQ = 2048
DEPTH = 4

CHUNK = 64
N_META = 16
EPS = 1e-6
N_BRANCH = 3

A_HEAD = 64
A_WIDTH = D_MODEL
A_HEADS = A_WIDTH // A_HEAD
A_DECAY_LORA = 64
A_ICLR_LORA = 64
A_VRES_LORA = 32
A_GATE_LORA = 128
A_LN_EPS = 64e-5
A_IN = 3 * A_WIDTH + A_DECAY_LORA + A_ICLR_LORA + A_GATE_LORA

B_WIDTH = 2 * D_MODEL
B_HEAD = 64
B_HEADS = B_WIDTH // B_HEAD
B_GROUPS = 4
B_HEADS_PER_GROUP = B_HEADS // B_GROUPS
B_STATE = 128
B_CONV = 4
B_CONV_DIM = B_WIDTH + 2 * B_GROUPS * B_STATE
B_IN = B_WIDTH + B_CONV_DIM + B_HEADS

C_HEADS = 8
C_QK_HEAD = 64
C_V_HEAD = 128
C_QK = C_HEADS * C_QK_HEAD
C_WIDTH = C_HEADS * C_V_HEAD
C_IN = 2 * C_QK + 2 * C_WIDTH
ROPE_BASE = 10000.0

GATE_IN = N_BRANCH * D_MODEL
W_IN = A_IN + B_IN + C_IN + GATE_IN
MIX_WIDTH = A_WIDTH + B_WIDTH + C_WIDTH

FFN_HIDDEN = 2816
FFN_CONV = 3

kernel_name = "hybrid_rwkv7_ssd_retention_convffn"


def split_last(t, sizes):
    offs = np.cumsum([0] + list(sizes)).tolist()
    return [t[..., offs[i]:offs[i + 1]] for i in range(len(sizes))]


def rms_norm(x, g):
    xf = x.astype(jnp.float32)
    y = xf * lax.rsqrt(jnp.mean(xf * xf, axis=-1, keepdims=True) + EPS)
    return (y * g.astype(jnp.float32)).astype(x.dtype)


def head_norm(y, eps):
    yc = y - jnp.mean(y, axis=-1, keepdims=True)
    return yc * lax.rsqrt(jnp.mean(yc * yc, axis=-1, keepdims=True) + eps)


def token_shift(p, mu):
    prev = jnp.pad(p, ((0, 0), (1, 0), (0, 0)))[:, :-1]
    return p + (prev - p) * mu


def causal_dwconv(u, w, b):
    width, ch = w.shape
    out = lax.conv_general_dilated(u, w[:, None, :].astype(u.dtype), window_strides=(1,),
                                   padding=[(width - 1, 0)],
                                   dimension_numbers=("NWC", "WIO", "NWC"),
                                   feature_group_count=ch)
    return out + b.astype(u.dtype)


def front_pad_chunks(t):
    pad = (-t.shape[1]) % CHUNK
    return jnp.pad(t, [(0, 0), (pad, 0)] + [(0, 0)] * (t.ndim - 2)), pad


def rotary(t, cos, sin):
    c, s = cos[None, :, None, :], sin[None, :, None, :]
    t1, t2 = t[..., 0::2], t[..., 1::2]
    return jnp.stack([t1 * c - t2 * s, t2 * c + t1 * s], axis=-1).reshape(t.shape)


def rwkv7_mix(p, mu, w0, w2, a0, a2, g2, k_k, k_a, r_k, ln_g, ln_b, v_first, vres):
    bsz, L, _ = p.shape
    f32 = jnp.float32
    ps = token_shift(p, mu)
    r, k, v, w_lo, a_lo, g_lo = split_last(ps, [A_WIDTH] * 3 + [A_DECAY_LORA, A_ICLR_LORA, A_GATE_LORA])
    w_log = -jax.nn.softplus(-(w0 + jnp.tanh(w_lo) @ w2)) - 0.5
    decay = jnp.exp(-jnp.exp(w_log.astype(f32)))
    a = jax.nn.sigmoid(a0 + a_lo @ a2)
    g = jax.nn.sigmoid(g_lo) @ g2
    if vres is None:
        v_first = v
    else:
        p_v, mu_v, v0, v2 = vres
        v = v + (v_first - v) * jax.nn.sigmoid(v0 + token_shift(p_v, mu_v) @ v2)

    def heads(t):
        return t.reshape(bsz, L, A_HEADS, A_HEAD).astype(f32)

    kk = heads(k * k_k)
    kk = kk * lax.rsqrt(jnp.maximum(jnp.sum(kk * kk, axis=-1, keepdims=True), 1e-24))
    a_h = heads(a)
    k_h = heads(k * (1.0 + (a - 1.0) * k_a))
    r_h, v_h, w_h = heads(r), heads(v), heads(decay)

    def step(state, inp):
        r_t, w_t, k_t, v_t, kk_t, a_t = inp
        sa = jnp.einsum("bhvk,bhk->bhv", state, -kk_t)
        state = (state * w_t[:, :, None, :] + sa[..., None] * (kk_t * a_t)[:, :, None, :]
                 + v_t[..., None] * k_t[:, :, None, :])
        return state, jnp.einsum("bhvk,bhk->bhv", state, r_t)

    xs = tuple(jnp.moveaxis(t, 1, 0) for t in (r_h, w_h, k_h, v_h, kk, a_h))
    s0 = jnp.zeros((bsz, A_HEADS, A_HEAD, A_HEAD), f32)
    _, y = lax.scan(step, s0, xs)
    y = jnp.moveaxis(y, 0, 1)
    y = head_norm(y, A_LN_EPS).reshape(bsz, L, A_WIDTH) * ln_g + ln_b
    bonus = jnp.sum(r_h * k_h * r_k, axis=-1, keepdims=True) * v_h
    y = (y + bonus.reshape(bsz, L, A_WIDTH)) * g
    return y.astype(p.dtype), v_first


def ssd_chunked(xdt, a, bm, cm):
    bsz = xdt.shape[0]
    xdt, pad = front_pad_chunks(xdt)
    a, _ = front_pad_chunks(a)
    bm, _ = front_pad_chunks(bm)
    cm, _ = front_pad_chunks(cm)
    nc = xdt.shape[1] // CHUNK
    x = xdt.reshape(bsz, nc, CHUNK, B_GROUPS, B_HEADS_PER_GROUP, B_HEAD)
    a = a.reshape(bsz, nc, CHUNK, B_GROUPS, B_HEADS_PER_GROUP)
    bm = bm.reshape(bsz, nc, CHUNK, B_GROUPS, B_STATE)
    cm = cm.reshape(bsz, nc, CHUNK, B_GROUPS, B_STATE)
    a_cs = jnp.cumsum(a, axis=2)
    causal = jnp.tril(jnp.ones((CHUNK, CHUNK), dtype=bool))
    seg = a_cs[:, :, :, None] - a_cs[:, :, None, :]
    lmat = jnp.exp(jnp.where(causal[None, None, :, :, None, None], seg, -jnp.inf))
    cb = jnp.einsum("bclgn,bcsgn->bclsg", cm, bm)
    y_diag = jnp.einsum("bclsg,bclsgr,bcsgrp->bclgrp", cb, lmat, x)
    decay_to_end = jnp.exp(a_cs[:, :, -1:] - a_cs)
    chunk_states = jnp.einsum("bclgn,bclgr,bclgrp->cbgrpn", bm, decay_to_end, x)
    chunk_decay = jnp.moveaxis(jnp.exp(a_cs[:, :, -1]), 1, 0)

    def step(h, inp):
        s_c, d_c = inp
        return h * d_c[..., None, None] + s_c, h

    h0 = jnp.zeros((bsz, B_GROUPS, B_HEADS_PER_GROUP, B_HEAD, B_STATE), x.dtype)
    _, h_in = lax.scan(step, h0, (chunk_states, chunk_decay))
    y_off = jnp.einsum("bclgn,cbgrpn,bclgr->bclgrp", cm, h_in, jnp.exp(a_cs))
    y = (y_diag + y_off).reshape(bsz, nc * CHUNK, B_HEADS, B_HEAD)
    return y[:, pad:]


def mamba2_mix(p, conv_w, conv_b, dt_bias, a_log, d_skip, norm_g):
    bsz, L, _ = p.shape
    f32 = jnp.float32
    z, xbc, dt_raw = split_last(p, [B_WIDTH, B_CONV_DIM, B_HEADS])
    xbc = jax.nn.silu(causal_dwconv(xbc, conv_w, conv_b))
    xs, bm, cm = split_last(xbc, [B_WIDTH, B_GROUPS * B_STATE, B_GROUPS * B_STATE])
    dt = jax.nn.softplus((dt_raw + dt_bias).astype(f32))
    a_neg = -jnp.exp(a_log.astype(f32))
    xh = xs.reshape(bsz, L, B_HEADS, B_HEAD).astype(f32)
    y = ssd_chunked(xh * dt[..., None], dt * a_neg,
                    bm.reshape(bsz, L, B_GROUPS, B_STATE).astype(f32),
                    cm.reshape(bsz, L, B_GROUPS, B_STATE).astype(f32))
    y = y + xh * d_skip.astype(f32)[:, None]
    y = y.reshape(bsz, L, B_WIDTH) * jax.nn.silu(z.astype(f32))
    yg = y.reshape(bsz, L, B_GROUPS, B_WIDTH // B_GROUPS)
    yg = yg * lax.rsqrt(jnp.mean(yg * yg, axis=-1, keepdims=True) + EPS)
    return (yg.reshape(bsz, L, B_WIDTH) * norm_g.astype(f32)).astype(p.dtype)


def retention_mix(p, cos, sin):
    bsz, L, _ = p.shape
    f32 = jnp.float32
    q, k, v, g = split_last(p, [C_QK, C_QK, C_WIDTH, C_WIDTH])
    q = rotary(q.reshape(bsz, L, C_HEADS, C_QK_HEAD).astype(f32), cos, sin)
    k = rotary(k.reshape(bsz, L, C_HEADS, C_QK_HEAD).astype(f32), cos, sin) * (C_QK_HEAD ** -0.5)
    v = v.reshape(bsz, L, C_HEADS, C_V_HEAD).astype(f32)
    q, pad = front_pad_chunks(q)
    k, _ = front_pad_chunks(k)
    v, _ = front_pad_chunks(v)
    nc = q.shape[1] // CHUNK
    q = q.reshape(bsz, nc, CHUNK, C_HEADS, C_QK_HEAD)
    k = k.reshape(bsz, nc, CHUNK, C_HEADS, C_QK_HEAD)
    v = v.reshape(bsz, nc, CHUNK, C_HEADS, C_V_HEAD)
    log_gamma = jnp.log(1.0 - 2.0 ** (-5.0 - jnp.arange(C_HEADS, dtype=f32)))
    idx = jnp.arange(CHUNK, dtype=f32)
    rel = idx[:, None] - idx[None, :]
    d_intra = jnp.where(rel >= 0, jnp.exp(jnp.maximum(rel, 0.0) * log_gamma[:, None, None]), 0.0)
    scores = jnp.einsum("bclhd,bcshd->bchls", q, k) * d_intra
    y_intra = jnp.einsum("bchls,bcshe->bclhe", scores, v)
    k_decay = jnp.exp((CHUNK - 1.0 - idx)[None, :] * log_gamma[:, None])
    kv = jnp.einsum("bclhd,bclhe,hl->cbhde", k, v, k_decay)
    chunk_decay = jnp.exp(CHUNK * log_gamma)[:, None, None]

    def step(s, kv_c):
        return s * chunk_decay + kv_c, s

    s0 = jnp.zeros((bsz, C_HEADS, C_QK_HEAD, C_V_HEAD), f32)
    _, s_in = lax.scan(step, s0, kv)
    q_decay = jnp.exp((idx + 1.0)[None, :] * log_gamma[:, None])
    y_cross = jnp.einsum("bclhd,cbhde,hl->bclhe", q, s_in, q_decay)
    y = (y_intra + y_cross).reshape(bsz, nc * CHUNK, C_HEADS, C_V_HEAD)[:, pad:]
    y = head_norm(y, EPS).reshape(bsz, L, C_WIDTH)
    return (jax.nn.silu(g.astype(f32)) * y).astype(p.dtype)


def setup_inputs(seed: int = 0) -> dict:
    key = jax.random.key(seed)
    ks = iter(jax.random.split(key, 48))
    f32 = jnp.float32
    D = D_MODEL
    L1 = DEPTH - 1

    def nrm(shape, scale):
        return jax.random.normal(next(ks), shape, f32) * scale

    def uni(shape, lo, hi):
        return jax.random.uniform(next(ks), shape, f32, lo, hi)

    x = nrm((BATCH, SEQ, D), 1.0)
    meta = nrm((N_META, D), 1.0)
    norm_mix = 1.0 + nrm((DEPTH, D), 0.02)
    norm_ffn = 1.0 + nrm((DEPTH, D), 0.02)
    norm_final = 1.0 + nrm((D,), 0.02)
    w_in = nrm((DEPTH, D, W_IN), D ** -0.5)
    w_in_vres = nrm((L1, D, A_VRES_LORA), D ** -0.5)
    gate_bias = nrm((DEPTH, GATE_IN), 0.02)
    rwkv_mu = uni((DEPTH, A_IN), 0.0, 1.0)
    rwkv_mu_vres = uni((L1, A_VRES_LORA), 0.0, 1.0)
    rwkv_w0 = uni((DEPTH, A_WIDTH), -6.0, -1.0)
    rwkv_w2 = nrm((DEPTH, A_DECAY_LORA, A_WIDTH), 0.1 * A_DECAY_LORA ** -0.5)
    rwkv_a0 = nrm((DEPTH, A_WIDTH), 0.1)
    rwkv_a2 = nrm((DEPTH, A_ICLR_LORA, A_WIDTH), 0.5 * A_ICLR_LORA ** -0.5)
    rwkv_v0 = 1.0 + nrm((L1, A_WIDTH), 0.1)
    rwkv_v2 = nrm((L1, A_VRES_LORA, A_WIDTH), 0.5 * A_VRES_LORA ** -0.5)
    rwkv_g2 = nrm((DEPTH, A_GATE_LORA, A_WIDTH), A_GATE_LORA ** -0.5)
    rwkv_k_k = 0.85 + nrm((DEPTH, A_WIDTH), 0.02)
    rwkv_k_a = 1.0 + nrm((DEPTH, A_WIDTH), 0.02)
    rwkv_r_k = nrm((DEPTH, A_HEADS, A_HEAD), 0.1)
    rwkv_ln_g = 1.0 + nrm((DEPTH, A_WIDTH), 0.02)
    rwkv_ln_b = nrm((DEPTH, A_WIDTH), 0.02)
    ssm_conv_w = nrm((DEPTH, B_CONV, B_CONV_DIM), B_CONV ** -0.5)
    ssm_conv_b = nrm((DEPTH, B_CONV_DIM), 0.02)
    dt0 = jnp.exp(uni((DEPTH, B_HEADS), math.log(1e-3), math.log(1e-1)))
    ssm_dt_bias = dt0 + jnp.log(-jnp.expm1(-dt0))
    ssm_a_log = jnp.log(uni((DEPTH, B_HEADS), 1.0, 16.0))
    ssm_d = 1.0 + nrm((DEPTH, B_HEADS), 0.1)
    ssm_norm_g = 1.0 + nrm((DEPTH, B_WIDTH), 0.02)
    w_branch = jnp.concatenate([nrm((DEPTH, A_WIDTH, D), A_WIDTH ** -0.5),
                                nrm((DEPTH, B_WIDTH, D), B_WIDTH ** -0.5),
                                nrm((DEPTH, C_WIDTH, D), C_WIDTH ** -0.5)], axis=1)
    w_out = nrm((DEPTH, D, D), D ** -0.5)
    ffn_w_up = nrm((DEPTH, D, 2 * FFN_HIDDEN), D ** -0.5)
    ffn_conv_w = nrm((DEPTH, FFN_CONV, 2 * FFN_HIDDEN), FFN_CONV ** -0.5)
    ffn_conv_b = nrm((DEPTH, 2 * FFN_HIDDEN), 0.02)
    ffn_w_down = nrm((DEPTH, FFN_HIDDEN, D), FFN_HIDDEN ** -0.5)
    return {"x": x, "meta": meta, "norm_mix": norm_mix, "norm_ffn": norm_ffn,
            "norm_final": norm_final, "w_in": w_in, "w_in_vres": w_in_vres,
            "gate_bias": gate_bias, "rwkv_mu": rwkv_mu, "rwkv_mu_vres": rwkv_mu_vres,
            "rwkv_w0": rwkv_w0, "rwkv_w2": rwkv_w2, "rwkv_a0": rwkv_a0, "rwkv_a2": rwkv_a2,
            "rwkv_v0": rwkv_v0, "rwkv_v2": rwkv_v2, "rwkv_g2": rwkv_g2, "rwkv_k_k": rwkv_k_k,
            "rwkv_k_a": rwkv_k_a, "rwkv_r_k": rwkv_r_k, "rwkv_ln_g": rwkv_ln_g,
            "rwkv_ln_b": rwkv_ln_b, "ssm_conv_w": ssm_conv_w, "ssm_conv_b": ssm_conv_b,
            "ssm_dt_bias": ssm_dt_bias, "ssm_a_log": ssm_a_log, "ssm_d": ssm_d,
            "ssm_norm_g": ssm_norm_g, "w_branch": w_branch, "w_out": w_out,
            "ffn_w_up": ffn_w_up, "ffn_conv_w": ffn_conv_w, "ffn_conv_b": ffn_conv_b,
            "ffn_w_down": ffn_w_down}


def reference(x, meta, norm_mix, norm_ffn, norm_final, w_in, w_in_vres, gate_bias,
              rwkv_mu, rwkv_mu_vres, rwkv_w0, rwkv_w2, rwkv_a0, rwkv_a2, rwkv_v0, rwkv_v2,
              rwkv_g2, rwkv_k_k, rwkv_k_a, rwkv_r_k, rwkv_ln_g, rwkv_ln_b,
              ssm_conv_w, ssm_conv_b, ssm_dt_bias, ssm_a_log, ssm_d, ssm_norm_g,
              w_branch, w_out, ffn_w_up, ffn_conv_w, ffn_conv_b, ffn_w_down):
    bsz = x.shape[0]
    h = jnp.concatenate([jnp.broadcast_to(meta.astype(x.dtype), (bsz, N_META, D_MODEL)), x], axis=1)
    L = h.shape[1]
    pos = jnp.arange(L, dtype=jnp.float32)
    inv_freq = 1.0 / (ROPE_BASE ** jnp.linspace(0.0, 1.0, C_QK_HEAD // 2, dtype=jnp.float32))
    ang = pos[:, None] * inv_freq[None, :]
    cos, sin = jnp.cos(ang), jnp.sin(ang)
    v_first = None
    for l in range(DEPTH):
        u = rms_norm(h, norm_mix[l])
        if l == 0:
            w = w_in[0]
        else:
            w = jnp.concatenate([w_in[l], w_in_vres[l - 1]], axis=1)
        proj = u @ w
        p_a, p_b, p_c, p_g = split_last(proj[..., :W_IN], [A_IN, B_IN, C_IN, GATE_IN])
        vres = None if l == 0 else (proj[..., W_IN:], rwkv_mu_vres[l - 1], rwkv_v0[l - 1], rwkv_v2[l - 1])
        y_a, v_first = rwkv7_mix(p_a, rwkv_mu[l], rwkv_w0[l], rwkv_w2[l], rwkv_a0[l], rwkv_a2[l],
                                 rwkv_g2[l], rwkv_k_k[l], rwkv_k_a[l], rwkv_r_k[l],
                                 rwkv_ln_g[l], rwkv_ln_b[l], v_first, vres)
        y_b = mamba2_mix(p_b, ssm_conv_w[l], ssm_conv_b[l], ssm_dt_bias[l], ssm_a_log[l],
                         ssm_d[l], ssm_norm_g[l])
        y_c = retention_mix(p_c, cos, sin)
        gates = jax.nn.sigmoid(p_g + gate_bias[l]).reshape(bsz, L, N_BRANCH, D_MODEL)
        wb = w_branch[l]
        merged = (gates[:, :, 0] * (y_a @ wb[:A_WIDTH])
                  + gates[:, :, 1] * (y_b @ wb[A_WIDTH:A_WIDTH + B_WIDTH])
                  + gates[:, :, 2] * (y_c @ wb[A_WIDTH + B_WIDTH:]))
        h = h + merged @ w_out[l]
        u = rms_norm(h, norm_ffn[l])
        up = causal_dwconv(u @ ffn_w_up[l], ffn_conv_w[l], ffn_conv_b[l])
        gate, val = split_last(up, [FFN_HIDDEN, FFN_HIDDEN])
        h = h + (jax.nn.silu(gate) * val) @ ffn_w_down[l]
    return rms_norm(h[:, N_META:], norm_final)
```

```python
import math
import numpy as np
import concourse.bass as bass
import concourse.mybir as mybir
from concourse.bass_utils import run_bass_kernel_spmd

F32 = mybir.dt.float32
BF16 = mybir.dt.bfloat16
U8 = mybir.dt.uint8
AF = mybir.ActivationFunctionType
ALU = mybir.AluOpType
AX = mybir.AxisListType

D = 1024
NM = 16
A_IN = 3328
B_IN = 5152
C_IN = 3072
W_IN = 14624
OFF_B = A_IN
OFF_C = A_IN + B_IN
OFF_G = A_IN + B_IN + C_IN
FH = 2816
NFC = 22
EPS = 1e-6
A_LN_EPS = 64e-5
SHIFT = 3
NEGBIG = -30000.0
WDEC = math.exp(-0.5)


class Buf:
    __slots__ = ("name", "w", "r")
    ALL = []

    def __init__(self, name):
        self.name = name
        self.w = None
        self.r = []
        Buf.ALL.append(self)


class Rec:
    ENGS = ("pe", "act", "dve", "pool", "sp")

    def __init__(self):
        self.ops = []
        self.pending = {e: set() for e in self.ENGS}
        self.last = {e: None for e in self.ENGS}
        self.dmas_since = []

    def op(self, eng, fn, reads=(), writes=(), dma=False):
        deps = {}
        for b in reads:
            if b.w is not None:
                deps[b.w] = "raw"
        for b in writes:
            if b.w is not None and b.w not in deps:
                deps[b.w] = "waw"
            for r in b.r:
                if r not in deps:
                    deps[r] = "war"
        for d in self.pending[eng]:
            deps[d] = "raw"
        self.pending[eng] = set()
        i = len(self.ops)
        deps.pop(i, None)
        self.ops.append(dict(eng=eng, fn=fn, dma=dma, deps=deps, id=i, ph=getattr(self, "phase", "")))
        for b in writes:
            b.w = i
            b.r = []
        for b in reads:
            if b.w != i:
                b.r.append(i)
        if dma:
            self.dmas_since.append(i)
        else:
            self.last[eng] = i
        return i

    def interleave(self, a0, a1, b1):
        assert b1 == len(self.ops)
        sa, sb = self.ops[a0:a1], self.ops[a1:b1]
        merged = []
        ia = ib = 0
        na, nb = len(sa), len(sb)
        while ia < na or ib < nb:
            if ib >= nb or (ia < na and ia * nb <= ib * na):
                merged.append(sa[ia]); ia += 1
            else:
                merged.append(sb[ib]); ib += 1
        remap = {}
        for k, o in enumerate(merged):
            remap[o["id"]] = a0 + k
        f = lambda i: remap.get(i, i)
        for o in merged:
            o["deps"] = {f(d): kind for d, kind in o["deps"].items()}
            o["id"] = f(o["id"])
        self.ops[a0:b1] = merged
        for b in Buf.ALL:
            if b.w is not None:
                b.w = f(b.w)
            b.r = [f(x) for x in b.r]
        for e in self.ENGS:
            if self.last[e] is not None:
                self.last[e] = f(self.last[e])
            self.pending[e] = set(f(x) for x in self.pending[e])
        self.dmas_since = [f(x) for x in self.dmas_since]
        for e in self.ENGS:
            cand = [o["id"] for o in merged if o["eng"] == e and not o["dma"]]
            if cand:
                self.last[e] = max(cand)

    def barrier(self):
        ids = set(v for v in self.last.values() if v is not None) | set(self.dmas_since)
        for e in self.ENGS:
            self.pending[e] |= ids
        self.dmas_since = []

    def emit(self, nc, engines):
        ops = self.ops
        per = {e: [o for o in ops if o["eng"] == e] for e in self.ENGS}
        for e in self.ENGS:
            n = 0
            for o in per[e]:
                if not o["dma"]:
                    n += 1
                    o["eidx"] = n
        signal = set()
        for e in self.ENGS:
            waited = {x: 0 for x in self.ENGS}
            wdma = set()
            for o in per[e]:
                cw = {}
                dw = []
                for d, kind in o["deps"].items():
                    P = ops[d]
                    if P["dma"]:
                        if d not in wdma:
                            wdma.add(d)
                            dw.append(d)
                    else:
                        E = P["eng"]
                        if E == e and kind != "raw":
                            continue
                        if P["eidx"] > waited[E]:
                            if E not in cw or ops[cw[E]]["eidx"] < P["eidx"]:
                                cw[E] = d
                for E, d in cw.items():
                    waited[E] = ops[d]["eidx"]
                    signal.add(d)
                o["cw"] = list(cw.values())
                o["dw"] = dw
        EPOCH = 4000
        sems = {}

        def getsem(key):
            if key not in sems:
                sems[key] = nc.alloc_semaphore("s_%s_%d" % key)
            return sems[key]

        for e in self.ENGS:
            r = 0
            for o in per[e]:
                if not o["dma"] and o["id"] in signal:
                    o["sig"] = (e, r // EPOCH, r % EPOCH + 1)
                    r += 1
        NSLOT = {"sp": 32, "pool": 12, "act": 4, "dve": 4, "pe": 4}
        for e in self.ENGS:
            k = 0
            uses = [0] * NSLOT[e]
            for o in per[e]:
                if o["dma"]:
                    s = k % NSLOT[e]
                    o["slot"] = (e, s, uses[s] * 16)
                    uses[s] += 1
                    o["tok"] = (("dma_" + e, s), uses[s] * 16)
                    assert uses[s] * 16 < 4000
                    k += 1

        def run(e, eng):
            for o in per[e]:
                for d in o["cw"]:
                    E, ep, val = ops[d]["sig"]
                    eng.wait_ge(getsem((E, ep)), val)
                for d in o["dw"]:
                    key, val = ops[d]["tok"]
                    eng.wait_ge(getsem(key), val)
                if o["dma"]:
                    _, s, prev = o["slot"]
                    sm = getsem(("dma_" + e, s))
                    if prev > 0:
                        eng.wait_ge(sm, prev)
                    ins = o["fn"](eng)
                    ins.then_inc(sm, 16)
                else:
                    ins = o["fn"](eng)
                    if "sig" in o:
                        E, ep, val = o["sig"]
                        ins.then_inc(getsem((E, ep)), 1)

        for e in self.ENGS:
            for o in per[e]:
                if "sig" in o:
                    getsem((o["sig"][0], o["sig"][1]))
                if o["dma"]:
                    getsem(("dma_" + e, o["slot"][1]))
        with nc.Block() as block:
            @block.tensor
            def _(eng):
                run("pe", eng)

            @block.scalar
            def _(eng):
                run("act", eng)

            @block.vector
            def _(eng):
                run("dve", eng)

            @block.gpsimd
            def _(eng):
                run("pool", eng)

            @block.sync
            def _(eng):
                run("sp", eng)


class Arena:
    def __init__(self, nc, nbytes, t=None, base=0):
        self.t = nc.alloc_sbuf_tensor("arena", [128, nbytes], U8).ap() if t is None else t
        self.n = nbytes
        self.off = base
        self.peak = 0

    def sub(self, base, limit):
        return Arena(None, limit, t=self.t, base=base)

    def alloc(self, shape, dtype):
        esz = 4 if dtype == F32 else 2
        n = 1
        for s in shape[1:]:
            n *= s
        nb = (n * esz + 31) // 32 * 32
        assert self.off + nb <= self.n, "SBUF arena overflow %d + %d" % (self.off, nb)
        v = self.t[0:shape[0], self.off:self.off + n * esz].bitcast(dtype)
        self.off += nb
        self.peak = max(self.peak, self.off)
        if len(shape) == 3:
            v = v.rearrange("p (a b) -> p a b", b=shape[2])
        elif len(shape) == 4:
            v = v.rearrange("p (a b c) -> p a b c", b=shape[2], c=shape[3])
        return v

    def mark(self):
        return self.off

    def release(self, m):
        self.off = m


class Cfg:
    def __init__(self, lx, depth):
        self.lx = lx
        self.depth = depth
        self.L = NM + lx
        self.NT = (self.L + 127) // 128
        self.TP = self.NT * 128
        self.do_rwkv = self.do_ssm = self.do_ret = self.do_merge = self.do_ffn = True
        self.groups = []
        s = 0
        while s < self.TP:
            n = min(512, self.TP - s)
            self.groups.append((s, n))
            s += n


def host_consts(cfg):
    TP = cfg.TP
    c = {}
    c["ident"] = np.eye(128, dtype=np.float32)
    j = np.arange(128)
    blk = (j[:, None] // 64) == (j[None, :] // 64)
    c["tri64"] = (blk & (j[:, None] <= j[None, :])).astype(np.float32)
    c["sfx64"] = (blk & (j[:, None] > j[None, :])).astype(np.float32)
    c["tri128"] = (j[:, None] <= j[None, :]).astype(np.float32)
    c["sfx128"] = (j[:, None] > j[None, :]).astype(np.float32)
    c["negU"] = np.tile(((j[:, None] > j[None, :]) * NEGBIG).astype(np.float32), (1, 4))
    c["causal"] = (j[:, None] <= j[None, :]).astype(np.float32)
    s = j % 64
    m = np.zeros((128, 128), np.float32)
    m[:, :64] = (s[:, None] < s[None, :64])
    m[:, 64:] = (s[:, None] <= s[None, :64])
    c["mmask"] = m
    c["mmaskT"] = np.ascontiguousarray((s[None, :64] < s[:, None]).astype(np.float32))
    cc = j // 64
    c["nmask2"] = ((cc[:, None] == cc[None, :]) & (s[None, :] < s[:, None])).astype(np.float32)
    m2 = np.zeros((128, 2, 128), np.float32)
    same = (cc[:, None] == cc[None, :])
    m2[:, 0, :] = same & (s[:, None] < s[None, :])
    m2[:, 1, :] = same & (s[:, None] <= s[None, :])
    c["mmask2"] = m2.reshape(128, 256)
    c["ident64x2"] = np.concatenate([np.eye(64, dtype=np.float32)] * 2, axis=0)
    pos = np.arange(TP, dtype=np.float32)
    inv = (1.0 / (10000.0 ** np.linspace(0.0, 1.0, 32, dtype=np.float32))).astype(np.float32)
    ang = pos[None, :] * inv[:, None]
    cos = np.cos(ang).astype(np.float32)
    sin = np.sin(ang).astype(np.float32)
    cos64 = np.concatenate([cos, cos], 0)
    sin64 = np.concatenate([-sin, sin], 0)
    c["cosq"] = np.concatenate([cos64, cos64], 0)
    c["sinq"] = np.concatenate([sin64, sin64], 0)
    c["cosk"] = c["cosq"] * np.float32(0.125)
    c["sink"] = c["sinq"] * np.float32(0.125)
    lg = np.log(1.0 - 2.0 ** (-5.0 - np.arange(8, dtype=np.float64)))
    rel = np.arange(TP)[None, :] - np.arange(128)[:, None]
    tabs = []
    for h in range(8):
        tabs.append(np.where(rel >= 0, np.exp(np.maximum(rel, 0) * lg[h]), 0.0).astype(np.float32))
    c["rdec"] = np.stack(tabs, 0)
    return c


STACKED = ["w_in", "w_rot", "w_qk", "w_vres", "w_branch", "w_out", "w_up", "w_down", "rw2", "rg2", "rv2", "colp", "rowa",
           "rowb", "mu_all", "cw_all", "brow"]
CONST_NAMES = ["ident", "tri64", "sfx64", "tri128", "sfx128", "negU", "causal", "mmask", "mmaskT",
               "cosq", "sinq", "cosk", "sink", "rdec", "nmask2", "mmask2"]


def build(cfg, debug=False):
    Buf.ALL = []
    nc = bass.Bass("TRN2", target_bir_lowering=False)
    R = Rec()
    TP, NT, L, depth = cfg.TP, cfg.NT, cfg.L, cfg.depth
    groups = cfg.groups

    def din(name, shape):
        return nc.dram_tensor(name, list(shape), F32, kind="ExternalInput").ap()

    xT = din("xT", [D, TP])
    cst = {}
    ch = host_consts(cfg)
    for k in CONST_NAMES:
        cst[k] = din("c_" + k, ch[k].shape)
    w_in = [din("w_in_%d" % i_, [D, W_IN]) for i_ in range(depth)]
    w_rot = [din("w_rot_%d" % i_, [D, 1024]) for i_ in range(depth)]
    w_qk = [din("w_qk_%d" % i_, [D, 1024]) for i_ in range(depth)]
    w_vres = [din("w_vres_%d" % i_, [D, 32]) for i_ in range(max(depth - 1, 1))]
    w_branch = [din("w_branch_%d" % i_, [4096, D]) for i_ in range(depth)]
    w_out = [din("w_out_%d" % i_, [D, D]) for i_ in range(depth)]
    w_up = [din("w_up_%d" % i_, [D, 2 * FH]) for i_ in range(depth)]
    w_down = [din("w_down_%d" % i_, [FH, D]) for i_ in range(depth)]
    rw2 = [din("rw2_%d" % i_, [128, D]) for i_ in range(depth)]
    rg2 = [din("rg2_%d" % i_, [128, D]) for i_ in range(depth)]
    rv2 = [din("rv2_%d" % i_, [32, D]) for i_ in range(max(depth - 1, 1))]
    NCOLP = 8 + 8 + 24 + 8 + 4 * 44 + 1
    colp = [din("colp_%d" % i_, [128, NCOLP]) for i_ in range(depth)]
    NROWA = 9 * 1024
    rowa = [din("rowa_%d" % i_, [128, NROWA]) for i_ in range(depth)]
    NROWB = 2048 + 32 * 3
    rowb = [din("rowb_%d" % i_, [128, NROWB]) for i_ in range(depth)]
    mu_all = [din("mu_all_%d" % i_, [128, 3328 + 32]) for i_ in range(depth)]
    cw_all = [din("cw_all_%d" % i_, [128, 4 * 3072]) for i_ in range(depth)]
    brow = [din("brow_%d" % i_, [1, 3072 + 32]) for i_ in range(depth)]
    nfin = din("nfin", [128, 8])
    outT = nc.dram_tensor("outT", [D, cfg.lx], F32, kind="ExternalOutput").ap()
    dbg_h = nc.dram_tensor("dbg_h", [depth * 3, D, TP], F32, kind="ExternalOutput").ap() if debug else None
    def dscr(name, shape, dt=F32):
        return nc.dram_tensor(name, list(shape), dt, kind="Internal").ap()

    s_rkv = dscr("s_rkv", [TP, 3072])
    s_vfirst = s_rkv
    s_vf = dscr("s_vf", [TP, 1024])
    s_mx = dscr("s_mx", [TP, 2048 + 2048 + 512 + 32])
    s_y = (nc.dram_tensor("s_y", [4096, TP], BF16, kind="ExternalOutput").ap() if debug else dscr("s_y", [4096, TP], BF16))
    s_lo = dscr("s_lo", [384, TP], BF16)
    b_rkv = [Buf("s_rkv%d" % i) for i in range(NT)]
    b_vf = [Buf("s_vf%d" % i) for i in range(NT)]
    b_mx = [Buf("s_mx%d" % i) for i in range(NT)]
    b_y = [[Buf("s_y%d_%d" % (m, i)) for i in range(NT)] for m in range(3)]

    A = Arena(nc, 208000)
    hT = A.alloc([128, 8, TP], F32)
    b_h = [Buf("h%d" % g) for g in range(len(groups))]
    uT = A.alloc([128, 8, SHIFT + TP], BF16)
    b_u = Buf("uT")
    hu_bytes = A.off
    s_hT = dscr("s_hT", [128, 8 * TP])
    s_uT = dscr("s_uT", [128, 8 * (SHIFT + TP)], BF16)
    ident_f = A.alloc([128, 128], F32)
    ident_b = A.alloc([128, 128], BF16)
    ones_f = A.alloc([128, 128], F32)
    ones_b = A.alloc([128, 128], BF16)
    b_const = Buf("const")
    PS = nc.alloc_psum_tensor("ps", [128, 8, 512], F32).ap()
    b_ps = [Buf("ps%d" % i) for i in range(8)]
    ps_rr = [0]
    ps_lim = [0, 6]

    def psum(nb=1):
        lo, hi = ps_lim
        s = ps_rr[0]
        if s < lo or s >= hi:
            s = lo
        if (s - lo) % nb:
            s += nb - (s - lo) % nb
        if s + nb > hi:
            s = lo
        ps_rr[0] = s + nb
        ap = PS[:, s:s + nb, :].rearrange("p a b -> p (a b)") if nb > 1 else PS[:, s, :]
        return ap, b_ps[s:s + nb]

    pin_rr = [0]

    def psum_pin():
        s = 6 + pin_rr[0] % 2
        pin_rr[0] += 1
        return PS[:, s, :], b_ps[s:s + 1]

    def dma(q, out, in_, reads=(), writes=()):
        return R.op(q, lambda e: e.dma_start(out=out, in_=in_), reads, writes, dma=True)

    def mm(out, lhsT, rhs, start, stop, reads, writes):
        return R.op("pe", lambda e: e.matmul(out, lhsT, rhs, start=start, stop=stop), reads, writes)

    def tr(out, in_, ident, reads, writes):
        return R.op("pe", lambda e: e.transpose(out, in_, ident), reads, writes)

    def act(out, in_, func, reads, writes, bias=None, scale=None, eng="act", accum=None):
        kw = {}
        if bias is not None:
            kw["bias"] = bias
        if scale is not None:
            kw["scale"] = scale
        if accum is not None:
            kw["accum_out"] = accum
        return R.op(eng, lambda e: e.activation(out, in_, func, **kw), reads, writes)

    def tt(eng, out, in0, in1, op, reads, writes):
        return R.op(eng, lambda e: e.tensor_tensor(out, in0, in1, op), reads, writes)

    def ts(eng, out, in0, s1, s2, op0, op1, reads, writes):
        return R.op(eng, lambda e: e.tensor_scalar(out, in0, s1, s2, op0, op1), reads, writes)

    def stt(out, in0, scalar, in1, op0, op1, reads, writes):
        return R.op("dve", lambda e: e.scalar_tensor_tensor(out, in0, scalar, in1, op0, op1), reads, writes)

    def cp(eng, out, in_, reads, writes):
        if eng == "act":
            return R.op("act", lambda e: e.copy(out, in_), reads, writes)
        return R.op(eng, lambda e: e.tensor_copy(out, in_), reads, writes)

    def memset(eng, ap, val, writes):
        return R.op(eng, lambda e: e.memset(ap, val), (), writes)

    def rsqrt_inplace(ap, bufs, scale, eps):
        ts("dve", ap, ap, scale, eps, ALU.mult, ALU.add, bufs, bufs)
        act(ap, ap, AF.Ln, bufs, bufs)
        act(ap, ap, AF.Exp, bufs, bufs, scale=-0.5)

    dma("sp", ident_f, cst["ident"], (), [b_const])
    cp("dve", ident_b, ident_f, [b_const], [b_const])
    memset("dve", ones_f, 1.0, [b_const])
    memset("dve", ones_b, 1.0, [b_const])
    memset("pool", uT[:, :, 0:SHIFT], 0.0, [b_u])
    for gi, (gs, gn) in enumerate(groups):
        dma("sp", hT[:, :, gs:gs + gn], xT.rearrange("(a p) t -> p a t", p=128)[:, :, gs:gs + gn], (), [b_h[gi]])

    def load_cast(dst_bf, src_f32, bufs_w, reads=()):
        return dma("pool", dst_bf, src_f32, reads, bufs_w)

    def phase_norm(l, which, gsel=None, dst=None, dst_b=None, colp_t=None, b_colp=None):
        m = A.mark()
        sq = A.alloc([128, 512], F32)
        rs = A.alloc([128, 512], F32)
        b_sq, b_rs = Buf("sq"), Buf("rs")
        for gi, (gs, gn) in enumerate(groups):
            if gsel is not None and gi not in gsel:
                continue
            pa, pb = psum(1)
            for kc in range(8):
                act(sq[:, :gn], hT[:, kc, gs:gs + gn], AF.Square, [b_h[gi]], [b_sq])
                mm(pa[:, :gn], ones_f, sq[:, :gn], kc == 0, kc == 7, [b_sq, b_const], pb)
            ts("dve", rs[:, :gn], pa[:, :gn], 1.0 / D, EPS, ALU.mult, ALU.add, pb, [b_rs])
            act(rs[:, :gn], rs[:, :gn], AF.Ln, [b_rs], [b_rs])
            act(rs[:, :gn], rs[:, :gn], AF.Exp, [b_rs], [b_rs], scale=-0.5)
            for kc in range(8):
                o = (dst[:, kc, 0:gn] if dst is not None else uT[:, kc, SHIFT + gs:SHIFT + gs + gn])
                stt(o, hT[:, kc, gs:gs + gn], colp_t[:, which * 8 + kc:which * 8 + kc + 1], rs[:, :gn],
                    ALU.mult, ALU.mult, [b_h[gi], b_rs, b_colp], [dst_b if dst is not None else b_u])
        A.release(m)

    CP_NMIX, CP_NFFN, CP_GB, CP_BCB, CP_FW, CP_FB = 0, 1, 16, 40, 48, 48 + 3 * 44

    def phase_ffn(l, colp_t, b_colp):
        m = A.mark()
        u2 = A.alloc([128, 8, 512], BF16)
        b_u2 = Buf("u2")
        halo = A.alloc([128, 44, 2], F32)
        b_halo = Buf("halo")
        actT = A.alloc([128, NFC, 512], BF16)
        b_actT = Buf("actT")
        wup = [A.alloc([128, 8, 256], BF16) for _ in range(2)]
        b_wup = [Buf("wup0"), Buf("wup1")]
        wdn = [A.alloc([128, NFC, 128], BF16) for _ in range(2)]
        b_wdn = [Buf("wdn0"), Buf("wdn1")]
        X = [A.alloc([128, 514], F32) for _ in range(4)]
        b_X = [Buf("X%d" % i) for i in range(4)]
        cc = [A.alloc([128, 512], F32) for _ in range(2)]
        b_cc = [Buf("cc0"), Buf("cc1")]
        sg = A.alloc([128, 512], F32)
        b_sg = Buf("sg")
        memset("dve", halo, 0.0, [b_halo])
        wu = w_up[l].rearrange("(a p) f -> p a f", p=128)
        wd = w_down[l].rearrange("(a p) d -> p a d", p=128)
        for gi, (gs, gn) in enumerate(groups):
            phase_norm(l, 1, gsel=[gi], dst=u2, dst_b=b_u2, colp_t=colp_t, b_colp=b_colp)
            for fc in range(NFC):
                wbf = wup[fc % 2]
                load_cast(wbf[:, :, 0:128], wu[:, :, fc * 128:(fc + 1) * 128], [b_wup[fc % 2]])
                load_cast(wbf[:, :, 128:256], wu[:, :, FH + fc * 128:FH + (fc + 1) * 128], [b_wup[fc % 2]])
                for half in range(2):
                    ci = half * NFC + fc
                    pa, pb = psum(1)
                    for kc in range(8):
                        mm(pa[:, :gn], wbf[:, kc, half * 128:(half + 1) * 128], u2[:, kc, :gn], kc == 0, kc == 7,
                           [b_wup[fc % 2], b_u2], pb)
                    xi = (fc % 2) * 2 + half
                    Xh = X[xi]
                    cp("act", Xh[:, 2:2 + gn], pa[:, :gn], pb, [b_X[xi]])
                    cp("act", Xh[:, 0:2], halo[:, ci, :], [b_halo], [b_X[xi]])
                    c = cc[half]
                    w = lambda k: colp_t[:, CP_FW + k * 44 + ci:CP_FW + k * 44 + ci + 1]
                    bcol = colp_t[:, CP_FB + ci:CP_FB + ci + 1]
                    ts("dve", c[:, :gn], Xh[:, 2:2 + gn], w(2), bcol, ALU.mult, ALU.add, [b_X[xi], b_colp], [b_cc[half]])
                    stt(c[:, :gn], Xh[:, 1:1 + gn], w(1), c[:, :gn], ALU.mult, ALU.add, [b_X[xi], b_colp, b_cc[half]], [b_cc[half]])
                    stt(c[:, :gn], Xh[:, 0:gn], w(0), c[:, :gn], ALU.mult, ALU.add, [b_X[xi], b_colp, b_cc[half]], [b_cc[half]])
                    cp("act", halo[:, ci, :], Xh[:, gn:gn + 2], [b_X[xi]], [b_halo])
                act(sg[:, :gn], cc[0][:, :gn], AF.Silu, [b_cc[0]], [b_sg])
                tt("dve", actT[:, fc, :gn], sg[:, :gn], cc[1][:, :gn], ALU.mult, [b_sg, b_cc[1]], [b_actT])
            for dc in range(8):
                load_cast(wdn[dc % 2], wd[:, :, dc * 128:(dc + 1) * 128], [b_wdn[dc % 2]])
                pa, pb = psum(1)
                for fc in range(NFC):
                    mm(pa[:, :gn], wdn[dc % 2][:, fc, :], actT[:, fc, :gn], fc == 0, fc == NFC - 1,
                       [b_wdn[dc % 2], b_actT], pb)
                tt("dve", hT[:, dc, gs:gs + gn], hT[:, dc, gs:gs + gn], pa[:, :gn], ALU.add, [b_h[gi]] + pb, [b_h[gi]])
        A.release(m)

    def phase_merge(l, colp_t, b_colp):
        m = A.mark()
        yT = A.alloc([128, 32, 512], BF16)
        b_yT = Buf("yT")
        wbr = [A.alloc([128, 32, 128], BF16) for _ in range(2)]
        b_wbr = [Buf("wbr0"), Buf("wbr1")]
        wg = [A.alloc([128, 8, 384], BF16) for _ in range(2)]
        b_wg = [Buf("wg0"), Buf("wg1")]
        wo = A.alloc([128, 8, 1024], BF16)
        b_wo = Buf("wo")
        mT = A.alloc([128, 8, 512], BF16)
        b_mT = Buf("mT")
        sig = A.alloc([128, 512], F32)
        b_sig = Buf("sig")
        tmp = A.alloc([128, 512], F32)
        b_tmp = Buf("tmp")
        acc = A.alloc([128, 512], F32)
        b_acc = Buf("acc")
        load_cast(wo, w_out[l].rearrange("(a p) d -> p a d", p=128), [b_wo])
        wbd = w_branch[l].rearrange("(a p) d -> p a d", p=128)
        wi = w_in[l].rearrange("(a p) f -> p a f", p=128)
        syv = s_y.rearrange("(a p) t -> p a t", p=128)
        for gi, (gs, gn) in enumerate(groups):
            ybufs = [sy_buf(kc, gi) for kc in range(32)]
            dma("sp", yT[:, :, :gn], syv[:, :, gs:gs + gn], ybufs, [b_yT])
            for dc in range(8):
                load_cast(wbr[dc % 2], wbd[:, :, dc * 128:(dc + 1) * 128], [b_wbr[dc % 2]])
                for br in range(3):
                    c0 = OFF_G + br * 1024 + dc * 128
                    load_cast(wg[dc % 2][:, :, br * 128:(br + 1) * 128], wi[:, :, c0:c0 + 128], [b_wg[dc % 2]])
                for br, (k0, k1) in enumerate([(0, 8), (8, 24), (24, 32)]):
                    pa, pb = psum(1)
                    for kc in range(k0, k1):
                        mm(pa[:, :gn], wbr[dc % 2][:, kc, :], yT[:, kc, :gn], kc == k0, kc == k1 - 1,
                           [b_wbr[dc % 2], b_yT], pb)
                    pg, pgb = psum(1)
                    for kc in range(8):
                        mm(pg[:, :gn], wg[dc % 2][:, kc, br * 128:(br + 1) * 128],
                           uT[:, kc, SHIFT + gs:SHIFT + gs + gn], kc == 0, kc == 7, [b_wg[dc % 2], b_u], pgb)
                    gb = colp_t[:, CP_GB + br * 8 + dc:CP_GB + br * 8 + dc + 1]
                    act(sig[:, :gn], pg[:, :gn], AF.Sigmoid, pgb + [b_colp], [b_sig], bias=gb)
                    if br == 0:
                        tt("dve", acc[:, :gn], sig[:, :gn], pa[:, :gn], ALU.mult, [b_sig] + pb, [b_acc])
                    else:
                        tt("dve", tmp[:, :gn], sig[:, :gn], pa[:, :gn], ALU.mult, [b_sig] + pb, [b_tmp])
                        if br == 1:
                            tt("dve", acc[:, :gn], acc[:, :gn], tmp[:, :gn], ALU.add, [b_acc, b_tmp], [b_acc])
                        else:
                            tt("dve", mT[:, dc, :gn], acc[:, :gn], tmp[:, :gn], ALU.add, [b_acc, b_tmp], [b_mT])
            for dc2 in range(8):
                pa, pb = psum(1)
                for dc in range(8):
                    mm(pa[:, :gn], wo[:, dc, dc2 * 128:(dc2 + 1) * 128], mT[:, dc, :gn], dc == 0, dc == 7,
                       [b_wo, b_mT], pb)
                tt("dve", hT[:, dc2, gs:gs + gn], hT[:, dc2, gs:gs + gn], pa[:, :gn], ALU.add, [b_h[gi]] + pb, [b_h[gi]])
        A.release(m)


    sy_bufs = {}

    def sy_buf(kc, gi):
        if (kc, gi) not in sy_bufs:
            sy_bufs[(kc, gi)] = Buf("sy%d_%d" % (kc, gi))
        return sy_bufs[(kc, gi)]

    def phase_ret(l, colp_t, b_colp):
        m = A.mark()
        wq = A.alloc([128, 8, 512], BF16)
        wv = A.alloc([128, 8, 256], BF16)
        wgt = A.alloc([128, 8, 256], BF16)
        b_wq, b_wv, b_wgt = Buf("wq"), Buf("wv"), Buf("wgt")
        qT = A.alloc([128, TP], BF16)
        kT = A.alloc([128, TP], BF16)
        b_qT, b_kT = Buf("qT"), Buf("kT")
        vtok = A.alloc([128, NT, 256], BF16)
        b_vtok = Buf("vtok")
        gT = A.alloc([128, 2, TP], BF16)
        b_gT = Buf("gT")
        tabs = [A.alloc([128, 512], F32) for _ in range(4)]
        b_tabs = [Buf("tab%d" % i) for i in range(4)]
        dect = A.alloc([128, TP], F32)
        b_dect = Buf("dect")
        Pm = [A.alloc([128, 512], BF16) for _ in range(2)]
        b_Pm = [Buf("P0"), Buf("P1")]
        t1 = A.alloc([128, 512], F32)
        t2 = A.alloc([128, 512], F32)
        b_t1, b_t2 = Buf("t1"), Buf("t2")
        ysb = A.alloc([128, 512], F32)
        sq = A.alloc([128, 512], F32)
        msb = A.alloc([128, 512], F32)
        var = A.alloc([128, 512], F32)
        b_ysb, b_sq, b_msb, b_var = Buf("ysb"), Buf("sq"), Buf("msb"), Buf("var")
        yo = A.alloc([128, 512], BF16)
        b_yo = Buf("yo")
        wi = w_in[l].rearrange("(a p) f -> p a f", p=128)
        wqk = w_qk[l].rearrange("(a p) f -> p a f", p=128)
        wrt = w_rot[l].rearrange("(a p) f -> p a f", p=128)
        tabn = ["cosq", "sinq", "cosk", "sink"]
        pcount = 0
        for hp in range(4):
            load_cast(wq[:, :, 0:128], wqk[:, :, hp * 128:(hp + 1) * 128], [b_wq])
            load_cast(wq[:, :, 128:256], wrt[:, :, hp * 128:(hp + 1) * 128], [b_wq])
            load_cast(wq[:, :, 256:384], wqk[:, :, 512 + hp * 128:512 + (hp + 1) * 128], [b_wq])
            load_cast(wq[:, :, 384:512], wrt[:, :, 512 + hp * 128:512 + (hp + 1) * 128], [b_wq])
            load_cast(wv, wi[:, :, OFF_C + 1024 + hp * 256:OFF_C + 1024 + (hp + 1) * 256], [b_wv])
            load_cast(wgt, wi[:, :, OFF_C + 2048 + hp * 256:OFF_C + 2048 + (hp + 1) * 256], [b_wgt])
            for gi, (gs, gn) in enumerate(groups):
                for ti in range(4):
                    dma("sp", tabs[ti][:, :gn], cst[tabn[ti]][:, gs:gs + gn], (), [b_tabs[ti]])
                for qi in range(2):
                    pq, pqb = psum(1)
                    pr, prb = psum(1)
                    for kc in range(8):
                        mm(pq[:, :gn], wq[:, kc, qi * 256:qi * 256 + 128], uT[:, kc, SHIFT + gs:SHIFT + gs + gn],
                           kc == 0, kc == 7, [b_wq, b_u], pqb)
                    for kc in range(8):
                        mm(pr[:, :gn], wq[:, kc, qi * 256 + 128:qi * 256 + 256], uT[:, kc, SHIFT + gs:SHIFT + gs + gn],
                           kc == 0, kc == 7, [b_wq, b_u], prb)
                    tt("dve", t1[:, :gn], pq[:, :gn], tabs[qi * 2][:, :gn], ALU.mult, pqb + [b_tabs[qi * 2]], [b_t1])
                    tt("dve", t2[:, :gn], pr[:, :gn], tabs[qi * 2 + 1][:, :gn], ALU.mult, prb + [b_tabs[qi * 2 + 1]], [b_t2])
                    dst, dstb = (qT, b_qT) if qi == 0 else (kT, b_kT)
                    tt("pool", dst[:, gs:gs + gn], t1[:, :gn], t2[:, :gn], ALU.add, [b_t1, b_t2], [dstb])
                for hh in range(2):
                    pg, pgb = psum(1)
                    for kc in range(8):
                        mm(pg[:, :gn], wgt[:, kc, hh * 128:(hh + 1) * 128], uT[:, kc, SHIFT + gs:SHIFT + gs + gn],
                           kc == 0, kc == 7, [b_wgt, b_u], pgb)
                    act(gT[:, hh, gs:gs + gn], pg[:, :gn], AF.Silu, pgb, [b_gT])
            for i in range(NT):
                pv, pvb = psum(1)
                for kc in range(8):
                    mm(pv[:, :256], uT[:, kc, SHIFT + i * 128:SHIFT + (i + 1) * 128], wv[:, kc, :], kc == 0, kc == 7,
                       [b_wv, b_u], pvb)
                cp("act", vtok[:, i, :], pv[:, :256], pvb, [b_vtok])
            for hh in range(2):
                h = hp * 2 + hh
                base = hh * 64
                dma("sp", dect, cst["rdec"][h], (), [b_dect])
                for gi, (gs, gn) in enumerate(groups):
                    yps, ypb = psum_pin()
                    nst = (gs + gn) // 128
                    for si in range(nst):
                        s0 = si * 128
                        ls = max(gs, s0)
                        n = gs + gn - ls
                        sc, scb = psum(1)
                        mm(sc[:, :n], kT[base:base + 64, s0:s0 + 128], qT[base:base + 64, ls:ls + n], True, True,
                           [b_kT, b_qT], scb)
                        P = Pm[pcount % 2]
                        bP = b_Pm[pcount % 2]
                        pcount += 1
                        tt("dve", P[:, :n], sc[:, :n], dect[:, ls - s0:ls - s0 + n], ALU.mult, scb + [b_dect], [bP])
                        mm(yps[:, ls - gs:ls - gs + n], vtok[:, si, hh * 128:(hh + 1) * 128], P[:, :n], si == 0,
                           si == nst - 1, [b_vtok, bP], ypb)
                    cp("act", ysb[:, :gn], yps[:, :gn], ypb, [b_ysb])
                    act(sq[:, :gn], ysb[:, :gn], AF.Square, [b_ysb], [b_sq])
                    pm, pmb = psum(1)
                    mm(pm[:, :gn], ones_f, ysb[:, :gn], True, True, [b_ysb, b_const], pmb)
                    pv2, pv2b = psum(1)
                    mm(pv2[:, :gn], ones_f, sq[:, :gn], True, True, [b_sq, b_const], pv2b)
                    ts("dve", msb[:, :gn], pm[:, :gn], 1.0 / 128, None, ALU.mult, ALU.bypass, pmb, [b_msb])
                    tt("pool", sq[:, :gn], msb[:, :gn], msb[:, :gn], ALU.mult, [b_msb], [b_sq])
                    stt(var[:, :gn], pv2[:, :gn], 1.0 / 128, sq[:, :gn], ALU.mult, ALU.subtract, pv2b + [b_sq], [b_var])
                    ts("dve", var[:, :gn], var[:, :gn], 1.0, EPS, ALU.mult, ALU.add, [b_var], [b_var])
                    act(var[:, :gn], var[:, :gn], AF.Ln, [b_var], [b_var])
                    act(var[:, :gn], var[:, :gn], AF.Exp, [b_var], [b_var], scale=-0.5)
                    tt("pool", ysb[:, :gn], ysb[:, :gn], msb[:, :gn], ALU.subtract, [b_ysb, b_msb], [b_ysb])
                    tt("dve", ysb[:, :gn], ysb[:, :gn], var[:, :gn], ALU.mult, [b_ysb, b_var], [b_ysb])
                    tt("dve", yo[:, :gn], ysb[:, :gn], gT[:, hh, gs:gs + gn], ALU.mult, [b_ysb, b_gT], [b_yo])
                    dma("sp", s_y[(24 + h) * 128:(25 + h) * 128, gs:gs + gn], yo[:, :gn], [b_yo], [sy_buf(24 + h, gi)])
        A.release(m)


    def phase_ssm(l, colp_t, b_colp):
        m0 = A.mark()
        BT = A.alloc([128, 4, TP], BF16)
        CT = A.alloc([128, 4, TP], BF16)
        b_BT, b_CT = Buf("BT"), Buf("CT")
        brow_t = A.alloc([1, 3072 + 32], F32)
        b_brow = Buf("brow")
        dma("sp", brow_t, brow[l], (), [b_brow])
        wi = w_in[l].rearrange("(a p) f -> p a f", p=128)
        smx_b = [Buf("smx%d" % i) for i in range(NT)]
        m1 = A.mark()
        CBW = 256
        stage = A.alloc([128, 8, CBW], F32)
        b_stage = Buf("stage")
        cwt = A.alloc([128, 4, CBW], F32)
        b_cwt = Buf("cwt")
        wt = A.alloc([128, 4, 8, CBW], BF16)
        b_wt = Buf("wt")
        ev = [A.alloc([128, 512], F32) for _ in range(2)]
        b_ev = [Buf("ev0"), Buf("ev1")]
        evc = 0
        wz = wt[:, 0, :, :]
        for cb in range(2048 // CBW):
            load_cast(wz, wi[:, :, OFF_B + cb * CBW:OFF_B + (cb + 1) * CBW], [b_wt])
            for i in range(NT):
                pa, pb = psum(1)
                for kc in range(8):
                    mm(pa[:, :CBW], uT[:, kc, SHIFT + i * 128:SHIFT + (i + 1) * 128], wz[:, kc, :], kc == 0, kc == 7,
                       [b_wt, b_u], pb)
                e = ev[evc % 2]
                be = b_ev[evc % 2]
                evc += 1
                act(e[:, :CBW], pa[:, :CBW], AF.Silu, pb, [be])
                dma("sp", s_mx[i * 128:(i + 1) * 128, cb * CBW:(cb + 1) * CBW], e[:, :CBW], [be], [smx_b[i]])
        for cb in range(3072 // CBW):
            c0 = cb * CBW
            dma("sp", stage, wi[:, :, OFF_B + 2048 + c0:OFF_B + 2048 + c0 + CBW], (), [b_stage])
            for k in range(4):
                dma("sp", cwt[:, k, :], cw_all[l][:, k * 3072 + c0:k * 3072 + c0 + CBW], (), [b_cwt])
            for k in range(4):
                tt("pool" if k % 2 else "dve", wt[:, k, :, :], stage,
                   cwt[:, k, :].unsqueeze(1).to_broadcast([128, 8, CBW]), ALU.mult, [b_stage, b_cwt], [b_wt])
            if c0 < 2560:
                for i in range(NT):
                    pa, pb = psum(1)
                    for k in range(4):
                        for kc in range(8):
                            mm(pa[:, :CBW], uT[:, kc, i * 128 + k:(i + 1) * 128 + k], wt[:, k, kc, :],
                               k == 0 and kc == 0, False, [b_wt, b_u], pb)
                    mm(pa[:, :CBW], ones_f[0:1, :], brow_t[0:1, c0:c0 + CBW], False, True, [b_const, b_brow], pb)
                    e = ev[evc % 2]
                    be = b_ev[evc % 2]
                    evc += 1
                    act(e[:, :CBW], pa[:, :CBW], AF.Silu, pb, [be])
                    dma("sp", s_mx[i * 128:(i + 1) * 128, 2048 + c0:2048 + c0 + CBW], e[:, :CBW], [be], [smx_b[i]])
            if c0 >= 2048:
                for sub in range(CBW // 128):
                    cc0 = c0 + sub * 128 - 2048
                    isC = cc0 >= 512
                    g = (cc0 % 512) // 128
                    dstT, dstb = (CT, b_CT) if isC else (BT, b_BT)
                    for gi, (gs, gn) in enumerate(groups):
                        pa, pb = psum(1)
                        for k in range(4):
                            for kc in range(8):
                                mm(pa[:, :gn], wt[:, k, kc, sub * 128:(sub + 1) * 128], uT[:, kc, gs + k:gs + k + gn],
                                   k == 0 and kc == 0, k == 3 and kc == 7, [b_wt, b_u], pb)
                        bc = colp_t[:, CP_BCB + cc0 // 128:CP_BCB + cc0 // 128 + 1]
                        act(dstT[:, g, gs:gs + gn], pa[:, :gn], AF.Silu, pb + [b_colp], [dstb], bias=bc)
        wdt = wt[:, 0, :, 0:32]
        load_cast(wdt, wi[:, :, OFF_B + 5120:OFF_B + 5152], [b_wt])
        for i in range(NT):
            pa, pb = psum(1)
            for kc in range(8):
                mm(pa[:, :32], uT[:, kc, SHIFT + i * 128:SHIFT + (i + 1) * 128], wdt[:, kc, :], kc == 0, False,
                   [b_wt, b_u], pb)
            mm(pa[:, :32], ones_f[0:1, :], brow_t[0:1, 3072:3104], False, True, [b_const, b_brow], pb)
            e = ev[evc % 2]
            be = b_ev[evc % 2]
            evc += 1
            act(e[:, :32], pa[:, :32], AF.Exp, pb, [be])
            ts("dve", e[:, :32], e[:, :32], 1.0, 1.0, ALU.mult, ALU.add, [be], [be])
            act(e[:, :32], e[:, :32], AF.Ln, [be], [be])
            dma("sp", s_mx[i * 128:(i + 1) * 128, 4608:4640], e[:, :32], [be], [smx_b[i]])
        R.barrier()
        A.release(m1)
        rb = A.alloc([128, NROWB], F32)
        b_rb = Buf("rb")
        dma("sp", rb, rowb[l], (), [b_rb])
        ng_bc = rb[:, 0:2048]
        D_bc = rb[:, 2048:2080]
        A_bc = rb[:, 2080:2112]
        act(A_bc, A_bc, AF.Exp, [b_rb], [b_rb])
        ts("dve", A_bc, A_bc, -1.0, None, ALU.mult, ALU.bypass, [b_rb], [b_rb])
        tri = A.alloc([128, 128], F32)
        sfx = A.alloc([128, 128], F32)
        negU = A.alloc([128, 512], F32)
        caus = A.alloc([128, 128], F32)
        b_k = Buf("ssmconst")
        dma("sp", tri, cst["tri128"], (), [b_k])
        dma("sp", sfx, cst["sfx128"], (), [b_k])
        dma("sp", negU, cst["negU"], (), [b_k])
        dma("sp", caus, cst["causal"], (), [b_k])
        S32 = A.alloc([128, 4, 512], F32)
        Sbf = A.alloc([128, 4, 512], BF16)
        b_S32, b_Sbf = Buf("S32"), Buf("Sbf")
        memset("pool", S32, 0.0, [b_S32])
        memset("pool", Sbf, 0.0, [b_Sbf])
        sm = A.alloc([128, 8, 32], F32)
        b_sm = Buf("sm")
        dtt, aa, acs, nacs, eacs, eend, cdb, dte = [sm[:, j, :] for j in range(8)]
        ss = A.alloc([128, 8], F32)
        b_ss = Buf("ss")
        Rt = A.alloc([128, 4, 128], F32)
        b_Rt = Buf("Rt")
        xs = A.alloc([128, 512], F32)
        zs = A.alloc([128, 512], F32)
        Bt = A.alloc([128, 128], F32)
        Btb = A.alloc([128, 128], BF16)
        b_xs, b_zs, b_Bt, b_Btb = Buf("xs"), Buf("zs"), Buf("Bt"), Buf("Btb")
        xdt = A.alloc([128, 8, 64], BF16)
        xde = A.alloc([128, 8, 64], BF16)
        b_xdt, b_xde = Buf("xdt"), Buf("xde")
        CBm = A.alloc([128, 128], F32)
        b_CBm = Buf("CBm")
        E = A.alloc([128, 4, 128], F32)
        MT = A.alloc([128, 4, 128], BF16)
        b_E, b_MT = Buf("E"), Buf("MT")
        yt = A.alloc([128, 8, 64], F32)
        t3 = A.alloc([128, 8, 64], F32)
        junk = A.alloc([128, 512], F32)
        b_yt, b_t3, b_junk = Buf("yt"), Buf("t3"), Buf("junk")
        ybf = A.alloc([128, 2048], BF16)
        b_ybf = Buf("ybf")
        ybT = A.alloc([128, 16, 128], BF16)
        b_ybT = Buf("ybT")
        for i in range(NT):
            tc0 = i * 128
            dma("sp", dtt, s_mx[tc0:tc0 + 128, 4608:4640], [smx_b[i]], [b_sm])
            tt("dve", aa, dtt, A_bc, ALU.mult, [b_sm, b_rb], [b_sm])
            p1, p1b = psum(1)
            mm(p1[:, 0:32], tri, aa, True, True, [b_k, b_sm], p1b)
            mm(p1[:, 32:64], sfx, aa, True, True, [b_k, b_sm], p1b)
            mm(p1[:, 64:96], ones_f, aa, True, True, [b_const, b_sm], p1b)
            cp("dve", acs, p1[:, 0:32], p1b, [b_sm])
            ts("dve", nacs, p1[:, 0:32], -1.0, None, ALU.mult, ALU.bypass, p1b, [b_sm])
            act(eacs, p1[:, 0:32], AF.Exp, p1b, [b_sm])
            act(eend, p1[:, 32:64], AF.Exp, p1b, [b_sm])
            act(cdb, p1[:, 64:96], AF.Exp, p1b, [b_sm])
            tt("dve", dte, dtt, eend, ALU.mult, [b_sm], [b_sm])
            for g in range(4):
                dma("sp", xs, s_mx[tc0:tc0 + 128, 2048 + g * 512:2048 + (g + 1) * 512], [smx_b[i]], [b_xs])
                dma("sp", zs, s_mx[tc0:tc0 + 128, g * 512:(g + 1) * 512], [smx_b[i]], [b_zs])
                dma("sp", Bt, s_mx[tc0:tc0 + 128, 4096 + g * 128:4096 + (g + 1) * 128], [smx_b[i]], [b_Bt])
                xs3 = xs.rearrange("p (r q) -> p r q", q=64)
                tt("dve", xdt, xs3, dtt[:, g * 8:(g + 1) * 8].unsqueeze(2).to_broadcast([128, 8, 64]), ALU.mult,
                   [b_xs, b_sm], [b_xdt])
                tt("pool", xde, xs3, dte[:, g * 8:(g + 1) * 8].unsqueeze(2).to_broadcast([128, 8, 64]), ALU.mult,
                   [b_xs, b_sm], [b_xde])
                cp("pool", Btb, Bt, [b_Bt], [b_Btb])
                pcb, pcbb = psum(1)
                mm(pcb[:, :128], BT[:, g, tc0:tc0 + 128], CT[:, g, tc0:tc0 + 128], True, True, [b_BT, b_CT], pcbb)
                tt("dve", CBm, pcb[:, :128], caus, ALU.mult, pcbb + [b_k], [b_CBm])
                yd, ydb = psum_pin()
                for blk in range(2):
                    r0 = g * 8 + blk * 4
                    tt("pool", Rt, tri.unsqueeze(1).to_broadcast([128, 4, 128]),
                       aa[:, r0:r0 + 4].unsqueeze(2).to_broadcast([128, 4, 128]), ALU.mult, [b_k, b_sm], [b_Rt])
                    pe_, peb = psum(1)
                    mm(pe_[:, :512], ones_f, Rt.rearrange("p a b -> p (a b)"), True, False, [b_const, b_Rt], peb)
                    mm(pe_[:, :512], ident_f, negU, False, True, [b_const, b_k], peb)
                    for r in range(4):
                        act(E[:, r, :], pe_[:, r * 128:(r + 1) * 128], AF.Exp, peb + [b_sm], [b_E],
                            bias=nacs[:, r0 + r:r0 + r + 1])
                    tt("dve", MT, E, CBm.unsqueeze(1).to_broadcast([128, 4, 128]), ALU.mult, [b_E, b_CBm], [b_MT])
                    for r in range(4):
                        hh = blk * 4 + r
                        mm(yd[:, hh * 64:(hh + 1) * 64], MT[:, r, :], xdt[:, hh, :], True, True, [b_MT, b_xdt], ydb)
                po, pob = psum(1)
                mm(po[:, :512], CT[:, g, tc0:tc0 + 128], Sbf[:, g, :], True, True, [b_CT, b_Sbf], pob)
                tt("dve", yt, po[:, :512].rearrange("p (r q) -> p r q", q=64),
                   eacs[:, g * 8:(g + 1) * 8].unsqueeze(2).to_broadcast([128, 8, 64]), ALU.mult, pob + [b_sm], [b_yt])
                yt2 = yt.rearrange("p r q -> p (r q)")
                tt("dve", yt2, yt2, yd[:, :512], ALU.add, [b_yt] + ydb, [b_yt])
                tt("pool", t3, xs3, D_bc[:, g * 8:(g + 1) * 8].unsqueeze(2).to_broadcast([128, 8, 64]), ALU.mult,
                   [b_xs, b_rb], [b_t3])
                tt("pool", yt, yt, t3, ALU.add, [b_yt, b_t3], [b_yt])
                tt("dve", yt2, yt2, zs, ALU.mult, [b_yt, b_zs], [b_yt])
                act(junk, yt2, AF.Square, [b_yt], [b_junk])
                R.op("dve", lambda e, g=g: e.tensor_reduce(ss[:, g:g + 1], junk, AX.X, ALU.add), [b_junk], [b_ss])
                ts("dve", ss[:, g:g + 1], ss[:, g:g + 1], 1.0 / 512, EPS, ALU.mult, ALU.add, [b_ss], [b_ss])
                act(ss[:, g:g + 1], ss[:, g:g + 1], AF.Ln, [b_ss], [b_ss])
                act(ss[:, g:g + 1], ss[:, g:g + 1], AF.Exp, [b_ss], [b_ss], scale=-0.5)
                stt(ybf[:, g * 512:(g + 1) * 512], yt2, ss[:, g:g + 1], ng_bc[:, g * 512:(g + 1) * 512], ALU.mult,
                    ALU.mult, [b_yt, b_ss, b_rb], [b_ybf])
                pst, pstb = psum(1)
                mm(pst[:, :512], Btb, xde.rearrange("p r q -> p (r q)"), True, True, [b_Btb, b_xde], pstb)
                S3 = S32[:, g, :].rearrange("p (r q) -> p r q", q=64)
                tt("dve", S3, S3, cdb[:, g * 8:(g + 1) * 8].unsqueeze(2).to_broadcast([128, 8, 64]), ALU.mult,
                   [b_S32, b_sm], [b_S32])
                tt("dve", S32[:, g, :], S32[:, g, :], pst[:, :512], ALU.add, [b_S32] + pstb, [b_S32])
                cp("act", Sbf[:, g, :], S32[:, g, :], [b_S32], [b_Sbf])
            for half in range(2):
                pt, ptb = psum(1)
                ptv = pt.bitcast(BF16)
                for c in range(8):
                    cc = half * 8 + c
                    tr(ptv[:, c * 128:(c + 1) * 128], ybf[:, cc * 128:(cc + 1) * 128], ident_b, [b_ybf, b_const], ptb)
                cp("act" if half else "dve", ybT[:, half * 8:(half + 1) * 8, :].rearrange("p a b -> p (a b)"),
                   ptv[:, :1024], ptb, [b_ybT])
            gi = [k for k, (gs, gn) in enumerate(groups) if gs <= tc0 < gs + gn][0]
            dma("sp", s_y[1024:3072, tc0:tc0 + 128].rearrange("(a p) t -> p a t", p=128), ybT, [b_ybT],
                [sy_buf(kc, gi) for kc in range(8, 24)])
        A.release(m0)


    def phase_rwkv(l, colp_t, b_colp):
        m0 = A.mark()
        wi = w_in[l].rearrange("(a p) f -> p a f", p=128)
        rkv_b = [Buf("rkv%d" % i) for i in range(NT)]
        lo_b = [Buf("lo%d" % g) for g in range(len(groups))]
        CBW = 256
        stage = A.alloc([128, 8, CBW], F32)
        tmpw = A.alloc([128, 8, CBW], F32)
        mut = A.alloc([128, CBW], F32)
        wt = A.alloc([128, 2, 8, CBW], BF16)
        b_stage, b_tmpw, b_mut, b_wt = Buf("stage"), Buf("tmpw"), Buf("mut"), Buf("wt")
        ev = [A.alloc([128, 512], F32) for _ in range(2)]
        b_ev = [Buf("ev0"), Buf("ev1")]
        lo_sb = A.alloc([128, 512], BF16)
        b_lo_sb = Buf("lo_sb")
        evc = 0

        def make_w(src_ap, mu_ap, n):
            dma("sp", stage[:, :, :n], src_ap, (), [b_stage])
            dma("sp", mut[:, :n], mu_ap, (), [b_mut])
            tt("dve", tmpw[:, :, :n], stage[:, :, :n], mut[:, :n].unsqueeze(1).to_broadcast([128, 8, n]), ALU.mult,
               [b_stage, b_mut], [b_tmpw])
            tt("pool", wt[:, 0, :, :n], stage[:, :, :n], tmpw[:, :, :n], ALU.subtract, [b_stage, b_tmpw], [b_wt])
            cp("act", wt[:, 1, :, :n], tmpw[:, :, :n], [b_tmpw], [b_wt])

        for cb in range(3072 // CBW):
            c0 = cb * CBW
            make_w(wi[:, :, c0:c0 + CBW], mu_all[l][:, c0:c0 + CBW], CBW)
            for i in range(NT):
                pa, pb = psum(1)
                for kc in range(8):
                    mm(pa[:, :CBW], uT[:, kc, SHIFT + i * 128:SHIFT + (i + 1) * 128], wt[:, 0, kc, :], kc == 0, False,
                       [b_wt, b_u], pb)
                for kc in range(8):
                    mm(pa[:, :CBW], uT[:, kc, SHIFT - 1 + i * 128:SHIFT - 1 + (i + 1) * 128], wt[:, 1, kc, :], False,
                       kc == 7, [b_wt, b_u], pb)
                e = ev[evc % 2]
                be = b_ev[evc % 2]
                evc += 1
                cp("act" if evc % 2 else "dve", e[:, :CBW], pa[:, :CBW], pb, [be])
                dma("sp", s_rkv[i * 128:(i + 1) * 128, c0:c0 + CBW], e[:, :CBW], [be], [rkv_b[i]])
                if l == 0 and c0 >= 2048:
                    dma("sp", s_vf[i * 128:(i + 1) * 128, c0 - 2048:c0 - 2048 + CBW], e[:, :CBW], [be], [b_vf[i]])
        blocks = [("A", wi[:, :, 3072:3200], mu_all[l][:, 3072:3200], 128),
                  ("G", wi[:, :, 3200:3328], mu_all[l][:, 3200:3328], 128)]
        if l > 0:
            blocks.append(("V", w_vres[l - 1].rearrange("(a p) f -> p a f", p=128), mu_all[l][:, 3328:3360], 32))
        for bi, (nm, wsrc, musrc, n) in enumerate(blocks):
            make_w(wsrc, musrc, n)
            for gi, (gs, gn) in enumerate(groups):
                pa, pb = psum(1)
                for kc in range(8):
                    mm(pa[:n, :gn], wt[:, 0, kc, :n], uT[:, kc, SHIFT + gs:SHIFT + gs + gn], kc == 0, False,
                       [b_wt, b_u], pb)
                for kc in range(8):
                    mm(pa[:n, :gn], wt[:, 1, kc, :n], uT[:, kc, SHIFT - 1 + gs:SHIFT - 1 + gs + gn], False, kc == 7,
                       [b_wt, b_u], pb)
                if nm == "A":
                    act(lo_sb[0:64, :gn], pa[0:64, :gn], AF.Tanh, pb, [b_lo_sb])
                    cp("act", lo_sb[64:128, :gn], pa[64:128, :gn], pb, [b_lo_sb])
                elif nm == "G":
                    act(lo_sb[:, :gn], pa[:, :gn], AF.Sigmoid, pb, [b_lo_sb])
                else:
                    cp("act", lo_sb[0:32, :gn], pa[0:32, :gn], pb, [b_lo_sb])
                dma("sp", s_lo[bi * 128:bi * 128 + n, gs:gs + gn], lo_sb[:n, :gn], [b_lo_sb], [lo_b[gi]])
        R.barrier()
        A.release(m0)
        hv = hT.rearrange("p a t -> p (a t)")
        uv = uT.rearrange("p a t -> p (a t)")
        dma("sp", s_hT, hv, b_h, [Buf("s_hT")])
        dma("sp", s_uT, uv, [b_u], [Buf("s_uT")])
        R.barrier()
        A2 = A.sub(0, hu_bytes) if hu_bytes >= 100000 else A
        g2t = A2.alloc([128, 1024], BF16)
        b_w2 = Buf("w2")
        wa2z = A2.alloc([128, 2, 1024], BF16)
        b_wa2z = Buf("wa2z")
        load_cast(g2t, rg2[l], [b_w2])
        memset("pool", wa2z, 0.0, [b_wa2z])
        load_cast(wa2z[0:64, 0, :], rw2[l][0:64, :], [b_wa2z])
        load_cast(wa2z[64:128, 1, :], rw2[l][64:128, :], [b_wa2z])
        v2z = A2.alloc([128, 1024], BF16)
        b_v2z = Buf("v2z")
        memset("pool", v2z, 0.0, [b_v2z])
        if l > 0:
            load_cast(v2z[0:32, :], rv2[l - 1], [b_v2z])
        kst = A2.alloc([128, 128 * 3 + 256], F32)
        b_kst = Buf("kst")
        tri, sfxm, nmask2 = kst[:, 0:128], kst[:, 128:256], kst[:, 256:384]
        mmask2 = kst[:, 384:640]
        dma("sp", tri, cst["tri64"], (), [b_kst])
        dma("sp", sfxm, cst["sfx64"], (), [b_kst])
        dma("sp", nmask2, cst["nmask2"], (), [b_kst])
        dma("sp", mmask2, cst["mmask2"], (), [b_kst])

        def r2_stream(hf, AA):
            ra = AA.alloc([128, 8, 512], F32)
            b_ra = Buf("ra")
            names = ["r", "k", "v", "vf", "s", "a", "kk", "kp", "b", "g", "e1", "e2", "e3", "e4", "t1", "t2"]
            alias = {"e4": "e1", "vf": "e2", "e3": "a"}
            T = {n: AA.alloc([128, 512], F32) for n in names if n not in alias}
            Bf = {n: Buf("T_" + n) for n in names if n not in alias}
            for n_, o_ in alias.items():
                T[n_] = T[o_]
                Bf[n_] = Bf[o_]
            bnames = ["at", "rt", "bt", "kt", "Vt", "Bz0", "Bz1", "Kz0", "Kz1"]
            TB = {n: AA.alloc([128, 512], BF16) for n in bnames}
            BB = {n: Buf("TB_" + n) for n in bnames}
            ARTz = [AA.alloc([128, 4, 2, 128], BF16) for _ in range(2)]
            b_ARTz = Buf("ARTz")
            BKT = AA.alloc([128, 4, 2, 128], BF16)
            b_BKT = Buf("BKT")
            Mb = AA.alloc([128, 8, 2, 128], BF16)
            Mk = AA.alloc([128, 8, 2, 128], BF16)
            b_Mb, b_Mk = Buf("Mb"), Buf("Mk")
            Nn = [AA.alloc([128, 8, 128], BF16) for _ in range(2)]
            NTt = [AA.alloc([128, 8, 128], BF16) for _ in range(2)]
            Pp = AA.alloc([128, 8, 128], BF16)
            b_Nn = [Buf("N0"), Buf("N1")]
            b_NTt = [Buf("NT0"), Buf("NT1")]
            b_Pp = Buf("Pp")
            Wsb = AA.alloc([128, 8, 64], BF16)
            Usb = AA.alloc([128, 8, 64], BF16)
            b_Wsb, b_Usb = Buf("Wsb"), Buf("Usb")
            S32 = AA.alloc([128, 4, 64], F32)
            Sbf = AA.alloc([128, 4, 64], BF16)
            b_S32, b_Sbf = Buf("S32"), Buf("Sbf")
            Ysb = AA.alloc([128, 8, 64], F32)
            b_Ysb = Buf("Ysb")
            ya = AA.alloc([128, 512], BF16)
            yaT = AA.alloc([128, 4, 128], BF16)
            b_ya, b_yaT = Buf("ya"), Buf("yaT")
            lo_t = AA.alloc([128, 3, 128], BF16)
            b_lo_t = Buf("lo_t")
            st = AA.alloc([128, 8, 8], F32)
            b_st = Buf("st")
            ssq, bon, s1, s2, mean, varr = [st[:, j, :] for j in range(6)]
            gC = AA.alloc([128, 8], F32)
            b_gC = Buf("gC")
            memset("pool", ARTz[0], 0.0, [b_ARTz])
            memset("pool", ARTz[1], 0.0, [b_ARTz])
            memset("pool", Wsb, 0.0, [b_Wsb])
            memset("pool", Usb, 0.0, [b_Usb])
            memset("pool", lo_t, 0.0, [b_lo_t])
            slo = s_lo.rearrange("(a p) t -> p a t", p=128)
            rav = rowa[l].rearrange("p (j f) -> p j f", f=1024)
            ind = [tri[:, 63:64], tri[:, 127:128]]

            def bc8(ap8):
                return ap8.unsqueeze(2).to_broadcast([128, 8, 64])

            def v3(ap):
                return ap.rearrange("p (h q) -> p h q", q=64)

            f0 = hf * 512
            dma("sp", ra, rav[:, 0:8, f0:f0 + 512], (), [b_ra])
            w0b, a0b, kkb, kab, rkb, lgb, lbb, v0b = [ra[:, j, :] for j in range(8)]
            memset("pool", S32, 0.0, [b_S32])
            memset("pool", Sbf, 0.0, [b_Sbf])
            for i in range(NT):
                tc0 = i * 128
                gi = [k for k, (gs, gn) in enumerate(groups) if gs <= tc0 < gs + gn][0]
                dma("sp", T["r"], s_rkv[tc0:tc0 + 128, f0:f0 + 512], [rkv_b[i]], [Bf["r"]])
                dma("sp", T["k"], s_rkv[tc0:tc0 + 128, 1024 + f0:1024 + f0 + 512], [rkv_b[i]], [Bf["k"]])
                dma("sp", T["v"], s_rkv[tc0:tc0 + 128, 2048 + f0:2048 + f0 + 512], [rkv_b[i]], [Bf["v"]])
                nlo = 3 if l > 0 else 2
                for j in range(nlo):
                    npart = 128 if j < 2 else 32
                    dma("sp", lo_t[0:npart, j, :], slo[0:npart, j, tc0:tc0 + 128], [lo_b[gi]], [b_lo_t])
                pw, pwb = psum(1)
                mm(pw[:, :512], lo_t[:, 0, :], wa2z[:, 0, f0:f0 + 512], True, False, [b_lo_t, b_wa2z], pwb)
                mm(pw[:, :512], ones_f[0:1, :], ra[0:1, 0, :], False, True, [b_const, b_ra], pwb)
                pa_, pab = psum(1)
                mm(pa_[:, :512], lo_t[:, 0, :], wa2z[:, 1, f0:f0 + 512], True, False, [b_lo_t, b_wa2z], pab)
                mm(pa_[:, :512], ones_f[0:1, :], ra[0:1, 1, :], False, True, [b_const, b_ra], pab)
                pg, pgb = psum(1)
                mm(pg[:, :512], lo_t[:, 1, :], g2t[:, f0:f0 + 512], True, True, [b_lo_t, b_w2], pgb)
                act(T["s"], pw[:, :512], AF.Sigmoid, pwb, [Bf["s"]])
                act(T["a"], pa_[:, :512], AF.Sigmoid, pab, [Bf["a"]])
                cp("act", T["g"], pg[:, :512], pgb, [Bf["g"]])
                if l > 0:
                    dma("sp", T["vf"], s_vf[tc0:tc0 + 128, f0:f0 + 512], [b_vf[i]], [Bf["vf"]])
                    pvr, pvrb = psum(1)
                    mm(pvr[:, :512], lo_t[:, 2, :], v2z[:, f0:f0 + 512], True, False, [b_lo_t, b_v2z], pvrb)
                    mm(pvr[:, :512], ones_f[0:1, :], ra[0:1, 7, :], False, True, [b_const, b_ra], pvrb)
                    act(T["t1"], pvr[:, :512], AF.Sigmoid, pvrb, [Bf["t1"]])
                    tt("pool", T["t2"], T["vf"], T["v"], ALU.subtract, [Bf["vf"], Bf["v"]], [Bf["t2"]])
                    tt("dve", T["t2"], T["t2"], T["t1"], ALU.mult, [Bf["t2"], Bf["t1"]], [Bf["t2"]])
                    tt("pool", T["v"], T["v"], T["t2"], ALU.add, [Bf["v"], Bf["t2"]], [Bf["v"]])
                tt("pool", T["kk"], T["k"], kkb, ALU.mult, [Bf["k"], b_ra], [Bf["kk"]])
                act(T["t1"], T["kk"], AF.Square, [Bf["kk"]], [Bf["t1"]])
                R.op("dve", lambda e: e.tensor_reduce(ssq, v3(T["t1"]), AX.X, ALU.add), [Bf["t1"]], [b_st])
                ts("dve", ssq, ssq, 1e-24, None, ALU.max, ALU.bypass, [b_st], [b_st])
                act(ssq, ssq, AF.Ln, [b_st], [b_st])
                act(ssq, ssq, AF.Exp, [b_st], [b_st], scale=-0.5)
                tt("dve", v3(T["kk"]), v3(T["kk"]), bc8(ssq), ALU.mult, [Bf["kk"], b_st], [Bf["kk"]])
                stt(T["t1"], T["a"], -1.0, kab, ALU.add, ALU.mult, [Bf["a"], b_ra], [Bf["t1"]])
                stt(T["kp"], T["t1"], 1.0, T["k"], ALU.add, ALU.mult, [Bf["t1"], Bf["k"]], [Bf["kp"]])
                tt("pool", T["b"], T["kk"], T["a"], ALU.mult, [Bf["kk"], Bf["a"]], [Bf["b"]])
                tt("pool", T["t2"], T["r"], rkb, ALU.mult, [Bf["r"], b_ra], [Bf["t2"]])
                tt("dve", T["t2"], T["t2"], T["kp"], ALU.mult, [Bf["t2"], Bf["kp"]], [Bf["t2"]])
                R.op("dve", lambda e: e.tensor_reduce(bon, v3(T["t2"]), AX.X, ALU.add), [Bf["t2"]], [b_st])
                pcs, pcsb = psum(1)
                mm(pcs[:, :512], tri, T["s"], True, True, [b_kst, Bf["s"]], pcsb)
                psf, psfb = psum(1)
                mm(psf[:, :512], sfxm, T["s"], True, True, [b_kst, Bf["s"]], psfb)
                pgc, pgcb = psum(1)
                for pr in range(4):
                    mm(pgc[:, pr * 2:pr * 2 + 2], T["s"][:, pr * 128:(pr + 1) * 128], tri[:, 63:128:64], True, True,
                       [Bf["s"], b_kst], pgcb)
                act(gC, pgc[:, 0:8], AF.Exp, pgcb, [b_gC], scale=-WDEC)
                act(T["e1"], pcs[:, :512], AF.Exp, pcsb, [Bf["e1"]], scale=-WDEC)
                tt("pool", TB["rt"], T["r"], T["e1"], ALU.mult, [Bf["r"], Bf["e1"]], [BB["rt"]])
                act(T["e2"], pcs[:, :512], AF.Exp, pcsb, [Bf["e2"]], scale=WDEC)
                tt("dve", T["t1"], pcs[:, :512], T["s"], ALU.subtract, pcsb + [Bf["s"]], [Bf["t1"]])
                act(T["e3"], T["t1"], AF.Exp, [Bf["t1"]], [Bf["e3"]], scale=-WDEC)
                act(T["e4"], psf[:, :512], AF.Exp, psfb, [Bf["e4"]], scale=-WDEC)
                stt(TB["at"], T["kk"], -1.0, T["e3"], ALU.mult, ALU.mult, [Bf["kk"], Bf["e3"]], [BB["at"]])
                tt("dve", TB["bt"], T["b"], T["e2"], ALU.mult, [Bf["b"], Bf["e2"]], [BB["bt"]])
                tt("pool", TB["kt"], T["kp"], T["e2"], ALU.mult, [Bf["kp"], Bf["e2"]], [BB["kt"]])
                tt("dve", T["t1"], T["b"], T["e4"], ALU.mult, [Bf["b"], Bf["e4"]], [Bf["t1"]])
                tt("pool", T["t2"], T["kp"], T["e4"], ALU.mult, [Bf["kp"], Bf["e4"]], [Bf["t2"]])
                for c in range(2):
                    ts("dve", TB["Bz%d" % c], T["t1"], ind[c], None, ALU.mult, ALU.bypass, [Bf["t1"], b_kst],
                       [BB["Bz%d" % c]])
                    ts("pool", TB["Kz%d" % c], T["t2"], ind[c], None, ALU.mult, ALU.bypass, [Bf["t2"], b_kst],
                       [BB["Kz%d" % c]])
                cp("act", TB["Vt"], T["v"], [Bf["v"]], [BB["Vt"]])
                pt, ptb = psum(1)
                ptv = pt.bitcast(BF16)
                for pr in range(4):
                    for q, nmq in enumerate(("at", "rt")):
                        tr(ptv[:, (pr * 2 + q) * 128:(pr * 2 + q + 1) * 128], TB[nmq][:, pr * 128:(pr + 1) * 128],
                           ident_b, [BB[nmq], b_const], ptb)
                cp("dve", ARTz[0][0:64].rearrange("p a b c -> p (a b c)"), ptv[0:64, :1024], ptb, [b_ARTz])
                cp("act", ARTz[1][64:128].rearrange("p a b c -> p (a b c)"), ptv[64:128, :1024], ptb, [b_ARTz])
                pt, ptb = psum(1)
                ptv = pt.bitcast(BF16)
                for pr in range(4):
                    for q, nmq in enumerate(("bt", "kt")):
                        tr(ptv[:, (pr * 2 + q) * 128:(pr * 2 + q + 1) * 128], TB[nmq][:, pr * 128:(pr + 1) * 128],
                           ident_b, [BB[nmq], b_const], ptb)
                cp("act", BKT.rearrange("p a b c -> p (a b c)"), ptv[:, :1024], ptb, [b_BKT])
                mk4 = mmask2.rearrange("p (q t) -> p q t", t=128).unsqueeze(1).to_broadcast([128, 4, 2, 128])
                for hg in range(2):
                    pmb, pmbb = psum(2)
                    pmk, pmkb = psum(2)
                    for hl in range(4):
                        h = hg * 4 + hl
                        pr, hh = h // 2, h % 2
                        rhs = ARTz[hh][:, pr, :, :].rearrange("p q t -> p (q t)")
                        mm(pmb[:, hl * 256:(hl + 1) * 256], BKT[:, pr, 0, :], rhs, True, True, [b_BKT, b_ARTz], pmbb)
                        mm(pmk[:, hl * 256:(hl + 1) * 256], BKT[:, pr, 1, :], rhs, True, True, [b_BKT, b_ARTz], pmkb)
                    tt("dve", Mb[:, hg * 4:(hg + 1) * 4], pmb[:, :1024].rearrange("p (h q t) -> p h q t", q=2, t=128), mk4,
                       ALU.mult, pmbb + [b_kst], [b_Mb])
                    tt("dve", Mk[:, hg * 4:(hg + 1) * 4], pmk[:, :1024].rearrange("p (h q t) -> p h q t", q=2, t=128), mk4,
                       ALU.mult, pmkb + [b_kst], [b_Mk])
                pnt, pntb = psum(2)
                for h in range(8):
                    pr, hh = h // 2, h % 2
                    mm(pnt[:, h * 128:(h + 1) * 128], ARTz[hh][:, pr, 0, :], BKT[:, pr, 0, :], True, True,
                       [b_BKT, b_ARTz], pntb)
                tt("dve", NTt[0], pnt[:, :1024].rearrange("p (h t) -> p h t", t=128),
                   nmask2.unsqueeze(1).to_broadcast([128, 8, 128]), ALU.mult, pntb + [b_kst], [b_NTt[0]])
                cp("pool", Nn[0], Mb[:, :, 0, :], [b_Mb], [b_Nn[0]])
                tt("pool", Pp, Mb[:, :, 0, :], ident_b.unsqueeze(1).to_broadcast([128, 8, 128]), ALU.add,
                   [b_Mb, b_const], [b_Pp])
                cur = 0
                for lev in range(5):
                    nx = 1 - cur
                    pN, pNb = psum(2)
                    pNT, pNTb = psum(2)
                    for h in range(8):
                        hs2 = slice(h * 128, (h + 1) * 128)
                        mm(pN[:, hs2], NTt[cur][:, h, :], Nn[cur][:, h, :], True, True, [b_NTt[cur], b_Nn[cur]], pNb)
                        mm(pNT[:, hs2], Nn[cur][:, h, :], NTt[cur][:, h, :], True, True, [b_NTt[cur], b_Nn[cur]], pNTb)
                    cp("dve", Nn[nx].rearrange("p h t -> p (h t)"), pN[:, :1024], pNb, [b_Nn[nx]])
                    cp("act", NTt[nx].rearrange("p h t -> p (h t)"), pNT[:, :1024], pNTb, [b_NTt[nx]])
                    pP, pPb = psum(2)
                    for h in range(8):
                        hs2 = slice(h * 128, (h + 1) * 128)
                        mm(pP[:, hs2], ident_b, Pp[:, h, :], True, False, [b_const, b_Pp], pPb)
                        mm(pP[:, hs2], NTt[nx][:, h, :], Pp[:, h, :], False, True, [b_NTt[nx], b_Pp], pPb)
                    cp("act", Pp.rearrange("p h t -> p (h t)"), pP[:, :1024], pPb, [b_Pp])
                    cur = nx
                for c in range(2):
                    cs_ = slice(c * 64, (c + 1) * 64)
                    pW, pWb = psum(1)
                    for h in range(8):
                        pr, hh = h // 2, h % 2
                        hs = slice(h * 64, (h + 1) * 64)
                        mm(pW[:, hs], ARTz[hh][:, pr, 0, :], Sbf[:, pr, :], True, False, [b_ARTz, b_Sbf], pWb)
                        mm(pW[:, hs], Mk[:, h, 0, :], TB["Vt"][:, hs], False, True, [b_Mk, BB["Vt"]], pWb)
                    cp("dve", Wsb[cs_].rearrange("p h t -> p (h t)"), pW[cs_, :512], pWb, [b_Wsb])
                    pU, pUb = psum(1)
                    for h in range(8):
                        hs = slice(h * 64, (h + 1) * 64)
                        mm(pU[:, hs], Pp[:, h, :], Wsb[:, h, :], True, True, [b_Pp, b_Wsb], pUb)
                    cp("act", Usb[cs_].rearrange("p h t -> p (h t)"), pU[cs_, :512], pUb, [b_Usb])
                    pY, pYb = psum(1)
                    pS, pSb = psum(1)
                    for h in range(8):
                        pr, hh = h // 2, h % 2
                        hs = slice(h * 64, (h + 1) * 64)
                        mm(pY[:, hs], ARTz[hh][:, pr, 1, :], Sbf[:, pr, :], True, False, [b_ARTz, b_Sbf], pYb)
                        mm(pY[:, hs], Mb[:, h, 1, :], Usb[:, h, :], False, False, [b_Mb, b_Usb], pYb)
                        mm(pY[:, hs], Mk[:, h, 1, :], TB["Vt"][:, hs], False, True, [b_Mk, BB["Vt"]], pYb)
                        mm(pS[:, hs], TB["Bz%d" % c][:, pr * 128:(pr + 1) * 128], Usb[:, h, :], True, False,
                           [BB["Bz%d" % c], b_Usb], pSb)
                        mm(pS[:, hs], TB["Kz%d" % c][:, pr * 128:(pr + 1) * 128], TB["Vt"][:, hs], False, True,
                           [BB["Kz%d" % c], BB["Vt"]], pSb)
                    cp("act", Ysb[cs_].rearrange("p h t -> p (h t)"), pY[cs_, :512], pYb, [b_Ysb])
                    gcv = gC.rearrange("p (a c) -> p a c", c=2)[:, :, c:c + 1].to_broadcast([128, 4, 64])
                    tt("dve", S32, S32, gcv, ALU.mult, [b_S32, b_gC], [b_S32])
                    pS4 = pS[:, :512].rearrange("p (a hh v) -> p a hh v", hh=2, v=64)
                    for hh in range(2):
                        rs_ = slice(hh * 64, (hh + 1) * 64)
                        tt("dve", S32[rs_], S32[rs_], pS4[rs_, :, hh, :], ALU.add, [b_S32] + pSb, [b_S32])
                    cp("act", Sbf, S32, [b_S32], [b_Sbf])
                R.op("dve", lambda e: e.tensor_reduce(s1, Ysb, AX.X, ALU.add), [b_Ysb], [b_st])
                act(T["t1"], Ysb.rearrange("p h t -> p (h t)"), AF.Square, [b_Ysb], [Bf["t1"]])
                R.op("dve", lambda e: e.tensor_reduce(s2, v3(T["t1"]), AX.X, ALU.add), [Bf["t1"]], [b_st])
                ts("dve", mean, s1, 1.0 / 64, None, ALU.mult, ALU.bypass, [b_st], [b_st])
                tt("dve", varr, mean, mean, ALU.mult, [b_st], [b_st])
                stt(varr, s2, 1.0 / 64, varr, ALU.mult, ALU.subtract, [b_st], [b_st])
                ts("dve", varr, varr, 1.0, A_LN_EPS, ALU.mult, ALU.add, [b_st], [b_st])
                act(varr, varr, AF.Ln, [b_st], [b_st])
                act(varr, varr, AF.Exp, [b_st], [b_st], scale=-0.5)
                tt("dve", Ysb, Ysb, bc8(mean), ALU.subtract, [b_Ysb, b_st], [b_Ysb])
                tt("dve", Ysb, Ysb, bc8(varr), ALU.mult, [b_Ysb, b_st], [b_Ysb])
                y2 = Ysb.rearrange("p h t -> p (h t)")
                tt("pool", y2, y2, lgb, ALU.mult, [b_Ysb, b_ra], [b_Ysb])
                tt("pool", y2, y2, lbb, ALU.add, [b_Ysb, b_ra], [b_Ysb])
                tt("dve", v3(T["t2"]), v3(T["v"]), bc8(bon), ALU.mult, [Bf["v"], b_st], [Bf["t2"]])
                tt("pool", y2, y2, T["t2"], ALU.add, [b_Ysb, Bf["t2"]], [b_Ysb])
                tt("dve", ya, y2, T["g"], ALU.mult, [b_Ysb, Bf["g"]], [b_ya])
                pt, ptb = psum(1)
                ptv = pt.bitcast(BF16)
                for pr in range(4):
                    tr(ptv[:, pr * 128:(pr + 1) * 128], ya[:, pr * 128:(pr + 1) * 128], ident_b, [b_ya, b_const], ptb)
                cp("act", yaT.rearrange("p a t -> p (a t)"), ptv[:, :512], ptb, [b_yaT])
                dma("sp", s_y[f0:f0 + 512, tc0:tc0 + 128].rearrange("(a p) t -> p a t", p=128), yaT, [b_yaT],
                    [sy_buf(hf * 4 + kc, gi) for kc in range(4)])

        n0 = len(R.ops)
        ps_lim[0], ps_lim[1] = 0, 4
        r2_stream(0, A)
        n1 = len(R.ops)
        ps_lim[0], ps_lim[1] = 4, 8
        r2_stream(1, A2)
        n2 = len(R.ops)
        ps_lim[0], ps_lim[1] = 0, 6
        R.interleave(n0, n1, n2)
        R.barrier()
        dma("sp", hv, s_hT, (), b_h)
        dma("sp", uv, s_uT, (), [b_u])
        R.barrier()
        A.release(m0)

    PHASES_PLACEHOLDER = None

    for l in range(depth):
        mL = A.mark()
        colp_t = A.alloc([128, NCOLP], F32)
        b_colp = Buf("colp")
        dma("sp", colp_t, colp[l], (), [b_colp])
        phase_norm(l, 0, colp_t=colp_t, b_colp=b_colp)
        R.barrier()
        if l == 0 and cfg.do_merge:
            zr = []
            if not cfg.do_rwkv:
                zr += list(range(0, 8))
            if not cfg.do_ssm:
                zr += list(range(8, 24))
            if not cfg.do_ret:
                zr += list(range(24, 32))
            if zr:
                mz = A.mark()
                zt = A.alloc([128, 512], BF16)
                b_zt = Buf("zt")
                memset("pool", zt, 0.0, [b_zt])
                for kc in zr:
                    for gi, (gs, gn) in enumerate(groups):
                        dma("sp", s_y[kc * 128:(kc + 1) * 128, gs:gs + gn], zt[:, :gn], [b_zt], [sy_buf(kc, gi)])
                R.barrier()
                A.release(mz)
        if cfg.do_rwkv:
            R.phase = "L%d_rwkv" % l
            phase_rwkv(l, colp_t, b_colp)
            R.barrier()
        if cfg.do_ssm:
            R.phase = "L%d_ssm" % l
            phase_ssm(l, colp_t, b_colp)
            R.barrier()
        if cfg.do_ret:
            R.phase = "L%d_ret" % l
            phase_ret(l, colp_t, b_colp)
            R.barrier()
        def dump(slot):
            if debug:
                for gi, (gs, gn) in enumerate(groups):
                    dma("sp", dbg_h[slot].rearrange("(a p) t -> p a t", p=128)[:, :, gs:gs + gn], hT[:, :, gs:gs + gn],
                        [b_h[gi]], [Buf("dbg")])
                R.barrier()
        def zero_pad():
            if TP > L:
                memset("pool", hT[:, :, L:TP], 0.0, [b_h[len(groups) - 1]])
        if cfg.do_merge:
            R.phase = "L%d_merge" % l
            phase_merge(l, colp_t, b_colp)
            zero_pad()
            R.barrier()
        dump(l * 3)
        if cfg.do_ffn:
            R.phase = "L%d_ffn" % l
            phase_ffn(l, colp_t, b_colp)
            zero_pad()
            R.barrier()
        dump(l * 3 + 1)
        A.release(mL)

    m = A.mark()
    nf_t = A.alloc([128, 8], F32)
    b_nf = Buf("nf")
    dma("sp", nf_t, nfin, (), [b_nf])
    ot = A.alloc([128, 8, 512], F32)
    b_ot = Buf("ot")
    for gi, (gs, gn) in enumerate(groups):
        phase_norm(0, 0, gsel=[gi], dst=ot, dst_b=b_ot, colp_t=nf_t, b_colp=b_nf)
        lo = max(gs, NM)
        hi = min(gs + gn, L)
        if hi > lo:
            dma("sp", outT.rearrange("(a p) t -> p a t", p=128)[:, :, lo - NM:hi - NM], ot[:, :, lo - gs:hi - gs],
                [b_ot], [Buf("out%d" % gi)])
    A.release(m)
    R.barrier()
    R.op("sp", lambda e: e.nop(), (), ())
    R.emit(nc, None)
    return nc


def _cols(v, n):
    return np.ascontiguousarray(np.asarray(v, np.float32).reshape(n, 128).T)


def _rep(v):
    return np.broadcast_to(np.asarray(v, np.float32).reshape(1, -1), (128, np.asarray(v).size))


_PERM = np.concatenate([np.arange(0, 64, 2), np.arange(1, 64, 2)])
_PART = np.concatenate([_PERM[32:], _PERM[:32]])


def prep_shared(cfg, inp):
    depth = cfg.depth
    f = lambda k: np.asarray(inp[k], np.float32)
    sh = {}
    ch = host_consts(cfg)
    for k in CONST_NAMES:
        sh["c_" + k] = np.ascontiguousarray(ch[k])
    w_in = f("w_in")[:depth]
    sh["w_in"] = np.ascontiguousarray(w_in)
    qk = np.empty((depth, D, 1024), np.float32)
    rot = np.empty((depth, D, 1024), np.float32)
    for part in range(2):
        for h in range(8):
            base = OFF_C + part * 512 + h * 64
            qk[:, :, part * 512 + h * 64:part * 512 + (h + 1) * 64] = w_in[:, :, base + _PERM]
            rot[:, :, part * 512 + h * 64:part * 512 + (h + 1) * 64] = w_in[:, :, base + _PART]
    sh["w_qk"] = qk
    sh["w_rot"] = rot
    nv = max(depth - 1, 1)
    wv = np.zeros((nv, D, 32), np.float32)
    rv2 = np.zeros((nv, 32, D), np.float32)
    if depth > 1:
        wv[:] = f("w_in_vres")[:depth - 1]
        rv2[:] = f("rwkv_v2")[:depth - 1]
    sh["w_vres"] = wv
    sh["rv2"] = rv2
    sh["w_branch"] = np.ascontiguousarray(f("w_branch")[:depth])
    sh["w_out"] = np.ascontiguousarray(f("w_out")[:depth])
    sh["w_up"] = np.ascontiguousarray(f("ffn_w_up")[:depth])
    sh["w_down"] = np.ascontiguousarray(f("ffn_w_down")[:depth])
    sh["rw2"] = np.ascontiguousarray(np.concatenate([f("rwkv_w2")[:depth], f("rwkv_a2")[:depth]], axis=1))
    sh["rg2"] = np.ascontiguousarray(f("rwkv_g2")[:depth])
    colp = []
    rowa = []
    rowb = []
    mu_all = []
    cw_all = []
    brow = []
    for l in range(depth):
        fw = f("ffn_conv_w")[l]
        colp.append(np.concatenate([
            _cols(f("norm_mix")[l], 8), _cols(f("norm_ffn")[l], 8), _cols(f("gate_bias")[l], 24),
            _cols(f("ssm_conv_b")[l][2048:3072], 8),
            _cols(fw[0], 44), _cols(fw[1], 44), _cols(fw[2], 44), _cols(f("ffn_conv_b")[l], 44),
            np.zeros((128, 1), np.float32)], axis=1))
        v0 = f("rwkv_v0")[l - 1] if l > 0 else np.zeros(1024, np.float32)
        rowa.append(np.concatenate([_rep(f("rwkv_w0")[l]), _rep(f("rwkv_a0")[l]), _rep(f("rwkv_k_k")[l]),
                                    _rep(f("rwkv_k_a")[l]), _rep(f("rwkv_r_k")[l].reshape(-1)),
                                    _rep(f("rwkv_ln_g")[l]), _rep(f("rwkv_ln_b")[l]), _rep(v0),
                                    np.zeros((128, 1024), np.float32)], axis=1))
        rowb.append(np.concatenate([_rep(f("ssm_norm_g")[l]), _rep(f("ssm_d")[l]), _rep(f("ssm_a_log")[l]),
                                    np.zeros((128, 32), np.float32)], axis=1))
        muv = f("rwkv_mu_vres")[l - 1] if l > 0 else np.zeros(32, np.float32)
        mu_all.append(_rep(np.concatenate([f("rwkv_mu")[l], muv])))
        cw_all.append(_rep(f("ssm_conv_w")[l].reshape(-1)))
        brow.append(np.concatenate([f("ssm_conv_b")[l], f("ssm_dt_bias")[l]]).reshape(1, -1))
    sh["colp"] = np.ascontiguousarray(np.stack(colp))
    sh["rowa"] = np.ascontiguousarray(np.stack(rowa))
    sh["rowb"] = np.ascontiguousarray(np.stack(rowb))
    sh["mu_all"] = np.ascontiguousarray(np.stack(mu_all))
    sh["cw_all"] = np.ascontiguousarray(np.stack(cw_all))
    sh["brow"] = np.ascontiguousarray(np.stack(brow))
    sh["nfin"] = _cols(f("norm_final"), 8)
    for n in STACKED:
        arr = sh.pop(n)
        for i in range(arr.shape[0]):
            sh["%s_%d" % (n, i)] = np.ascontiguousarray(arr[i])
    return sh


def run(cfg, inp, n_cores=8, debug=False):
    x = np.asarray(inp["x"], np.float32)
    meta = np.asarray(inp["meta"], np.float32)
    bsz = x.shape[0]
    sh = prep_shared(cfg, inp)
    nc = build(cfg, debug=debug)
    in_maps = []
    for b in range(bsz):
        xT = np.zeros((D, cfg.TP), np.float32)
        xT[:, :NM] = meta.T
        xT[:, NM:cfg.L] = x[b].T
        m = dict(sh)
        m["xT"] = xT
        in_maps.append(m)
    res = run_bass_kernel_spmd(nc, in_maps, core_ids=list(range(bsz)))
    out = np.stack([np.ascontiguousarray(r["outT"].T) for r in res.results], axis=0)
    if debug:
        return out.astype(np.float32), [(r["dbg_h"], r["s_y"]) for r in res.results]
    return out.astype(np.float32)


def kernel(**inputs):
    cfg = Cfg(inputs["x"].shape[1], 4)
    return run(cfg, inputs)
```

```python
import math
import numpy as np
import concourse.bass as bass
import concourse.mybir as mybir
from concourse.bass_utils import run_bass_kernel_spmd

F32 = mybir.dt.float32
BF16 = mybir.dt.bfloat16
U8 = mybir.dt.uint8
AF = mybir.ActivationFunctionType
ALU = mybir.AluOpType
AX = mybir.AxisListType

D = 1024
NM = 16
A_IN = 3328
B_IN = 5152
C_IN = 3072
W_IN = 14624
OFF_B = A_IN
OFF_C = A_IN + B_IN
OFF_G = A_IN + B_IN + C_IN
FH = 2816
NFC = 22
EPS = 1e-6
A_LN_EPS = 64e-5
SHIFT = 3
NEGBIG = -30000.0
WDEC = math.exp(-0.5)


class Buf:
    __slots__ = ("name", "w", "r")
    ALL = []

    def __init__(self, name):
        self.name = name
        self.w = None
        self.r = []
        Buf.ALL.append(self)


class Rec:
    ENGS = ("pe", "act", "dve", "pool", "sp")

    def __init__(self):
        self.ops = []
        self.pending = {e: set() for e in self.ENGS}
        self.last = {e: None for e in self.ENGS}
        self.dmas_since = []

    def op(self, eng, fn, reads=(), writes=(), dma=False):
        deps = {}
        for b in reads:
            if b.w is not None:
                deps[b.w] = "raw"
        for b in writes:
            if b.w is not None and b.w not in deps:
                deps[b.w] = "waw"
            for r in b.r:
                if r not in deps:
                    deps[r] = "war"
        for d in self.pending[eng]:
            deps[d] = "raw"
        self.pending[eng] = set()
        i = len(self.ops)
        deps.pop(i, None)
        self.ops.append(dict(eng=eng, fn=fn, dma=dma, deps=deps, id=i, ph=getattr(self, "phase", "")))
        for b in writes:
            b.w = i
            b.r = []
        for b in reads:
            if b.w != i:
                b.r.append(i)
        if dma:
            self.dmas_since.append(i)
        else:
            self.last[eng] = i
        return i

    def interleave(self, a0, a1, b1):
        assert b1 == len(self.ops)
        sa, sb = self.ops[a0:a1], self.ops[a1:b1]
        merged = []
        ia = ib = 0
        na, nb = len(sa), len(sb)
        while ia < na or ib < nb:
            if ib >= nb or (ia < na and ia * nb <= ib * na):
                merged.append(sa[ia]); ia += 1
            else:
                merged.append(sb[ib]); ib += 1
        remap = {}
        for k, o in enumerate(merged):
            remap[o["id"]] = a0 + k
        f = lambda i: remap.get(i, i)
        for o in merged:
            o["deps"] = {f(d): kind for d, kind in o["deps"].items()}
            o["id"] = f(o["id"])
        self.ops[a0:b1] = merged
        for b in Buf.ALL:
            if b.w is not None:
                b.w = f(b.w)
            b.r = [f(x) for x in b.r]
        for e in self.ENGS:
            if self.last[e] is not None:
                self.last[e] = f(self.last[e])
            self.pending[e] = set(f(x) for x in self.pending[e])
        self.dmas_since = [f(x) for x in self.dmas_since]
        for e in self.ENGS:
            cand = [o["id"] for o in merged if o["eng"] == e and not o["dma"]]
            if cand:
                self.last[e] = max(cand)

    def barrier(self):
        ids = set(v for v in self.last.values() if v is not None) | set(self.dmas_since)
        for e in self.ENGS:
            self.pending[e] |= ids
        self.dmas_since = []

    def emit(self, nc, engines):
        ops = self.ops
        per = {e: [o for o in ops if o["eng"] == e] for e in self.ENGS}
        for e in self.ENGS:
            n = 0
            for o in per[e]:
                if not o["dma"]:
                    n += 1
                    o["eidx"] = n
        signal = set()
        for e in self.ENGS:
            waited = {x: 0 for x in self.ENGS}
            wdma = set()
            for o in per[e]:
                cw = {}
                dw = []
                for d, kind in o["deps"].items():
                    P = ops[d]
                    if P["dma"]:
                        if d not in wdma:
                            wdma.add(d)
                            dw.append(d)
                    else:
                        E = P["eng"]
                        if E == e and kind != "raw":
                            continue
                        if P["eidx"] > waited[E]:
                            if E not in cw or ops[cw[E]]["eidx"] < P["eidx"]:
                                cw[E] = d
                for E, d in cw.items():
                    waited[E] = ops[d]["eidx"]
                    signal.add(d)
                o["cw"] = list(cw.values())
                o["dw"] = dw
        EPOCH = 4000
        sems = {}

        def getsem(key):
            if key not in sems:
                sems[key] = nc.alloc_semaphore("s_%s_%d" % key)
            return sems[key]

        for e in self.ENGS:
            r = 0
            for o in per[e]:
                if not o["dma"] and o["id"] in signal:
                    o["sig"] = (e, r // EPOCH, r % EPOCH + 1)
                    r += 1
        NSLOT = {"sp": 32, "pool": 12, "act": 24, "dve": 4, "pe": 4}
        for e in self.ENGS:
            k = 0
            uses = [0] * NSLOT[e]
            for o in per[e]:
                if o["dma"]:
                    s = k % NSLOT[e]
                    o["slot"] = (e, s, uses[s] * 16)
                    uses[s] += 1
                    o["tok"] = (("dma_" + e, s), uses[s] * 16)
                    assert uses[s] * 16 < 4000
                    k += 1

        def run(e, eng):
            for o in per[e]:
                for d in o["cw"]:
                    E, ep, val = ops[d]["sig"]
                    eng.wait_ge(getsem((E, ep)), val)
                for d in o["dw"]:
                    key, val = ops[d]["tok"]
                    eng.wait_ge(getsem(key), val)
                if o["dma"]:
                    _, s, prev = o["slot"]
                    sm = getsem(("dma_" + e, s))
                    if prev > 0:
                        eng.wait_ge(sm, prev)
                    ins = o["fn"](eng)
                    ins.then_inc(sm, 16)
                else:
                    ins = o["fn"](eng)
                    if "sig" in o:
                        E, ep, val = o["sig"]
                        ins.then_inc(getsem((E, ep)), 1)

        for e in self.ENGS:
            for o in per[e]:
                if "sig" in o:
                    getsem((o["sig"][0], o["sig"][1]))
                if o["dma"]:
                    getsem(("dma_" + e, o["slot"][1]))
        with nc.Block() as block:
            @block.tensor
            def _(eng):
                run("pe", eng)

            @block.scalar
            def _(eng):
                run("act", eng)

            @block.vector
            def _(eng):
                run("dve", eng)

            @block.gpsimd
            def _(eng):
                run("pool", eng)

            @block.sync
            def _(eng):
                run("sp", eng)


class Arena:
    def __init__(self, nc, nbytes, t=None, base=0):
        self.t = nc.alloc_sbuf_tensor("arena", [128, nbytes], U8).ap() if t is None else t
        self.n = nbytes
        self.off = base
        self.peak = 0

    def sub(self, base, limit):
        return Arena(None, limit, t=self.t, base=base)

    def alloc(self, shape, dtype):
        esz = 4 if dtype == F32 else 2
        n = 1
        for s in shape[1:]:
            n *= s
        nb = (n * esz + 31) // 32 * 32
        assert self.off + nb <= self.n, "SBUF arena overflow %d + %d" % (self.off, nb)
        v = self.t[0:shape[0], self.off:self.off + n * esz].bitcast(dtype)
        self.off += nb
        self.peak = max(self.peak, self.off)
        if len(shape) == 3:
            v = v.rearrange("p (a b) -> p a b", b=shape[2])
        elif len(shape) == 4:
            v = v.rearrange("p (a b c) -> p a b c", b=shape[2], c=shape[3])
        return v

    def mark(self):
        return self.off

    def release(self, m):
        self.off = m


class Cfg:
    def __init__(self, lx, depth):
        self.lx = lx
        self.depth = depth
        self.L = NM + lx
        self.NT = (self.L + 127) // 128
        self.TP = self.NT * 128
        self.do_rwkv = self.do_ssm = self.do_ret = self.do_merge = self.do_ffn = True
        self.groups = []
        s = 0
        while s < self.TP:
            n = min(512, self.TP - s)
            self.groups.append((s, n))
            s += n


def host_consts(cfg):
    TP = cfg.TP
    c = {}
    c["ident"] = np.eye(128, dtype=np.float32)
    j = np.arange(128)
    blk = (j[:, None] // 64) == (j[None, :] // 64)
    c["tri64"] = (blk & (j[:, None] <= j[None, :])).astype(np.float32)
    c["sfx64"] = (blk & (j[:, None] > j[None, :])).astype(np.float32)
    c["tri128"] = (j[:, None] <= j[None, :]).astype(np.float32)
    c["sfx128"] = (j[:, None] > j[None, :]).astype(np.float32)
    c["negU"] = np.tile(((j[:, None] > j[None, :]) * NEGBIG).astype(np.float32), (1, 4))
    c["causal"] = (j[:, None] <= j[None, :]).astype(np.float32)
    s = j % 64
    m = np.zeros((128, 128), np.float32)
    m[:, :64] = (s[:, None] < s[None, :64])
    m[:, 64:] = (s[:, None] <= s[None, :64])
    c["mmask"] = m
    c["mmaskT"] = np.ascontiguousarray((s[None, :64] < s[:, None]).astype(np.float32))
    cc = j // 64
    c["nmask2"] = ((cc[:, None] == cc[None, :]) & (s[None, :] < s[:, None])).astype(np.float32)
    m2 = np.zeros((128, 2, 128), np.float32)
    same = (cc[:, None] == cc[None, :])
    m2[:, 0, :] = same & (s[:, None] < s[None, :])
    m2[:, 1, :] = same & (s[:, None] <= s[None, :])
    c["mmask2"] = m2.reshape(128, 256)
    c["ident64x2"] = np.concatenate([np.eye(64, dtype=np.float32)] * 2, axis=0)
    pos = np.arange(TP, dtype=np.float32)
    inv = (1.0 / (10000.0 ** np.linspace(0.0, 1.0, 32, dtype=np.float32))).astype(np.float32)
    ang = pos[None, :] * inv[:, None]
    cos = np.cos(ang).astype(np.float32)
    sin = np.sin(ang).astype(np.float32)
    cos64 = np.concatenate([cos, cos], 0)
    sin64 = np.concatenate([-sin, sin], 0)
    c["cosq"] = np.concatenate([cos64, cos64], 0)
    c["sinq"] = np.concatenate([sin64, sin64], 0)
    c["cosk"] = c["cosq"] * np.float32(0.125)
    c["sink"] = c["sinq"] * np.float32(0.125)
    lg = np.log(1.0 - 2.0 ** (-5.0 - np.arange(8, dtype=np.float64)))
    rel = np.arange(TP)[None, :] - np.arange(128)[:, None]
    tabs = []
    for h in range(8):
        tabs.append(np.where(rel >= 0, np.exp(np.maximum(rel, 0) * lg[h]), 0.0).astype(np.float32))
    c["rdec"] = np.stack(tabs, 0)
    return c


STACKED = ["w_in", "w_rot", "w_qk", "w_vres", "w_branch", "w_out", "w_up", "w_down", "rw2", "rg2", "rv2", "colp", "rowa",
           "rowb", "mu_all", "cw_all", "brow"]
CONST_NAMES = ["ident", "tri64", "sfx64", "tri128", "sfx128", "negU", "causal", "mmask", "mmaskT",
               "cosq", "sinq", "cosk", "sink", "rdec", "nmask2", "mmask2"]


def build(cfg, debug=False):
    Buf.ALL = []
    nc = bass.Bass("TRN2", target_bir_lowering=False)
    R = Rec()
    TP, NT, L, depth = cfg.TP, cfg.NT, cfg.L, cfg.depth
    groups = cfg.groups

    def din(name, shape):
        return nc.dram_tensor(name, list(shape), F32, kind="ExternalInput").ap()

    xT = din("xT", [D, TP])
    cst = {}
    ch = host_consts(cfg)
    for k in CONST_NAMES:
        cst[k] = din("c_" + k, ch[k].shape)
    w_in = [din("w_in_%d" % i_, [D, W_IN]) for i_ in range(depth)]
    w_rot = [din("w_rot_%d" % i_, [D, 1024]) for i_ in range(depth)]
    w_qk = [din("w_qk_%d" % i_, [D, 1024]) for i_ in range(depth)]
    w_vres = [din("w_vres_%d" % i_, [D, 32]) for i_ in range(max(depth - 1, 1))]
    w_branch = [din("w_branch_%d" % i_, [4096, D]) for i_ in range(depth)]
    w_out = [din("w_out_%d" % i_, [D, D]) for i_ in range(depth)]
    w_up = [din("w_up_%d" % i_, [D, 2 * FH]) for i_ in range(depth)]
    w_down = [din("w_down_%d" % i_, [FH, D]) for i_ in range(depth)]
    rw2 = [din("rw2_%d" % i_, [128, D]) for i_ in range(depth)]
    rg2 = [din("rg2_%d" % i_, [128, D]) for i_ in range(depth)]
    rv2 = [din("rv2_%d" % i_, [32, D]) for i_ in range(max(depth - 1, 1))]
    NCOLP = 8 + 8 + 24 + 8 + 4 * 44 + 1
    colp = [din("colp_%d" % i_, [128, NCOLP]) for i_ in range(depth)]
    NROWA = 9 * 1024
    rowa = [din("rowa_%d" % i_, [128, NROWA]) for i_ in range(depth)]
    NROWB = 2048 + 32 * 3
    rowb = [din("rowb_%d" % i_, [128, NROWB]) for i_ in range(depth)]
    mu_all = [din("mu_all_%d" % i_, [128, 3328 + 32]) for i_ in range(depth)]
    cw_all = [din("cw_all_%d" % i_, [128, 4 * 3072]) for i_ in range(depth)]
    brow = [din("brow_%d" % i_, [1, 3072 + 32]) for i_ in range(depth)]
    nfin = din("nfin", [128, 8])
    outT = nc.dram_tensor("outT", [D, cfg.lx], F32, kind="ExternalOutput").ap()
    dbg_h = nc.dram_tensor("dbg_h", [depth * 3, D, TP], F32, kind="ExternalOutput").ap() if debug else None
    def dscr(name, shape, dt=F32):
        return nc.dram_tensor(name, list(shape), dt, kind="Internal").ap()

    s_rkv = dscr("s_rkv", [TP, 3072])
    s_vfirst = s_rkv
    s_vf = dscr("s_vf", [TP, 1024])
    s_mx = dscr("s_mx", [TP, 2048 + 2048 + 512 + 32])
    s_y = (nc.dram_tensor("s_y", [4096, TP], BF16, kind="ExternalOutput").ap() if debug else dscr("s_y", [4096, TP], BF16))
    s_lo = dscr("s_lo", [384, TP], BF16)
    b_rkv = [Buf("s_rkv%d" % i) for i in range(NT)]
    b_vf = [Buf("s_vf%d" % i) for i in range(NT)]
    b_mx = [Buf("s_mx%d" % i) for i in range(NT)]
    b_y = [[Buf("s_y%d_%d" % (m, i)) for i in range(NT)] for m in range(3)]

    A = Arena(nc, 208000)
    hT = A.alloc([128, 8, TP], F32)
    b_h = [Buf("h%d" % g) for g in range(len(groups))]
    uT = A.alloc([128, 8, SHIFT + TP], BF16)
    b_u = Buf("uT")
    hu_bytes = A.off
    s_hT = dscr("s_hT", [128, 8 * TP])
    s_uT = dscr("s_uT", [128, 8 * (SHIFT + TP)], BF16)
    ident_f = A.alloc([128, 128], F32)
    ident_b = A.alloc([128, 128], BF16)
    ones_f = A.alloc([128, 128], F32)
    ones_b = A.alloc([128, 128], BF16)
    b_const = Buf("const")
    PS = nc.alloc_psum_tensor("ps", [128, 8, 512], F32).ap()
    b_ps = [Buf("ps%d" % i) for i in range(8)]
    ps_rr = [0]
    ps_lim = [0, 6]

    def psum(nb=1):
        lo, hi = ps_lim
        s = ps_rr[0]
        if s < lo or s >= hi:
            s = lo
        if (s - lo) % nb:
            s += nb - (s - lo) % nb
        if s + nb > hi:
            s = lo
        ps_rr[0] = s + nb
        ap = PS[:, s:s + nb, :].rearrange("p a b -> p (a b)") if nb > 1 else PS[:, s, :]
        return ap, b_ps[s:s + nb]

    pin_rr = [0]

    def psum_pin():
        s = 6 + pin_rr[0] % 2
        pin_rr[0] += 1
        return PS[:, s, :], b_ps[s:s + 1]

    def dma(q, out, in_, reads=(), writes=()):
        return R.op(q, lambda e: e.dma_start(out=out, in_=in_), reads, writes, dma=True)

    def mm(out, lhsT, rhs, start, stop, reads, writes):
        return R.op("pe", lambda e: e.matmul(out, lhsT, rhs, start=start, stop=stop), reads, writes)

    def tr(out, in_, ident, reads, writes):
        return R.op("pe", lambda e: e.transpose(out, in_, ident), reads, writes)

    def act(out, in_, func, reads, writes, bias=None, scale=None, eng="act", accum=None):
        kw = {}
        if bias is not None:
            kw["bias"] = bias
        if scale is not None:
            kw["scale"] = scale
        if accum is not None:
            kw["accum_out"] = accum
        return R.op(eng, lambda e: e.activation(out, in_, func, **kw), reads, writes)

    def tt(eng, out, in0, in1, op, reads, writes):
        return R.op(eng, lambda e: e.tensor_tensor(out, in0, in1, op), reads, writes)

    def ts(eng, out, in0, s1, s2, op0, op1, reads, writes):
        return R.op(eng, lambda e: e.tensor_scalar(out, in0, s1, s2, op0, op1), reads, writes)

    def stt(out, in0, scalar, in1, op0, op1, reads, writes):
        return R.op("dve", lambda e: e.scalar_tensor_tensor(out, in0, scalar, in1, op0, op1), reads, writes)

    def cp(eng, out, in_, reads, writes):
        if eng == "act":
            return R.op("act", lambda e: e.copy(out, in_), reads, writes)
        return R.op(eng, lambda e: e.tensor_copy(out, in_), reads, writes)

    def memset(eng, ap, val, writes):
        return R.op(eng, lambda e: e.memset(ap, val), (), writes)

    def rsqrt_inplace(ap, bufs, scale, eps):
        ts("dve", ap, ap, scale, eps, ALU.mult, ALU.add, bufs, bufs)
        act(ap, ap, AF.Ln, bufs, bufs)
        act(ap, ap, AF.Exp, bufs, bufs, scale=-0.5)

    dma("sp", ident_f, cst["ident"], (), [b_const])
    cp("dve", ident_b, ident_f, [b_const], [b_const])
    memset("dve", ones_f, 1.0, [b_const])
    memset("dve", ones_b, 1.0, [b_const])
    memset("pool", uT[:, :, 0:SHIFT], 0.0, [b_u])
    for gi, (gs, gn) in enumerate(groups):
        dma("sp", hT[:, :, gs:gs + gn], xT.rearrange("(a p) t -> p a t", p=128)[:, :, gs:gs + gn], (), [b_h[gi]])

    def load_cast(dst_bf, src_f32, bufs_w, reads=()):
        return dma("pool", dst_bf, src_f32, reads, bufs_w)

    def phase_norm(l, which, gsel=None, dst=None, dst_b=None, colp_t=None, b_colp=None):
        m = A.mark()
        sq = A.alloc([128, 512], F32)
        rs = A.alloc([128, 512], F32)
        b_sq, b_rs = Buf("sq"), Buf("rs")
        for gi, (gs, gn) in enumerate(groups):
            if gsel is not None and gi not in gsel:
                continue
            pa, pb = psum(1)
            for kc in range(8):
                act(sq[:, :gn], hT[:, kc, gs:gs + gn], AF.Square, [b_h[gi]], [b_sq])
                mm(pa[:, :gn], ones_f, sq[:, :gn], kc == 0, kc == 7, [b_sq, b_const], pb)
            ts("dve", rs[:, :gn], pa[:, :gn], 1.0 / D, EPS, ALU.mult, ALU.add, pb, [b_rs])
            act(rs[:, :gn], rs[:, :gn], AF.Ln, [b_rs], [b_rs])
            act(rs[:, :gn], rs[:, :gn], AF.Exp, [b_rs], [b_rs], scale=-0.5)
            for kc in range(8):
                o = (dst[:, kc, 0:gn] if dst is not None else uT[:, kc, SHIFT + gs:SHIFT + gs + gn])
                stt(o, hT[:, kc, gs:gs + gn], colp_t[:, which * 8 + kc:which * 8 + kc + 1], rs[:, :gn],
                    ALU.mult, ALU.mult, [b_h[gi], b_rs, b_colp], [dst_b if dst is not None else b_u])
        A.release(m)

    CP_NMIX, CP_NFFN, CP_GB, CP_BCB, CP_FW, CP_FB = 0, 1, 16, 40, 48, 48 + 3 * 44

    def phase_ffn(l, colp_t, b_colp):
        m = A.mark()
        u2 = A.alloc([128, 8, 512], BF16)
        b_u2 = Buf("u2")
        halo = A.alloc([128, 44, 2], F32)
        b_halo = Buf("halo")
        actT = A.alloc([128, NFC, 512], BF16)
        b_actT = Buf("actT")
        wup = [A.alloc([128, 8, 256], BF16) for _ in range(2)]
        b_wup = [Buf("wup0"), Buf("wup1")]
        wdn = [A.alloc([128, NFC, 128], BF16) for _ in range(2)]
        b_wdn = [Buf("wdn0"), Buf("wdn1")]
        X = [A.alloc([128, 514], F32) for _ in range(4)]
        b_X = [Buf("X%d" % i) for i in range(4)]
        cc = [A.alloc([128, 512], F32) for _ in range(2)]
        b_cc = [Buf("cc0"), Buf("cc1")]
        sg = A.alloc([128, 512], F32)
        b_sg = Buf("sg")
        memset("dve", halo, 0.0, [b_halo])
        wu = w_up[l].rearrange("(a p) f -> p a f", p=128)
        wd = w_down[l].rearrange("(a p) d -> p a d", p=128)
        for gi, (gs, gn) in enumerate(groups):
            phase_norm(l, 1, gsel=[gi], dst=u2, dst_b=b_u2, colp_t=colp_t, b_colp=b_colp)
            for fc in range(NFC):
                wbf = wup[fc % 2]
                load_cast(wbf[:, :, 0:128], wu[:, :, fc * 128:(fc + 1) * 128], [b_wup[fc % 2]])
                load_cast(wbf[:, :, 128:256], wu[:, :, FH + fc * 128:FH + (fc + 1) * 128], [b_wup[fc % 2]])
                for half in range(2):
                    ci = half * NFC + fc
                    pa, pb = psum(1)
                    for kc in range(8):
                        mm(pa[:, :gn], wbf[:, kc, half * 128:(half + 1) * 128], u2[:, kc, :gn], kc == 0, kc == 7,
                           [b_wup[fc % 2], b_u2], pb)
                    xi = (fc % 2) * 2 + half
                    Xh = X[xi]
                    cp("act", Xh[:, 2:2 + gn], pa[:, :gn], pb, [b_X[xi]])
                    cp("act", Xh[:, 0:2], halo[:, ci, :], [b_halo], [b_X[xi]])
                    c = cc[half]
                    w = lambda k: colp_t[:, CP_FW + k * 44 + ci:CP_FW + k * 44 + ci + 1]
                    bcol = colp_t[:, CP_FB + ci:CP_FB + ci + 1]
                    ts("dve", c[:, :gn], Xh[:, 2:2 + gn], w(2), bcol, ALU.mult, ALU.add, [b_X[xi], b_colp], [b_cc[half]])
                    stt(c[:, :gn], Xh[:, 1:1 + gn], w(1), c[:, :gn], ALU.mult, ALU.add, [b_X[xi], b_colp, b_cc[half]], [b_cc[half]])
                    stt(c[:, :gn], Xh[:, 0:gn], w(0), c[:, :gn], ALU.mult, ALU.add, [b_X[xi], b_colp, b_cc[half]], [b_cc[half]])
                    cp("act", halo[:, ci, :], Xh[:, gn:gn + 2], [b_X[xi]], [b_halo])
                act(sg[:, :gn], cc[0][:, :gn], AF.Silu, [b_cc[0]], [b_sg])
                tt("dve", actT[:, fc, :gn], sg[:, :gn], cc[1][:, :gn], ALU.mult, [b_sg, b_cc[1]], [b_actT])
            for dc in range(8):
                load_cast(wdn[dc % 2], wd[:, :, dc * 128:(dc + 1) * 128], [b_wdn[dc % 2]])
                pa, pb = psum(1)
                for fc in range(NFC):
                    mm(pa[:, :gn], wdn[dc % 2][:, fc, :], actT[:, fc, :gn], fc == 0, fc == NFC - 1,
                       [b_wdn[dc % 2], b_actT], pb)
                tt("dve", hT[:, dc, gs:gs + gn], hT[:, dc, gs:gs + gn], pa[:, :gn], ALU.add, [b_h[gi]] + pb, [b_h[gi]])
        A.release(m)

    def phase_merge(l, colp_t, b_colp):
        m = A.mark()
        yT = A.alloc([128, 32, 512], BF16)
        b_yT = Buf("yT")
        wbr = [A.alloc([128, 32, 128], BF16) for _ in range(2)]
        b_wbr = [Buf("wbr0"), Buf("wbr1")]
        wg = [A.alloc([128, 8, 384], BF16) for _ in range(2)]
        b_wg = [Buf("wg0"), Buf("wg1")]
        wo = A.alloc([128, 8, 1024], BF16)
        b_wo = Buf("wo")
        mT = A.alloc([128, 8, 512], BF16)
        b_mT = Buf("mT")
        sig = A.alloc([128, 512], F32)
        b_sig = Buf("sig")
        tmp = A.alloc([128, 512], F32)
        b_tmp = Buf("tmp")
        acc = A.alloc([128, 512], F32)
        b_acc = Buf("acc")
        load_cast(wo, w_out[l].rearrange("(a p) d -> p a d", p=128), [b_wo])
        wbd = w_branch[l].rearrange("(a p) d -> p a d", p=128)
        wi = w_in[l].rearrange("(a p) f -> p a f", p=128)
        syv = s_y.rearrange("(a p) t -> p a t", p=128)
        for gi, (gs, gn) in enumerate(groups):
            ybufs = [sy_buf(kc, gi) for kc in range(32)]
            dma("sp", yT[:, :, :gn], syv[:, :, gs:gs + gn], ybufs, [b_yT])
            for dc in range(8):
                load_cast(wbr[dc % 2], wbd[:, :, dc * 128:(dc + 1) * 128], [b_wbr[dc % 2]])
                for br in range(3):
                    c0 = OFF_G + br * 1024 + dc * 128
                    load_cast(wg[dc % 2][:, :, br * 128:(br + 1) * 128], wi[:, :, c0:c0 + 128], [b_wg[dc % 2]])
                for br, (k0, k1) in enumerate([(0, 8), (8, 24), (24, 32)]):
                    pa, pb = psum(1)
                    for kc in range(k0, k1):
                        mm(pa[:, :gn], wbr[dc % 2][:, kc, :], yT[:, kc, :gn], kc == k0, kc == k1 - 1,
                           [b_wbr[dc % 2], b_yT], pb)
                    pg, pgb = psum(1)
                    for kc in range(8):
                        mm(pg[:, :gn], wg[dc % 2][:, kc, br * 128:(br + 1) * 128],
                           uT[:, kc, SHIFT + gs:SHIFT + gs + gn], kc == 0, kc == 7, [b_wg[dc % 2], b_u], pgb)
                    gb = colp_t[:, CP_GB + br * 8 + dc:CP_GB + br * 8 + dc + 1]
                    act(sig[:, :gn], pg[:, :gn], AF.Sigmoid, pgb + [b_colp], [b_sig], bias=gb)
                    if br == 0:
                        tt("dve", acc[:, :gn], sig[:, :gn], pa[:, :gn], ALU.mult, [b_sig] + pb, [b_acc])
                    else:
                        tt("dve", tmp[:, :gn], sig[:, :gn], pa[:, :gn], ALU.mult, [b_sig] + pb, [b_tmp])
                        if br == 1:
                            tt("dve", acc[:, :gn], acc[:, :gn], tmp[:, :gn], ALU.add, [b_acc, b_tmp], [b_acc])
                        else:
                            tt("dve", mT[:, dc, :gn], acc[:, :gn], tmp[:, :gn], ALU.add, [b_acc, b_tmp], [b_mT])
            for dc2 in range(8):
                pa, pb = psum(1)
                for dc in range(8):
                    mm(pa[:, :gn], wo[:, dc, dc2 * 128:(dc2 + 1) * 128], mT[:, dc, :gn], dc == 0, dc == 7,
                       [b_wo, b_mT], pb)
                tt("dve", hT[:, dc2, gs:gs + gn], hT[:, dc2, gs:gs + gn], pa[:, :gn], ALU.add, [b_h[gi]] + pb, [b_h[gi]])
        A.release(m)


    sy_bufs = {}

    def sy_buf(kc, gi):
        if (kc, gi) not in sy_bufs:
            sy_bufs[(kc, gi)] = Buf("sy%d_%d" % (kc, gi))
        return sy_bufs[(kc, gi)]

    def phase_ret(l, colp_t, b_colp):
        m = A.mark()
        wq = A.alloc([128, 8, 512], BF16)
        wv = A.alloc([128, 8, 256], BF16)
        wgt = A.alloc([128, 8, 256], BF16)
        b_wq, b_wv, b_wgt = Buf("wq"), Buf("wv"), Buf("wgt")
        qT = A.alloc([128, TP], BF16)
        kT = A.alloc([128, TP], BF16)
        b_qT, b_kT = Buf("qT"), Buf("kT")
        vtok = A.alloc([128, NT, 256], BF16)
        b_vtok = Buf("vtok")
        gT = A.alloc([128, 2, TP], BF16)
        b_gT = Buf("gT")
        tabs = [A.alloc([128, 512], F32) for _ in range(4)]
        b_tabs = [Buf("tab%d" % i) for i in range(4)]
        dect = A.alloc([128, TP], F32)
        b_dect = Buf("dect")
        Pm = [A.alloc([128, 512], BF16) for _ in range(2)]
        b_Pm = [Buf("P0"), Buf("P1")]
        t1 = A.alloc([128, 512], F32)
        t2 = A.alloc([128, 512], F32)
        b_t1, b_t2 = Buf("t1"), Buf("t2")
        ysb = A.alloc([128, 512], F32)
        sq = A.alloc([128, 512], F32)
        msb = A.alloc([128, 512], F32)
        var = A.alloc([128, 512], F32)
        b_ysb, b_sq, b_msb, b_var = Buf("ysb"), Buf("sq"), Buf("msb"), Buf("var")
        yo = A.alloc([128, 512], BF16)
        b_yo = Buf("yo")
        wi = w_in[l].rearrange("(a p) f -> p a f", p=128)
        wqk = w_qk[l].rearrange("(a p) f -> p a f", p=128)
        wrt = w_rot[l].rearrange("(a p) f -> p a f", p=128)
        tabn = ["cosq", "sinq", "cosk", "sink"]
        pcount = 0
        for hp in range(4):
            load_cast(wq[:, :, 0:128], wqk[:, :, hp * 128:(hp + 1) * 128], [b_wq])
            load_cast(wq[:, :, 128:256], wrt[:, :, hp * 128:(hp + 1) * 128], [b_wq])
            load_cast(wq[:, :, 256:384], wqk[:, :, 512 + hp * 128:512 + (hp + 1) * 128], [b_wq])
            load_cast(wq[:, :, 384:512], wrt[:, :, 512 + hp * 128:512 + (hp + 1) * 128], [b_wq])
            load_cast(wv, wi[:, :, OFF_C + 1024 + hp * 256:OFF_C + 1024 + (hp + 1) * 256], [b_wv])
            load_cast(wgt, wi[:, :, OFF_C + 2048 + hp * 256:OFF_C + 2048 + (hp + 1) * 256], [b_wgt])
            for gi, (gs, gn) in enumerate(groups):
                for ti in range(4):
                    dma("sp", tabs[ti][:, :gn], cst[tabn[ti]][:, gs:gs + gn], (), [b_tabs[ti]])
                for qi in range(2):
                    pq, pqb = psum(1)
                    pr, prb = psum(1)
                    for kc in range(8):
                        mm(pq[:, :gn], wq[:, kc, qi * 256:qi * 256 + 128], uT[:, kc, SHIFT + gs:SHIFT + gs + gn],
                           kc == 0, kc == 7, [b_wq, b_u], pqb)
                    for kc in range(8):
                        mm(pr[:, :gn], wq[:, kc, qi * 256 + 128:qi * 256 + 256], uT[:, kc, SHIFT + gs:SHIFT + gs + gn],
                           kc == 0, kc == 7, [b_wq, b_u], prb)
                    tt("dve", t1[:, :gn], pq[:, :gn], tabs[qi * 2][:, :gn], ALU.mult, pqb + [b_tabs[qi * 2]], [b_t1])
                    tt("dve", t2[:, :gn], pr[:, :gn], tabs[qi * 2 + 1][:, :gn], ALU.mult, prb + [b_tabs[qi * 2 + 1]], [b_t2])
                    dst, dstb = (qT, b_qT) if qi == 0 else (kT, b_kT)
                    tt("pool", dst[:, gs:gs + gn], t1[:, :gn], t2[:, :gn], ALU.add, [b_t1, b_t2], [dstb])
                for hh in range(2):
                    pg, pgb = psum(1)
                    for kc in range(8):
                        mm(pg[:, :gn], wgt[:, kc, hh * 128:(hh + 1) * 128], uT[:, kc, SHIFT + gs:SHIFT + gs + gn],
                           kc == 0, kc == 7, [b_wgt, b_u], pgb)
                    act(gT[:, hh, gs:gs + gn], pg[:, :gn], AF.Silu, pgb, [b_gT])
            for i in range(NT):
                pv, pvb = psum(1)
                for kc in range(8):
                    mm(pv[:, :256], uT[:, kc, SHIFT + i * 128:SHIFT + (i + 1) * 128], wv[:, kc, :], kc == 0, kc == 7,
                       [b_wv, b_u], pvb)
                cp("act", vtok[:, i, :], pv[:, :256], pvb, [b_vtok])
            for hh in range(2):
                h = hp * 2 + hh
                base = hh * 64
                dma("sp", dect, cst["rdec"][h], (), [b_dect])
                for gi, (gs, gn) in enumerate(groups):
                    yps, ypb = psum_pin()
                    nst = (gs + gn) // 128
                    for si in range(nst):
                        s0 = si * 128
                        ls = max(gs, s0)
                        n = gs + gn - ls
                        sc, scb = psum(1)
                        mm(sc[:, :n], kT[base:base + 64, s0:s0 + 128], qT[base:base + 64, ls:ls + n], True, True,
                           [b_kT, b_qT], scb)
                        P = Pm[pcount % 2]
                        bP = b_Pm[pcount % 2]
                        pcount += 1
                        tt("dve", P[:, :n], sc[:, :n], dect[:, ls - s0:ls - s0 + n], ALU.mult, scb + [b_dect], [bP])
                        mm(yps[:, ls - gs:ls - gs + n], vtok[:, si, hh * 128:(hh + 1) * 128], P[:, :n], si == 0,
                           si == nst - 1, [b_vtok, bP], ypb)
                    cp("act", ysb[:, :gn], yps[:, :gn], ypb, [b_ysb])
                    act(sq[:, :gn], ysb[:, :gn], AF.Square, [b_ysb], [b_sq])
                    pm, pmb = psum(1)
                    mm(pm[:, :gn], ones_f, ysb[:, :gn], True, True, [b_ysb, b_const], pmb)
                    pv2, pv2b = psum(1)
                    mm(pv2[:, :gn], ones_f, sq[:, :gn], True, True, [b_sq, b_const], pv2b)
                    ts("dve", msb[:, :gn], pm[:, :gn], 1.0 / 128, None, ALU.mult, ALU.bypass, pmb, [b_msb])
                    tt("pool", sq[:, :gn], msb[:, :gn], msb[:, :gn], ALU.mult, [b_msb], [b_sq])
                    stt(var[:, :gn], pv2[:, :gn], 1.0 / 128, sq[:, :gn], ALU.mult, ALU.subtract, pv2b + [b_sq], [b_var])
                    ts("dve", var[:, :gn], var[:, :gn], 1.0, EPS, ALU.mult, ALU.add, [b_var], [b_var])
                    act(var[:, :gn], var[:, :gn], AF.Ln, [b_var], [b_var])
                    act(var[:, :gn], var[:, :gn], AF.Exp, [b_var], [b_var], scale=-0.5)
                    tt("pool", ysb[:, :gn], ysb[:, :gn], msb[:, :gn], ALU.subtract, [b_ysb, b_msb], [b_ysb])
                    tt("dve", ysb[:, :gn], ysb[:, :gn], var[:, :gn], ALU.mult, [b_ysb, b_var], [b_ysb])
                    tt("dve", yo[:, :gn], ysb[:, :gn], gT[:, hh, gs:gs + gn], ALU.mult, [b_ysb, b_gT], [b_yo])
                    dma("act", s_y[(24 + h) * 128:(25 + h) * 128, gs:gs + gn], yo[:, :gn], [b_yo], [sy_buf(24 + h, gi)])
        A.release(m)


    def phase_ssm(l, colp_t, b_colp):
        m0 = A.mark()
        BT = A.alloc([128, 4, TP], BF16)
        CT = A.alloc([128, 4, TP], BF16)
        b_BT, b_CT = Buf("BT"), Buf("CT")
        brow_t = A.alloc([1, 3072 + 32], F32)
        b_brow = Buf("brow")
        dma("sp", brow_t, brow[l], (), [b_brow])
        wi = w_in[l].rearrange("(a p) f -> p a f", p=128)
        smx_b = [Buf("smx%d" % i) for i in range(NT)]
        m1 = A.mark()
        CBW = 256
        stage = A.alloc([128, 8, CBW], F32)
        b_stage = Buf("stage")
        cwt = A.alloc([128, 4, CBW], F32)
        b_cwt = Buf("cwt")
        wt = A.alloc([128, 4, 8, CBW], BF16)
        b_wt = Buf("wt")
        ev = [A.alloc([128, 512], F32) for _ in range(2)]
        b_ev = [Buf("ev0"), Buf("ev1")]
        evc = 0
        wz = wt[:, 0, :, :]
        for cb in range(2048 // CBW):
            load_cast(wz, wi[:, :, OFF_B + cb * CBW:OFF_B + (cb + 1) * CBW], [b_wt])
            for i in range(NT):
                pa, pb = psum(1)
                for kc in range(8):
                    mm(pa[:, :CBW], uT[:, kc, SHIFT + i * 128:SHIFT + (i + 1) * 128], wz[:, kc, :], kc == 0, kc == 7,
                       [b_wt, b_u], pb)
                e = ev[evc % 2]
                be = b_ev[evc % 2]
                evc += 1
                act(e[:, :CBW], pa[:, :CBW], AF.Silu, pb, [be])
                dma("act", s_mx[i * 128:(i + 1) * 128, cb * CBW:(cb + 1) * CBW], e[:, :CBW], [be], [smx_b[i]])
        for cb in range(3072 // CBW):
            c0 = cb * CBW
            dma("sp", stage, wi[:, :, OFF_B + 2048 + c0:OFF_B + 2048 + c0 + CBW], (), [b_stage])
            for k in range(4):
                dma("sp", cwt[:, k, :], cw_all[l][:, k * 3072 + c0:k * 3072 + c0 + CBW], (), [b_cwt])
            for k in range(4):
                tt("pool" if k % 2 else "dve", wt[:, k, :, :], stage,
                   cwt[:, k, :].unsqueeze(1).to_broadcast([128, 8, CBW]), ALU.mult, [b_stage, b_cwt], [b_wt])
            if c0 < 2560:
                for i in range(NT):
                    pa, pb = psum(1)
                    for k in range(4):
                        for kc in range(8):
                            mm(pa[:, :CBW], uT[:, kc, i * 128 + k:(i + 1) * 128 + k], wt[:, k, kc, :],
                               k == 0 and kc == 0, False, [b_wt, b_u], pb)
                    mm(pa[:, :CBW], ones_f[0:1, :], brow_t[0:1, c0:c0 + CBW], False, True, [b_const, b_brow], pb)
                    e = ev[evc % 2]
                    be = b_ev[evc % 2]
                    evc += 1
                    act(e[:, :CBW], pa[:, :CBW], AF.Silu, pb, [be])
                    dma("act", s_mx[i * 128:(i + 1) * 128, 2048 + c0:2048 + c0 + CBW], e[:, :CBW], [be], [smx_b[i]])
            if c0 >= 2048:
                for sub in range(CBW // 128):
                    cc0 = c0 + sub * 128 - 2048
                    isC = cc0 >= 512
                    g = (cc0 % 512) // 128
                    dstT, dstb = (CT, b_CT) if isC else (BT, b_BT)
                    for gi, (gs, gn) in enumerate(groups):
                        pa, pb = psum(1)
                        for k in range(4):
                            for kc in range(8):
                                mm(pa[:, :gn], wt[:, k, kc, sub * 128:(sub + 1) * 128], uT[:, kc, gs + k:gs + k + gn],
                                   k == 0 and kc == 0, k == 3 and kc == 7, [b_wt, b_u], pb)
                        bc = colp_t[:, CP_BCB + cc0 // 128:CP_BCB + cc0 // 128 + 1]
                        act(dstT[:, g, gs:gs + gn], pa[:, :gn], AF.Silu, pb + [b_colp], [dstb], bias=bc)
        wdt = wt[:, 0, :, 0:32]
        load_cast(wdt, wi[:, :, OFF_B + 5120:OFF_B + 5152], [b_wt])
        for i in range(NT):
            pa, pb = psum(1)
            for kc in range(8):
                mm(pa[:, :32], uT[:, kc, SHIFT + i * 128:SHIFT + (i + 1) * 128], wdt[:, kc, :], kc == 0, False,
                   [b_wt, b_u], pb)
            mm(pa[:, :32], ones_f[0:1, :], brow_t[0:1, 3072:3104], False, True, [b_const, b_brow], pb)
            e = ev[evc % 2]
            be = b_ev[evc % 2]
            evc += 1
            act(e[:, :32], pa[:, :32], AF.Exp, pb, [be])
            ts("dve", e[:, :32], e[:, :32], 1.0, 1.0, ALU.mult, ALU.add, [be], [be])
            act(e[:, :32], e[:, :32], AF.Ln, [be], [be])
            dma("act", s_mx[i * 128:(i + 1) * 128, 4608:4640], e[:, :32], [be], [smx_b[i]])
        R.barrier()
        A.release(m1)
        rb = A.alloc([128, NROWB], F32)
        b_rb = Buf("rb")
        dma("sp", rb, rowb[l], (), [b_rb])
        ng_bc = rb[:, 0:2048]
        D_bc = rb[:, 2048:2080]
        A_bc = rb[:, 2080:2112]
        act(A_bc, A_bc, AF.Exp, [b_rb], [b_rb])
        ts("dve", A_bc, A_bc, -1.0, None, ALU.mult, ALU.bypass, [b_rb], [b_rb])
        tri = A.alloc([128, 128], F32)
        sfx = A.alloc([128, 128], F32)
        negU = A.alloc([128, 512], F32)
        caus = A.alloc([128, 128], F32)
        b_k = Buf("ssmconst")
        dma("sp", tri, cst["tri128"], (), [b_k])
        dma("sp", sfx, cst["sfx128"], (), [b_k])
        dma("sp", negU, cst["negU"], (), [b_k])
        dma("sp", caus, cst["causal"], (), [b_k])
        S32 = A.alloc([128, 4, 512], F32)
        Sbf = A.alloc([128, 4, 512], BF16)
        b_S32, b_Sbf = Buf("S32"), Buf("Sbf")
        memset("pool", S32, 0.0, [b_S32])
        memset("pool", Sbf, 0.0, [b_Sbf])
        sm = A.alloc([128, 8, 32], F32)
        b_sm = Buf("sm")
        dtt, aa, acs, nacs, eacs, eend, cdb, dte = [sm[:, j, :] for j in range(8)]
        ss = A.alloc([128, 8], F32)
        b_ss = Buf("ss")
        Rt = A.alloc([128, 4, 128], F32)
        b_Rt = Buf("Rt")
        xs = A.alloc([128, 512], F32)
        zs = A.alloc([128, 512], F32)
        Bt = A.alloc([128, 128], F32)
        Btb = A.alloc([128, 128], BF16)
        b_xs, b_zs, b_Bt, b_Btb = Buf("xs"), Buf("zs"), Buf("Bt"), Buf("Btb")
        xdt = A.alloc([128, 8, 64], BF16)
        xde = A.alloc([128, 8, 64], BF16)
        b_xdt, b_xde = Buf("xdt"), Buf("xde")
        CBm = A.alloc([128, 128], F32)
        b_CBm = Buf("CBm")
        E = A.alloc([128, 4, 128], F32)
        MT = A.alloc([128, 4, 128], BF16)
        b_E, b_MT = Buf("E"), Buf("MT")
        yt = A.alloc([128, 8, 64], F32)
        t3 = A.alloc([128, 8, 64], F32)
        junk = A.alloc([128, 512], F32)
        b_yt, b_t3, b_junk = Buf("yt"), Buf("t3"), Buf("junk")
        ybf = A.alloc([128, 2048], BF16)
        b_ybf = Buf("ybf")
        ybT = A.alloc([128, 16, 128], BF16)
        b_ybT = Buf("ybT")
        for i in range(NT):
            tc0 = i * 128
            dma("sp", dtt, s_mx[tc0:tc0 + 128, 4608:4640], [smx_b[i]], [b_sm])
            tt("dve", aa, dtt, A_bc, ALU.mult, [b_sm, b_rb], [b_sm])
            p1, p1b = psum(1)
            mm(p1[:, 0:32], tri, aa, True, True, [b_k, b_sm], p1b)
            mm(p1[:, 32:64], sfx, aa, True, True, [b_k, b_sm], p1b)
            mm(p1[:, 64:96], ones_f, aa, True, True, [b_const, b_sm], p1b)
            cp("dve", acs, p1[:, 0:32], p1b, [b_sm])
            ts("dve", nacs, p1[:, 0:32], -1.0, None, ALU.mult, ALU.bypass, p1b, [b_sm])
            act(eacs, p1[:, 0:32], AF.Exp, p1b, [b_sm])
            act(eend, p1[:, 32:64], AF.Exp, p1b, [b_sm])
            act(cdb, p1[:, 64:96], AF.Exp, p1b, [b_sm])
            tt("dve", dte, dtt, eend, ALU.mult, [b_sm], [b_sm])
            for g in range(4):
                dma("sp", xs, s_mx[tc0:tc0 + 128, 2048 + g * 512:2048 + (g + 1) * 512], [smx_b[i]], [b_xs])
                dma("sp", zs, s_mx[tc0:tc0 + 128, g * 512:(g + 1) * 512], [smx_b[i]], [b_zs])
                dma("sp", Bt, s_mx[tc0:tc0 + 128, 4096 + g * 128:4096 + (g + 1) * 128], [smx_b[i]], [b_Bt])
                xs3 = xs.rearrange("p (r q) -> p r q", q=64)
                tt("dve", xdt, xs3, dtt[:, g * 8:(g + 1) * 8].unsqueeze(2).to_broadcast([128, 8, 64]), ALU.mult,
                   [b_xs, b_sm], [b_xdt])
                tt("pool", xde, xs3, dte[:, g * 8:(g + 1) * 8].unsqueeze(2).to_broadcast([128, 8, 64]), ALU.mult,
                   [b_xs, b_sm], [b_xde])
                cp("pool", Btb, Bt, [b_Bt], [b_Btb])
                pcb, pcbb = psum(1)
                mm(pcb[:, :128], BT[:, g, tc0:tc0 + 128], CT[:, g, tc0:tc0 + 128], True, True, [b_BT, b_CT], pcbb)
                tt("dve", CBm, pcb[:, :128], caus, ALU.mult, pcbb + [b_k], [b_CBm])
                yd, ydb = psum_pin()
                for blk in range(2):
                    r0 = g * 8 + blk * 4
                    tt("pool", Rt, tri.unsqueeze(1).to_broadcast([128, 4, 128]),
                       aa[:, r0:r0 + 4].unsqueeze(2).to_broadcast([128, 4, 128]), ALU.mult, [b_k, b_sm], [b_Rt])
                    pe_, peb = psum(1)
                    mm(pe_[:, :512], ones_f, Rt.rearrange("p a b -> p (a b)"), True, False, [b_const, b_Rt], peb)
                    mm(pe_[:, :512], ident_f, negU, False, True, [b_const, b_k], peb)
                    for r in range(4):
                        act(E[:, r, :], pe_[:, r * 128:(r + 1) * 128], AF.Exp, peb + [b_sm], [b_E],
                            bias=nacs[:, r0 + r:r0 + r + 1])
                    tt("dve", MT, E, CBm.unsqueeze(1).to_broadcast([128, 4, 128]), ALU.mult, [b_E, b_CBm], [b_MT])
                    for r in range(4):
                        hh = blk * 4 + r
                        mm(yd[:, hh * 64:(hh + 1) * 64], MT[:, r, :], xdt[:, hh, :], True, True, [b_MT, b_xdt], ydb)
                po, pob = psum(1)
                mm(po[:, :512], CT[:, g, tc0:tc0 + 128], Sbf[:, g, :], True, True, [b_CT, b_Sbf], pob)
                tt("dve", yt, po[:, :512].rearrange("p (r q) -> p r q", q=64),
                   eacs[:, g * 8:(g + 1) * 8].unsqueeze(2).to_broadcast([128, 8, 64]), ALU.mult, pob + [b_sm], [b_yt])
                yt2 = yt.rearrange("p r q -> p (r q)")
                tt("dve", yt2, yt2, yd[:, :512], ALU.add, [b_yt] + ydb, [b_yt])
                tt("pool", t3, xs3, D_bc[:, g * 8:(g + 1) * 8].unsqueeze(2).to_broadcast([128, 8, 64]), ALU.mult,
                   [b_xs, b_rb], [b_t3])
                tt("pool", yt, yt, t3, ALU.add, [b_yt, b_t3], [b_yt])
                tt("dve", yt2, yt2, zs, ALU.mult, [b_yt, b_zs], [b_yt])
                act(junk, yt2, AF.Square, [b_yt], [b_junk])
                R.op("dve", lambda e, g=g: e.tensor_reduce(ss[:, g:g + 1], junk, AX.X, ALU.add), [b_junk], [b_ss])
                ts("dve", ss[:, g:g + 1], ss[:, g:g + 1], 1.0 / 512, EPS, ALU.mult, ALU.add, [b_ss], [b_ss])
                act(ss[:, g:g + 1], ss[:, g:g + 1], AF.Ln, [b_ss], [b_ss])
                act(ss[:, g:g + 1], ss[:, g:g + 1], AF.Exp, [b_ss], [b_ss], scale=-0.5)
                stt(ybf[:, g * 512:(g + 1) * 512], yt2, ss[:, g:g + 1], ng_bc[:, g * 512:(g + 1) * 512], ALU.mult,
                    ALU.mult, [b_yt, b_ss, b_rb], [b_ybf])
                pst, pstb = psum(1)
                mm(pst[:, :512], Btb, xde.rearrange("p r q -> p (r q)"), True, True, [b_Btb, b_xde], pstb)
                S3 = S32[:, g, :].rearrange("p (r q) -> p r q", q=64)
                tt("dve", S3, S3, cdb[:, g * 8:(g + 1) * 8].unsqueeze(2).to_broadcast([128, 8, 64]), ALU.mult,
                   [b_S32, b_sm], [b_S32])
                tt("dve", S32[:, g, :], S32[:, g, :], pst[:, :512], ALU.add, [b_S32] + pstb, [b_S32])
                cp("act", Sbf[:, g, :], S32[:, g, :], [b_S32], [b_Sbf])
            for half in range(2):
                pt, ptb = psum(1)
                ptv = pt.bitcast(BF16)
                for c in range(8):
                    cc = half * 8 + c
                    tr(ptv[:, c * 128:(c + 1) * 128], ybf[:, cc * 128:(cc + 1) * 128], ident_b, [b_ybf, b_const], ptb)
                cp("act" if half else "dve", ybT[:, half * 8:(half + 1) * 8, :].rearrange("p a b -> p (a b)"),
                   ptv[:, :1024], ptb, [b_ybT])
            gi = [k for k, (gs, gn) in enumerate(groups) if gs <= tc0 < gs + gn][0]
            dma("act", s_y[1024:3072, tc0:tc0 + 128].rearrange("(a p) t -> p a t", p=128), ybT, [b_ybT],
                [sy_buf(kc, gi) for kc in range(8, 24)])
        A.release(m0)


    def phase_rwkv(l, colp_t, b_colp):
        m0 = A.mark()
        wi = w_in[l].rearrange("(a p) f -> p a f", p=128)
        rkv_b = [Buf("rkv%d" % i) for i in range(NT)]
        lo_b = [Buf("lo%d" % g) for g in range(len(groups))]
        CBW = 256
        stage = A.alloc([128, 8, CBW], F32)
        tmpw = A.alloc([128, 8, CBW], F32)
        mut = A.alloc([128, CBW], F32)
        wt = A.alloc([128, 2, 8, CBW], BF16)
        b_stage, b_tmpw, b_mut, b_wt = Buf("stage"), Buf("tmpw"), Buf("mut"), Buf("wt")
        ev = [A.alloc([128, 512], F32) for _ in range(2)]
        b_ev = [Buf("ev0"), Buf("ev1")]
        lo_sb = A.alloc([128, 512], BF16)
        b_lo_sb = Buf("lo_sb")
        evc = 0

        def make_w(src_ap, mu_ap, n):
            dma("sp", stage[:, :, :n], src_ap, (), [b_stage])
            dma("sp", mut[:, :n], mu_ap, (), [b_mut])
            tt("dve", tmpw[:, :, :n], stage[:, :, :n], mut[:, :n].unsqueeze(1).to_broadcast([128, 8, n]), ALU.mult,
               [b_stage, b_mut], [b_tmpw])
            tt("pool", wt[:, 0, :, :n], stage[:, :, :n], tmpw[:, :, :n], ALU.subtract, [b_stage, b_tmpw], [b_wt])
            cp("act", wt[:, 1, :, :n], tmpw[:, :, :n], [b_tmpw], [b_wt])

        for cb in range(3072 // CBW):
            c0 = cb * CBW
            make_w(wi[:, :, c0:c0 + CBW], mu_all[l][:, c0:c0 + CBW], CBW)
            for i in range(NT):
                pa, pb = psum(1)
                for kc in range(8):
                    mm(pa[:, :CBW], uT[:, kc, SHIFT + i * 128:SHIFT + (i + 1) * 128], wt[:, 0, kc, :], kc == 0, False,
                       [b_wt, b_u], pb)
                for kc in range(8):
                    mm(pa[:, :CBW], uT[:, kc, SHIFT - 1 + i * 128:SHIFT - 1 + (i + 1) * 128], wt[:, 1, kc, :], False,
                       kc == 7, [b_wt, b_u], pb)
                e = ev[evc % 2]
                be = b_ev[evc % 2]
                evc += 1
                cp("act" if evc % 2 else "dve", e[:, :CBW], pa[:, :CBW], pb, [be])
                dma("act", s_rkv[i * 128:(i + 1) * 128, c0:c0 + CBW], e[:, :CBW], [be], [rkv_b[i]])
                if l == 0 and c0 >= 2048:
                    dma("act", s_vf[i * 128:(i + 1) * 128, c0 - 2048:c0 - 2048 + CBW], e[:, :CBW], [be], [b_vf[i]])
        blocks = [("A", wi[:, :, 3072:3200], mu_all[l][:, 3072:3200], 128),
                  ("G", wi[:, :, 3200:3328], mu_all[l][:, 3200:3328], 128)]
        if l > 0:
            blocks.append(("V", w_vres[l - 1].rearrange("(a p) f -> p a f", p=128), mu_all[l][:, 3328:3360], 32))
        for bi, (nm, wsrc, musrc, n) in enumerate(blocks):
            make_w(wsrc, musrc, n)
            for gi, (gs, gn) in enumerate(groups):
                pa, pb = psum(1)
                for kc in range(8):
                    mm(pa[:n, :gn], wt[:, 0, kc, :n], uT[:, kc, SHIFT + gs:SHIFT + gs + gn], kc == 0, False,
                       [b_wt, b_u], pb)
                for kc in range(8):
                    mm(pa[:n, :gn], wt[:, 1, kc, :n], uT[:, kc, SHIFT - 1 + gs:SHIFT - 1 + gs + gn], False, kc == 7,
                       [b_wt, b_u], pb)
                if nm == "A":
                    act(lo_sb[0:64, :gn], pa[0:64, :gn], AF.Tanh, pb, [b_lo_sb])
                    cp("act", lo_sb[64:128, :gn], pa[64:128, :gn], pb, [b_lo_sb])
                elif nm == "G":
                    act(lo_sb[:, :gn], pa[:, :gn], AF.Sigmoid, pb, [b_lo_sb])
                else:
                    cp("act", lo_sb[0:32, :gn], pa[0:32, :gn], pb, [b_lo_sb])
                dma("act", s_lo[bi * 128:bi * 128 + n, gs:gs + gn], lo_sb[:n, :gn], [b_lo_sb], [lo_b[gi]])
        R.barrier()
        A.release(m0)
        hv = hT.rearrange("p a t -> p (a t)")
        uv = uT.rearrange("p a t -> p (a t)")
        dma("act", s_hT, hv, b_h, [Buf("s_hT")])
        dma("act", s_uT, uv, [b_u], [Buf("s_uT")])
        R.barrier()
        A2 = A.sub(0, hu_bytes) if hu_bytes >= 100000 else A
        g2t = A2.alloc([128, 1024], BF16)
        b_w2 = Buf("w2")
        wa2z = A2.alloc([128, 2, 1024], BF16)
        b_wa2z = Buf("wa2z")
        load_cast(g2t, rg2[l], [b_w2])
        memset("pool", wa2z, 0.0, [b_wa2z])
        load_cast(wa2z[0:64, 0, :], rw2[l][0:64, :], [b_wa2z])
        load_cast(wa2z[64:128, 1, :], rw2[l][64:128, :], [b_wa2z])
        v2z = A2.alloc([128, 1024], BF16)
        b_v2z = Buf("v2z")
        memset("pool", v2z, 0.0, [b_v2z])
        if l > 0:
            load_cast(v2z[0:32, :], rv2[l - 1], [b_v2z])
        kst = A2.alloc([128, 128 * 3 + 256], F32)
        b_kst = Buf("kst")
        tri, sfxm, nmask2 = kst[:, 0:128], kst[:, 128:256], kst[:, 256:384]
        mmask2 = kst[:, 384:640]
        dma("sp", tri, cst["tri64"], (), [b_kst])
        dma("sp", sfxm, cst["sfx64"], (), [b_kst])
        dma("sp", nmask2, cst["nmask2"], (), [b_kst])
        dma("sp", mmask2, cst["mmask2"], (), [b_kst])

        def r2_stream(hf, AA):
            ra = AA.alloc([128, 8, 512], F32)
            b_ra = Buf("ra")
            names = ["r", "k", "v", "vf", "s", "a", "kk", "kp", "b", "g", "e1", "e2", "e3", "e4", "t1", "t2"]
            alias = {"e4": "e1", "vf": "e2", "e3": "a"}
            T = {n: AA.alloc([128, 512], F32) for n in names if n not in alias}
            Bf = {n: Buf("T_" + n) for n in names if n not in alias}
            for n_, o_ in alias.items():
                T[n_] = T[o_]
                Bf[n_] = Bf[o_]
            bnames = ["at", "rt", "bt", "kt", "Vt", "Bz0", "Bz1", "Kz0", "Kz1"]
            TB = {n: AA.alloc([128, 512], BF16) for n in bnames}
            BB = {n: Buf("TB_" + n) for n in bnames}
            ARTz = [AA.alloc([128, 4, 2, 128], BF16) for _ in range(2)]
            b_ARTz = Buf("ARTz")
            BKT = AA.alloc([128, 4, 2, 128], BF16)
            b_BKT = Buf("BKT")
            Mb = AA.alloc([128, 8, 2, 128], BF16)
            Mk = AA.alloc([128, 8, 2, 128], BF16)
            b_Mb, b_Mk = Buf("Mb"), Buf("Mk")
            Nn = [AA.alloc([128, 8, 128], BF16) for _ in range(2)]
            NTt = [AA.alloc([128, 8, 128], BF16) for _ in range(2)]
            Pp = AA.alloc([128, 8, 128], BF16)
            b_Nn = [Buf("N0"), Buf("N1")]
            b_NTt = [Buf("NT0"), Buf("NT1")]
            b_Pp = Buf("Pp")
            Wsb = AA.alloc([128, 8, 64], BF16)
            Usb = AA.alloc([128, 8, 64], BF16)
            b_Wsb, b_Usb = Buf("Wsb"), Buf("Usb")
            S32 = AA.alloc([128, 4, 64], F32)
            Sbf = AA.alloc([128, 4, 64], BF16)
            b_S32, b_Sbf = Buf("S32"), Buf("Sbf")
            Ysb = AA.alloc([128, 8, 64], F32)
            b_Ysb = Buf("Ysb")
            ya = AA.alloc([128, 512], BF16)
            yaT = AA.alloc([128, 4, 128], BF16)
            b_ya, b_yaT = Buf("ya"), Buf("yaT")
            lo_t = AA.alloc([128, 3, 128], BF16)
            b_lo_t = Buf("lo_t")
            st = AA.alloc([128, 8, 8], F32)
            b_st = Buf("st")
            ssq, bon, s1, s2, mean, varr = [st[:, j, :] for j in range(6)]
            gC = AA.alloc([128, 8], F32)
            b_gC = Buf("gC")
            memset("pool", ARTz[0], 0.0, [b_ARTz])
            memset("pool", ARTz[1], 0.0, [b_ARTz])
            memset("pool", Wsb, 0.0, [b_Wsb])
            memset("pool", Usb, 0.0, [b_Usb])
            memset("pool", lo_t, 0.0, [b_lo_t])
            slo = s_lo.rearrange("(a p) t -> p a t", p=128)
            rav = rowa[l].rearrange("p (j f) -> p j f", f=1024)
            ind = [tri[:, 63:64], tri[:, 127:128]]

            def bc8(ap8):
                return ap8.unsqueeze(2).to_broadcast([128, 8, 64])

            def v3(ap):
                return ap.rearrange("p (h q) -> p h q", q=64)

            f0 = hf * 512
            dma("sp", ra, rav[:, 0:8, f0:f0 + 512], (), [b_ra])
            w0b, a0b, kkb, kab, rkb, lgb, lbb, v0b = [ra[:, j, :] for j in range(8)]
            memset("pool", S32, 0.0, [b_S32])
            memset("pool", Sbf, 0.0, [b_Sbf])
            for i in range(NT):
                tc0 = i * 128
                gi = [k for k, (gs, gn) in enumerate(groups) if gs <= tc0 < gs + gn][0]
                dma("sp", T["r"], s_rkv[tc0:tc0 + 128, f0:f0 + 512], [rkv_b[i]], [Bf["r"]])
                dma("sp", T["k"], s_rkv[tc0:tc0 + 128, 1024 + f0:1024 + f0 + 512], [rkv_b[i]], [Bf["k"]])
                dma("sp", T["v"], s_rkv[tc0:tc0 + 128, 2048 + f0:2048 + f0 + 512], [rkv_b[i]], [Bf["v"]])
                nlo = 3 if l > 0 else 2
                for j in range(nlo):
                    npart = 128 if j < 2 else 32
                    dma("sp", lo_t[0:npart, j, :], slo[0:npart, j, tc0:tc0 + 128], [lo_b[gi]], [b_lo_t])
                pw, pwb = psum(1)
                mm(pw[:, :512], lo_t[:, 0, :], wa2z[:, 0, f0:f0 + 512], True, True, [b_lo_t, b_wa2z], pwb)
                pa_, pab = psum(1)
                mm(pa_[:, :512], lo_t[:, 0, :], wa2z[:, 1, f0:f0 + 512], True, True, [b_lo_t, b_wa2z], pab)
                pg, pgb = psum(1)
                mm(pg[:, :512], lo_t[:, 1, :], g2t[:, f0:f0 + 512], True, True, [b_lo_t, b_w2], pgb)
                tt("dve", T["t1"], pw[:, :512], w0b, ALU.add, pwb + [b_ra], [Bf["t1"]])
                act(T["s"], T["t1"], AF.Sigmoid, [Bf["t1"]], [Bf["s"]])
                tt("dve", T["t2"], pa_[:, :512], a0b, ALU.add, pab + [b_ra], [Bf["t2"]])
                act(T["a"], T["t2"], AF.Sigmoid, [Bf["t2"]], [Bf["a"]])
                cp("act", T["g"], pg[:, :512], pgb, [Bf["g"]])
                if l > 0:
                    dma("sp", T["vf"], s_vf[tc0:tc0 + 128, f0:f0 + 512], [b_vf[i]], [Bf["vf"]])
                    pvr, pvrb = psum(1)
                    mm(pvr[:, :512], lo_t[:, 2, :], v2z[:, f0:f0 + 512], True, True, [b_lo_t, b_v2z], pvrb)
                    tt("dve", T["t1"], pvr[:, :512], v0b, ALU.add, pvrb + [b_ra], [Bf["t1"]])
                    act(T["t1"], T["t1"], AF.Sigmoid, [Bf["t1"]], [Bf["t1"]])
                    tt("pool", T["t2"], T["vf"], T["v"], ALU.subtract, [Bf["vf"], Bf["v"]], [Bf["t2"]])
                    tt("dve", T["t2"], T["t2"], T["t1"], ALU.mult, [Bf["t2"], Bf["t1"]], [Bf["t2"]])
                    tt("pool", T["v"], T["v"], T["t2"], ALU.add, [Bf["v"], Bf["t2"]], [Bf["v"]])
                tt("pool", T["kk"], T["k"], kkb, ALU.mult, [Bf["k"], b_ra], [Bf["kk"]])
                act(T["t1"], T["kk"], AF.Square, [Bf["kk"]], [Bf["t1"]])
                R.op("dve", lambda e: e.tensor_reduce(ssq, v3(T["t1"]), AX.X, ALU.add), [Bf["t1"]], [b_st])
                ts("dve", ssq, ssq, 1e-24, None, ALU.max, ALU.bypass, [b_st], [b_st])
                act(ssq, ssq, AF.Ln, [b_st], [b_st])
                act(ssq, ssq, AF.Exp, [b_st], [b_st], scale=-0.5)
                tt("dve", v3(T["kk"]), v3(T["kk"]), bc8(ssq), ALU.mult, [Bf["kk"], b_st], [Bf["kk"]])
                stt(T["t1"], T["a"], -1.0, kab, ALU.add, ALU.mult, [Bf["a"], b_ra], [Bf["t1"]])
                stt(T["kp"], T["t1"], 1.0, T["k"], ALU.add, ALU.mult, [Bf["t1"], Bf["k"]], [Bf["kp"]])
                tt("pool", T["b"], T["kk"], T["a"], ALU.mult, [Bf["kk"], Bf["a"]], [Bf["b"]])
                tt("pool", T["t2"], T["r"], rkb, ALU.mult, [Bf["r"], b_ra], [Bf["t2"]])
                tt("dve", T["t2"], T["t2"], T["kp"], ALU.mult, [Bf["t2"], Bf["kp"]], [Bf["t2"]])
                R.op("dve", lambda e: e.tensor_reduce(bon, v3(T["t2"]), AX.X, ALU.add), [Bf["t2"]], [b_st])
                pcs, pcsb = psum(1)
                mm(pcs[:, :512], tri, T["s"], True, True, [b_kst, Bf["s"]], pcsb)
                psf, psfb = psum(1)
                mm(psf[:, :512], sfxm, T["s"], True, True, [b_kst, Bf["s"]], psfb)
                pgc, pgcb = psum(1)
                for pr in range(4):
                    mm(pgc[:, pr * 2:pr * 2 + 2], T["s"][:, pr * 128:(pr + 1) * 128], tri[:, 63:128:64], True, True,
                       [Bf["s"], b_kst], pgcb)
                act(gC, pgc[:, 0:8], AF.Exp, pgcb, [b_gC], scale=-WDEC)
                act(T["e1"], pcs[:, :512], AF.Exp, pcsb, [Bf["e1"]], scale=-WDEC)
                tt("pool", TB["rt"], T["r"], T["e1"], ALU.mult, [Bf["r"], Bf["e1"]], [BB["rt"]])
                act(T["e2"], pcs[:, :512], AF.Exp, pcsb, [Bf["e2"]], scale=WDEC)
                tt("dve", T["t1"], pcs[:, :512], T["s"], ALU.subtract, pcsb + [Bf["s"]], [Bf["t1"]])
                act(T["e3"], T["t1"], AF.Exp, [Bf["t1"]], [Bf["e3"]], scale=-WDEC)
                act(T["e4"], psf[:, :512], AF.Exp, psfb, [Bf["e4"]], scale=-WDEC)
                stt(TB["at"], T["kk"], -1.0, T["e3"], ALU.mult, ALU.mult, [Bf["kk"], Bf["e3"]], [BB["at"]])
                tt("dve", TB["bt"], T["b"], T["e2"], ALU.mult, [Bf["b"], Bf["e2"]], [BB["bt"]])
                tt("pool", TB["kt"], T["kp"], T["e2"], ALU.mult, [Bf["kp"], Bf["e2"]], [BB["kt"]])
                tt("dve", T["t1"], T["b"], T["e4"], ALU.mult, [Bf["b"], Bf["e4"]], [Bf["t1"]])
                tt("pool", T["t2"], T["kp"], T["e4"], ALU.mult, [Bf["kp"], Bf["e4"]], [Bf["t2"]])
                for c in range(2):
                    ts("dve", TB["Bz%d" % c], T["t1"], ind[c], None, ALU.mult, ALU.bypass, [Bf["t1"], b_kst],
                       [BB["Bz%d" % c]])
                    ts("pool", TB["Kz%d" % c], T["t2"], ind[c], None, ALU.mult, ALU.bypass, [Bf["t2"], b_kst],
                       [BB["Kz%d" % c]])
                cp("act", TB["Vt"], T["v"], [Bf["v"]], [BB["Vt"]])
                pt, ptb = psum(1)
                ptv = pt.bitcast(BF16)
                for pr in range(4):
                    for q, nmq in enumerate(("at", "rt")):
                        tr(ptv[:, (pr * 2 + q) * 128:(pr * 2 + q + 1) * 128], TB[nmq][:, pr * 128:(pr + 1) * 128],
                           ident_b, [BB[nmq], b_const], ptb)
                cp("dve", ARTz[0][0:64].rearrange("p a b c -> p (a b c)"), ptv[0:64, :1024], ptb, [b_ARTz])
                cp("act", ARTz[1][64:128].rearrange("p a b c -> p (a b c)"), ptv[64:128, :1024], ptb, [b_ARTz])
                pt, ptb = psum(1)
                ptv = pt.bitcast(BF16)
                for pr in range(4):
                    for q, nmq in enumerate(("bt", "kt")):
                        tr(ptv[:, (pr * 2 + q) * 128:(pr * 2 + q + 1) * 128], TB[nmq][:, pr * 128:(pr + 1) * 128],
                           ident_b, [BB[nmq], b_const], ptb)
                cp("dve", BKT.rearrange("p a b c -> p (a b c)"), ptv[:, :1024], ptb, [b_BKT])
                mk4 = mmask2.rearrange("p (q t) -> p q t", t=128).unsqueeze(1).to_broadcast([128, 4, 2, 128])
                for hg in range(2):
                    pmb, pmbb = psum(2)
                    pmk, pmkb = psum(2)
                    for hl in range(4):
                        h = hg * 4 + hl
                        pr, hh = h // 2, h % 2
                        rhs = ARTz[hh][:, pr, :, :].rearrange("p q t -> p (q t)")
                        mm(pmb[:, hl * 256:(hl + 1) * 256], BKT[:, pr, 0, :], rhs, True, True, [b_BKT, b_ARTz], pmbb)
                        mm(pmk[:, hl * 256:(hl + 1) * 256], BKT[:, pr, 1, :], rhs, True, True, [b_BKT, b_ARTz], pmkb)
                    tt("dve", Mb[:, hg * 4:(hg + 1) * 4], pmb[:, :1024].rearrange("p (h q t) -> p h q t", q=2, t=128), mk4,
                       ALU.mult, pmbb + [b_kst], [b_Mb])
                    tt("dve", Mk[:, hg * 4:(hg + 1) * 4], pmk[:, :1024].rearrange("p (h q t) -> p h q t", q=2, t=128), mk4,
                       ALU.mult, pmkb + [b_kst], [b_Mk])
                pnt, pntb = psum(2)
                for h in range(8):
                    pr, hh = h // 2, h % 2
                    mm(pnt[:, h * 128:(h + 1) * 128], ARTz[hh][:, pr, 0, :], BKT[:, pr, 0, :], True, True,
                       [b_BKT, b_ARTz], pntb)
                tt("dve", NTt[0], pnt[:, :1024].rearrange("p (h t) -> p h t", t=128),
                   nmask2.unsqueeze(1).to_broadcast([128, 8, 128]), ALU.mult, pntb + [b_kst], [b_NTt[0]])
                cp("pool", Nn[0], Mb[:, :, 0, :], [b_Mb], [b_Nn[0]])
                tt("pool", Pp, Mb[:, :, 0, :], ident_b.unsqueeze(1).to_broadcast([128, 8, 128]), ALU.add,
                   [b_Mb, b_const], [b_Pp])
                cur = 0
                for lev in range(5):
                    nx = 1 - cur
                    pN, pNb = psum(2)
                    pNT, pNTb = psum(2)
                    for h in range(8):
                        hs2 = slice(h * 128, (h + 1) * 128)
                        mm(pN[:, hs2], NTt[cur][:, h, :], Nn[cur][:, h, :], True, True, [b_NTt[cur], b_Nn[cur]], pNb)
                        mm(pNT[:, hs2], Nn[cur][:, h, :], NTt[cur][:, h, :], True, True, [b_NTt[cur], b_Nn[cur]], pNTb)
                    cp("dve", Nn[nx].rearrange("p h t -> p (h t)"), pN[:, :1024], pNb, [b_Nn[nx]])
                    cp("act", NTt[nx].rearrange("p h t -> p (h t)"), pNT[:, :1024], pNTb, [b_NTt[nx]])
                    pP, pPb = psum(2)
                    for h in range(8):
                        hs2 = slice(h * 128, (h + 1) * 128)
                        mm(pP[:, hs2], NTt[nx][:, h, :], Pp[:, h, :], True, True, [b_NTt[nx], b_Pp], pPb)
                    tt("dve", Pp.rearrange("p h t -> p (h t)"), Pp.rearrange("p h t -> p (h t)"), pP[:, :1024],
                       ALU.add, [b_Pp] + pPb, [b_Pp])
                    cur = nx
                for c in range(2):
                    cs_ = slice(c * 64, (c + 1) * 64)
                    pW, pWb = psum(1)
                    for h in range(8):
                        pr, hh = h // 2, h % 2
                        hs = slice(h * 64, (h + 1) * 64)
                        mm(pW[:, hs], ARTz[hh][:, pr, 0, :], Sbf[:, pr, :], True, False, [b_ARTz, b_Sbf], pWb)
                        mm(pW[:, hs], Mk[:, h, 0, :], TB["Vt"][:, hs], False, True, [b_Mk, BB["Vt"]], pWb)
                    cp("dve", Wsb[cs_].rearrange("p h t -> p (h t)"), pW[cs_, :512], pWb, [b_Wsb])
                    pU, pUb = psum(1)
                    for h in range(8):
                        hs = slice(h * 64, (h + 1) * 64)
                        mm(pU[:, hs], Pp[:, h, :], Wsb[:, h, :], True, True, [b_Pp, b_Wsb], pUb)
                    cp("act", Usb[cs_].rearrange("p h t -> p (h t)"), pU[cs_, :512], pUb, [b_Usb])
                    pY, pYb = psum(1)
                    pS, pSb = psum(1)
                    for h in range(8):
                        pr, hh = h // 2, h % 2
                        hs = slice(h * 64, (h + 1) * 64)
                        mm(pY[:, hs], ARTz[hh][:, pr, 1, :], Sbf[:, pr, :], True, False, [b_ARTz, b_Sbf], pYb)
                        mm(pY[:, hs], Mb[:, h, 1, :], Usb[:, h, :], False, False, [b_Mb, b_Usb], pYb)
                        mm(pY[:, hs], Mk[:, h, 1, :], TB["Vt"][:, hs], False, True, [b_Mk, BB["Vt"]], pYb)
                        mm(pS[:, hs], TB["Bz%d" % c][:, pr * 128:(pr + 1) * 128], Usb[:, h, :], True, False,
                           [BB["Bz%d" % c], b_Usb], pSb)
                        mm(pS[:, hs], TB["Kz%d" % c][:, pr * 128:(pr + 1) * 128], TB["Vt"][:, hs], False, True,
                           [BB["Kz%d" % c], BB["Vt"]], pSb)
                    cp("act", Ysb[cs_].rearrange("p h t -> p (h t)"), pY[cs_, :512], pYb, [b_Ysb])
                    gcv = gC.rearrange("p (a c) -> p a c", c=2)[:, :, c:c + 1].to_broadcast([128, 4, 64])
                    tt("dve", S32, S32, gcv, ALU.mult, [b_S32, b_gC], [b_S32])
                    pS4 = pS[:, :512].rearrange("p (a hh v) -> p a hh v", hh=2, v=64)
                    for hh in range(2):
                        rs_ = slice(hh * 64, (hh + 1) * 64)
                        tt("dve", S32[rs_], S32[rs_], pS4[rs_, :, hh, :], ALU.add, [b_S32] + pSb, [b_S32])
                    cp("act", Sbf, S32, [b_S32], [b_Sbf])
                R.op("dve", lambda e: e.tensor_reduce(s1, Ysb, AX.X, ALU.add), [b_Ysb], [b_st])
                act(T["t1"], Ysb.rearrange("p h t -> p (h t)"), AF.Square, [b_Ysb], [Bf["t1"]])
                R.op("dve", lambda e: e.tensor_reduce(s2, v3(T["t1"]), AX.X, ALU.add), [Bf["t1"]], [b_st])
                ts("dve", mean, s1, 1.0 / 64, None, ALU.mult, ALU.bypass, [b_st], [b_st])
                tt("dve", varr, mean, mean, ALU.mult, [b_st], [b_st])
                stt(varr, s2, 1.0 / 64, varr, ALU.mult, ALU.subtract, [b_st], [b_st])
                ts("dve", varr, varr, 1.0, A_LN_EPS, ALU.mult, ALU.add, [b_st], [b_st])
                act(varr, varr, AF.Ln, [b_st], [b_st])
                act(varr, varr, AF.Exp, [b_st], [b_st], scale=-0.5)
                tt("dve", Ysb, Ysb, bc8(mean), ALU.subtract, [b_Ysb, b_st], [b_Ysb])
                tt("dve", Ysb, Ysb, bc8(varr), ALU.mult, [b_Ysb, b_st], [b_Ysb])
                y2 = Ysb.rearrange("p h t -> p (h t)")
                tt("pool", y2, y2, lgb, ALU.mult, [b_Ysb, b_ra], [b_Ysb])
                tt("pool", y2, y2, lbb, ALU.add, [b_Ysb, b_ra], [b_Ysb])
                tt("dve", v3(T["t2"]), v3(T["v"]), bc8(bon), ALU.mult, [Bf["v"], b_st], [Bf["t2"]])
                tt("pool", y2, y2, T["t2"], ALU.add, [b_Ysb, Bf["t2"]], [b_Ysb])
                tt("dve", ya, y2, T["g"], ALU.mult, [b_Ysb, Bf["g"]], [b_ya])
                pt, ptb = psum(1)
                ptv = pt.bitcast(BF16)
                for pr in range(4):
                    tr(ptv[:, pr * 128:(pr + 1) * 128], ya[:, pr * 128:(pr + 1) * 128], ident_b, [b_ya, b_const], ptb)
                cp("act", yaT.rearrange("p a t -> p (a t)"), ptv[:, :512], ptb, [b_yaT])
                dma("act", s_y[f0:f0 + 512, tc0:tc0 + 128].rearrange("(a p) t -> p a t", p=128), yaT, [b_yaT],
                    [sy_buf(hf * 4 + kc, gi) for kc in range(4)])

        n0 = len(R.ops)
        ps_lim[0], ps_lim[1] = 0, 4
        r2_stream(0, A)
        n1 = len(R.ops)
        ps_lim[0], ps_lim[1] = 4, 8
        r2_stream(1, A2)
        n2 = len(R.ops)
        ps_lim[0], ps_lim[1] = 0, 6
        R.interleave(n0, n1, n2)
        R.barrier()
        dma("sp", hv, s_hT, (), b_h)
        dma("sp", uv, s_uT, (), [b_u])
        R.barrier()
        A.release(m0)

    PHASES_PLACEHOLDER = None

    for l in range(depth):
        mL = A.mark()
        colp_t = A.alloc([128, NCOLP], F32)
        b_colp = Buf("colp")
        dma("sp", colp_t, colp[l], (), [b_colp])
        phase_norm(l, 0, colp_t=colp_t, b_colp=b_colp)
        R.barrier()
        if l == 0 and cfg.do_merge:
            zr = []
            if not cfg.do_rwkv:
                zr += list(range(0, 8))
            if not cfg.do_ssm:
                zr += list(range(8, 24))
            if not cfg.do_ret:
                zr += list(range(24, 32))
            if zr:
                mz = A.mark()
                zt = A.alloc([128, 512], BF16)
                b_zt = Buf("zt")
                memset("pool", zt, 0.0, [b_zt])
                for kc in zr:
                    for gi, (gs, gn) in enumerate(groups):
                        dma("act", s_y[kc * 128:(kc + 1) * 128, gs:gs + gn], zt[:, :gn], [b_zt], [sy_buf(kc, gi)])
                R.barrier()
                A.release(mz)
        if cfg.do_rwkv:
            R.phase = "L%d_rwkv" % l
            phase_rwkv(l, colp_t, b_colp)
            R.barrier()
        if cfg.do_ssm:
            R.phase = "L%d_ssm" % l
            phase_ssm(l, colp_t, b_colp)
            R.barrier()
        if cfg.do_ret:
            R.phase = "L%d_ret" % l
            phase_ret(l, colp_t, b_colp)
            R.barrier()
        def dump(slot):
            if debug:
                for gi, (gs, gn) in enumerate(groups):
                    dma("sp", dbg_h[slot].rearrange("(a p) t -> p a t", p=128)[:, :, gs:gs + gn], hT[:, :, gs:gs + gn],
                        [b_h[gi]], [Buf("dbg")])
                R.barrier()
        def zero_pad():
            if TP > L:
                memset("pool", hT[:, :, L:TP], 0.0, [b_h[len(groups) - 1]])
        if cfg.do_merge:
            R.phase = "L%d_merge" % l
            phase_merge(l, colp_t, b_colp)
            zero_pad()
            R.barrier()
        dump(l * 3)
        if cfg.do_ffn:
            R.phase = "L%d_ffn" % l
            phase_ffn(l, colp_t, b_colp)
            zero_pad()
            R.barrier()
        dump(l * 3 + 1)
        A.release(mL)

    m = A.mark()
    nf_t = A.alloc([128, 8], F32)
    b_nf = Buf("nf")
    dma("sp", nf_t, nfin, (), [b_nf])
    ot = A.alloc([128, 8, 512], F32)
    b_ot = Buf("ot")
    for gi, (gs, gn) in enumerate(groups):
        phase_norm(0, 0, gsel=[gi], dst=ot, dst_b=b_ot, colp_t=nf_t, b_colp=b_nf)
        lo = max(gs, NM)
        hi = min(gs + gn, L)
        if hi > lo:
            dma("act", outT.rearrange("(a p) t -> p a t", p=128)[:, :, lo - NM:hi - NM], ot[:, :, lo - gs:hi - gs],
                [b_ot], [Buf("out%d" % gi)])
    A.release(m)
    R.barrier()
    R.op("sp", lambda e: e.nop(), (), ())
    R.emit(nc, None)
    return nc


def _cols(v, n):
    return np.ascontiguousarray(np.asarray(v, np.float32).reshape(n, 128).T)


def _rep(v):
    return np.broadcast_to(np.asarray(v, np.float32).reshape(1, -1), (128, np.asarray(v).size))


_PERM = np.concatenate([np.arange(0, 64, 2), np.arange(1, 64, 2)])
_PART = np.concatenate([_PERM[32:], _PERM[:32]])


def prep_shared(cfg, inp):
    depth = cfg.depth
    f = lambda k: np.asarray(inp[k], np.float32)
    sh = {}
    ch = host_consts(cfg)
    for k in CONST_NAMES:
        sh["c_" + k] = np.ascontiguousarray(ch[k])
    w_in = f("w_in")[:depth]
    sh["w_in"] = np.ascontiguousarray(w_in)
    qk = np.empty((depth, D, 1024), np.float32)
    rot = np.empty((depth, D, 1024), np.float32)
    for part in range(2):
        for h in range(8):
            base = OFF_C + part * 512 + h * 64
            qk[:, :, part * 512 + h * 64:part * 512 + (h + 1) * 64] = w_in[:, :, base + _PERM]
            rot[:, :, part * 512 + h * 64:part * 512 + (h + 1) * 64] = w_in[:, :, base + _PART]
    sh["w_qk"] = qk
    sh["w_rot"] = rot
    nv = max(depth - 1, 1)
    wv = np.zeros((nv, D, 32), np.float32)
    rv2 = np.zeros((nv, 32, D), np.float32)
    if depth > 1:
        wv[:] = f("w_in_vres")[:depth - 1]
        rv2[:] = f("rwkv_v2")[:depth - 1]
    sh["w_vres"] = wv
    sh["rv2"] = rv2
    sh["w_branch"] = np.ascontiguousarray(f("w_branch")[:depth])
    sh["w_out"] = np.ascontiguousarray(f("w_out")[:depth])
    sh["w_up"] = np.ascontiguousarray(f("ffn_w_up")[:depth])
    sh["w_down"] = np.ascontiguousarray(f("ffn_w_down")[:depth])
    sh["rw2"] = np.ascontiguousarray(np.concatenate([f("rwkv_w2")[:depth], f("rwkv_a2")[:depth]], axis=1))
    sh["rg2"] = np.ascontiguousarray(f("rwkv_g2")[:depth])
    colp = []
    rowa = []
    rowb = []
    mu_all = []
    cw_all = []
    brow = []
    for l in range(depth):
        fw = f("ffn_conv_w")[l]
        colp.append(np.concatenate([
            _cols(f("norm_mix")[l], 8), _cols(f("norm_ffn")[l], 8), _cols(f("gate_bias")[l], 24),
            _cols(f("ssm_conv_b")[l][2048:3072], 8),
            _cols(fw[0], 44), _cols(fw[1], 44), _cols(fw[2], 44), _cols(f("ffn_conv_b")[l], 44),
            np.zeros((128, 1), np.float32)], axis=1))
        v0 = f("rwkv_v0")[l - 1] if l > 0 else np.zeros(1024, np.float32)
        rowa.append(np.concatenate([_rep(f("rwkv_w0")[l]), _rep(f("rwkv_a0")[l]), _rep(f("rwkv_k_k")[l]),
                                    _rep(f("rwkv_k_a")[l]), _rep(f("rwkv_r_k")[l].reshape(-1)),
                                    _rep(f("rwkv_ln_g")[l]), _rep(f("rwkv_ln_b")[l]), _rep(v0),
                                    np.zeros((128, 1024), np.float32)], axis=1))
        rowb.append(np.concatenate([_rep(f("ssm_norm_g")[l]), _rep(f("ssm_d")[l]), _rep(f("ssm_a_log")[l]),
                                    np.zeros((128, 32), np.float32)], axis=1))
        muv = f("rwkv_mu_vres")[l - 1] if l > 0 else np.zeros(32, np.float32)
        mu_all.append(_rep(np.concatenate([f("rwkv_mu")[l], muv])))
        cw_all.append(_rep(f("ssm_conv_w")[l].reshape(-1)))
        brow.append(np.concatenate([f("ssm_conv_b")[l], f("ssm_dt_bias")[l]]).reshape(1, -1))
    sh["colp"] = np.ascontiguousarray(np.stack(colp))
    sh["rowa"] = np.ascontiguousarray(np.stack(rowa))
    sh["rowb"] = np.ascontiguousarray(np.stack(rowb))
    sh["mu_all"] = np.ascontiguousarray(np.stack(mu_all))
    sh["cw_all"] = np.ascontiguousarray(np.stack(cw_all))
    sh["brow"] = np.ascontiguousarray(np.stack(brow))
    sh["nfin"] = _cols(f("norm_final"), 8)
    for n in STACKED:
        arr = sh.pop(n)
        for i in range(arr.shape[0]):
            sh["%s_%d" % (n, i)] = np.ascontiguousarray(arr[i])
    return sh


def run(cfg, inp, n_cores=8, debug=False):
    x = np.asarray(inp["x"], np.float32)
    meta = np.asarray(inp["meta"], np.float32)
    bsz = x.shape[0]
    sh = prep_shared(cfg, inp)
    nc = build(cfg, debug=debug)
    in_maps = []
    for b in range(bsz):
        xT = np.zeros((D, cfg.TP), np.float32)
        xT[:, :NM] = meta.T
        xT[:, NM:cfg.L] = x[b].T
        m = dict(sh)
        m["xT"] = xT
        in_maps.append(m)
    res = run_bass_kernel_spmd(nc, in_maps, core_ids=list(range(bsz)))
    out = np.stack([np.ascontiguousarray(r["outT"].T) for r in res.results], axis=0)
    if debug:
        return out.astype(np.float32), [(r["dbg_h"], r["s_y"]) for r in res.results]
    return out.astype(np.float32)


def kernel(**inputs):
    cfg = Cfg(inputs["x"].shape[1], 4)
    return run(cfg, inputs)
```

```python
import math
import numpy as np
import concourse.bass as bass
import concourse.mybir as mybir
from concourse.bass_utils import run_bass_kernel_spmd

F32 = mybir.dt.float32
BF16 = mybir.dt.bfloat16
U8 = mybir.dt.uint8
AF = mybir.ActivationFunctionType
ALU = mybir.AluOpType
AX = mybir.AxisListType

D = 1024
NM = 16
A_IN = 3328
B_IN = 5152
C_IN = 3072
W_IN = 14624
OFF_B = A_IN
OFF_C = A_IN + B_IN
OFF_G = A_IN + B_IN + C_IN
FH = 2816
NFC = 22
EPS = 1e-6
A_LN_EPS = 64e-5
SHIFT = 3
NEGBIG = -30000.0
WDEC = math.exp(-0.5)


class Buf:
    __slots__ = ("name", "w", "r")
    ALL = []

    def __init__(self, name):
        self.name = name
        self.w = None
        self.r = []
        Buf.ALL.append(self)


class Rec:
    ENGS = ("pe", "act", "dve", "pool", "sp")

    def __init__(self):
        self.ops = []
        self.pending = {e: set() for e in self.ENGS}
        self.last = {e: None for e in self.ENGS}
        self.dmas_since = []

    def op(self, eng, fn, reads=(), writes=(), dma=False):
        deps = {}
        for b in reads:
            if b.w is not None:
                deps[b.w] = "raw"
        for b in writes:
            if b.w is not None and b.w not in deps:
                deps[b.w] = "waw"
            for r in b.r:
                if r not in deps:
                    deps[r] = "war"
        for d in self.pending[eng]:
            deps[d] = "raw"
        self.pending[eng] = set()
        i = len(self.ops)
        deps.pop(i, None)
        self.ops.append(dict(eng=eng, fn=fn, dma=dma, deps=deps, id=i, ph=getattr(self, "phase", "")))
        for b in writes:
            b.w = i
            b.r = []
        for b in reads:
            if b.w != i:
                b.r.append(i)
        if dma:
            self.dmas_since.append(i)
        else:
            self.last[eng] = i
        return i

    def interleave(self, a0, a1, b1):
        assert b1 == len(self.ops)
        sa, sb = self.ops[a0:a1], self.ops[a1:b1]
        merged = []
        ia = ib = 0
        na, nb = len(sa), len(sb)
        while ia < na or ib < nb:
            if ib >= nb or (ia < na and ia * nb <= ib * na):
                merged.append(sa[ia]); ia += 1
            else:
                merged.append(sb[ib]); ib += 1
        remap = {}
        for k, o in enumerate(merged):
            remap[o["id"]] = a0 + k
        f = lambda i: remap.get(i, i)
        for o in merged:
            o["deps"] = {f(d): kind for d, kind in o["deps"].items()}
            o["id"] = f(o["id"])
        self.ops[a0:b1] = merged
        for b in Buf.ALL:
            if b.w is not None:
                b.w = f(b.w)
            b.r = [f(x) for x in b.r]
        for e in self.ENGS:
            if self.last[e] is not None:
                self.last[e] = f(self.last[e])
            self.pending[e] = set(f(x) for x in self.pending[e])
        self.dmas_since = [f(x) for x in self.dmas_since]
        for e in self.ENGS:
            cand = [o["id"] for o in merged if o["eng"] == e and not o["dma"]]
            if cand:
                self.last[e] = max(cand)

    def barrier(self):
        ids = set(v for v in self.last.values() if v is not None) | set(self.dmas_since)
        for e in self.ENGS:
            self.pending[e] |= ids
        self.dmas_since = []

    def emit(self, nc, engines):
        ops = self.ops
        per = {e: [o for o in ops if o["eng"] == e] for e in self.ENGS}
        for e in self.ENGS:
            n = 0
            for o in per[e]:
                if not o["dma"]:
                    n += 1
                    o["eidx"] = n
        signal = set()
        for e in self.ENGS:
            waited = {x: 0 for x in self.ENGS}
            wdma = set()
            for o in per[e]:
                cw = {}
                dw = []
                for d, kind in o["deps"].items():
                    P = ops[d]
                    if P["dma"]:
                        if d not in wdma:
                            wdma.add(d)
                            dw.append(d)
                    else:
                        E = P["eng"]
                        if E == e and kind != "raw":
                            continue
                        if P["eidx"] > waited[E]:
                            if E not in cw or ops[cw[E]]["eidx"] < P["eidx"]:
                                cw[E] = d
                for E, d in cw.items():
                    waited[E] = ops[d]["eidx"]
                    signal.add(d)
                o["cw"] = list(cw.values())
                o["dw"] = dw
        EPOCH = 4000
        sems = {}

        def getsem(key):
            if key not in sems:
                sems[key] = nc.alloc_semaphore("s_%s_%d" % key)
            return sems[key]

        for e in self.ENGS:
            r = 0
            for o in per[e]:
                if not o["dma"] and o["id"] in signal:
                    o["sig"] = (e, r // EPOCH, r % EPOCH + 1)
                    r += 1
        NSLOT = {"sp": 32, "pool": 12, "act": 24, "dve": 4, "pe": 4}
        for e in self.ENGS:
            k = 0
            uses = [0] * NSLOT[e]
            for o in per[e]:
                if o["dma"]:
                    s = k % NSLOT[e]
                    o["slot"] = (e, s, uses[s] * 16)
                    uses[s] += 1
                    o["tok"] = (("dma_" + e, s), uses[s] * 16)
                    assert uses[s] * 16 < 4000
                    k += 1

        def run(e, eng):
            for o in per[e]:
                for d in o["cw"]:
                    E, ep, val = ops[d]["sig"]
                    eng.wait_ge(getsem((E, ep)), val)
                for d in o["dw"]:
                    key, val = ops[d]["tok"]
                    eng.wait_ge(getsem(key), val)
                if o["dma"]:
                    _, s, prev = o["slot"]
                    sm = getsem(("dma_" + e, s))
                    if prev > 0:
                        eng.wait_ge(sm, prev)
                    ins = o["fn"](eng)
                    ins.then_inc(sm, 16)
                else:
                    ins = o["fn"](eng)
                    if "sig" in o:
                        E, ep, val = o["sig"]
                        ins.then_inc(getsem((E, ep)), 1)

        for e in self.ENGS:
            for o in per[e]:
                if "sig" in o:
                    getsem((o["sig"][0], o["sig"][1]))
                if o["dma"]:
                    getsem(("dma_" + e, o["slot"][1]))
        with nc.Block() as block:
            @block.tensor
            def _(eng):
                run("pe", eng)

            @block.scalar
            def _(eng):
                run("act", eng)

            @block.vector
            def _(eng):
                run("dve", eng)

            @block.gpsimd
            def _(eng):
                run("pool", eng)

            @block.sync
            def _(eng):
                run("sp", eng)


class Arena:
    def __init__(self, nc, nbytes, t=None, base=0):
        self.t = nc.alloc_sbuf_tensor("arena", [128, nbytes], U8).ap() if t is None else t
        self.n = nbytes
        self.off = base
        self.peak = 0

    def sub(self, base, limit):
        return Arena(None, limit, t=self.t, base=base)

    def alloc(self, shape, dtype):
        esz = 4 if dtype == F32 else 2
        n = 1
        for s in shape[1:]:
            n *= s
        nb = (n * esz + 31) // 32 * 32
        assert self.off + nb <= self.n, "SBUF arena overflow %d + %d" % (self.off, nb)
        v = self.t[0:shape[0], self.off:self.off + n * esz].bitcast(dtype)
        self.off += nb
        self.peak = max(self.peak, self.off)
        if len(shape) == 3:
            v = v.rearrange("p (a b) -> p a b", b=shape[2])
        elif len(shape) == 4:
            v = v.rearrange("p (a b c) -> p a b c", b=shape[2], c=shape[3])
        return v

    def mark(self):
        return self.off

    def release(self, m):
        self.off = m


class Cfg:
    def __init__(self, lx, depth):
        self.lx = lx
        self.depth = depth
        self.L = NM + lx
        self.NT = (self.L + 127) // 128
        self.TP = self.NT * 128
        self.do_rwkv = self.do_ssm = self.do_ret = self.do_merge = self.do_ffn = True
        self.groups = []
        s = 0
        while s < self.TP:
            n = min(512, self.TP - s)
            self.groups.append((s, n))
            s += n


def host_consts(cfg):
    TP = cfg.TP
    c = {}
    c["ident"] = np.eye(128, dtype=np.float32)
    j = np.arange(128)
    blk = (j[:, None] // 64) == (j[None, :] // 64)
    c["tri64"] = (blk & (j[:, None] <= j[None, :])).astype(np.float32)
    c["sfx64"] = (blk & (j[:, None] > j[None, :])).astype(np.float32)
    c["tri128"] = (j[:, None] <= j[None, :]).astype(np.float32)
    c["sfx128"] = (j[:, None] > j[None, :]).astype(np.float32)
    c["negU"] = np.tile(((j[:, None] > j[None, :]) * NEGBIG).astype(np.float32), (1, 4))
    c["causal"] = (j[:, None] <= j[None, :]).astype(np.float32)
    s = j % 64
    m = np.zeros((128, 128), np.float32)
    m[:, :64] = (s[:, None] < s[None, :64])
    m[:, 64:] = (s[:, None] <= s[None, :64])
    c["mmask"] = m
    c["mmaskT"] = np.ascontiguousarray((s[None, :64] < s[:, None]).astype(np.float32))
    cc = j // 64
    c["nmask2"] = ((cc[:, None] == cc[None, :]) & (s[None, :] < s[:, None])).astype(np.float32)
    m2 = np.zeros((128, 2, 128), np.float32)
    same = (cc[:, None] == cc[None, :])
    m2[:, 0, :] = same & (s[:, None] < s[None, :])
    m2[:, 1, :] = same & (s[:, None] <= s[None, :])
    c["mmask2"] = m2.reshape(128, 256)
    c["ident64x2"] = np.concatenate([np.eye(64, dtype=np.float32)] * 2, axis=0)
    pos = np.arange(TP, dtype=np.float32)
    inv = (1.0 / (10000.0 ** np.linspace(0.0, 1.0, 32, dtype=np.float32))).astype(np.float32)
    ang = pos[None, :] * inv[:, None]
    cos = np.cos(ang).astype(np.float32)
    sin = np.sin(ang).astype(np.float32)
    cos64 = np.concatenate([cos, cos], 0)
    sin64 = np.concatenate([-sin, sin], 0)
    c["cosq"] = np.concatenate([cos64, cos64], 0)
    c["sinq"] = np.concatenate([sin64, sin64], 0)
    c["cosk"] = c["cosq"] * np.float32(0.125)
    c["sink"] = c["sinq"] * np.float32(0.125)
    lg = np.log(1.0 - 2.0 ** (-5.0 - np.arange(8, dtype=np.float64)))
    rel = np.arange(TP)[None, :] - np.arange(128)[:, None]
    tabs = []
    for h in range(8):
        tabs.append(np.where(rel >= 0, np.exp(np.maximum(rel, 0) * lg[h]), 0.0).astype(np.float32))
    c["rdec"] = np.stack(tabs, 0)
    return c


STACKED = ["w_in", "w_rot", "w_qk", "w_vres", "w_branch", "w_out", "w_up", "w_down", "rw2", "rg2", "rv2", "colp", "rowa",
           "rowb", "mu_all", "cw_all", "brow"]
CONST_NAMES = ["ident", "tri64", "sfx64", "tri128", "sfx128", "negU", "causal", "mmask", "mmaskT",
               "cosq", "sinq", "cosk", "sink", "rdec", "nmask2", "mmask2"]


def build(cfg, debug=False):
    Buf.ALL = []
    nc = bass.Bass("TRN2", target_bir_lowering=False)
    R = Rec()
    TP, NT, L, depth = cfg.TP, cfg.NT, cfg.L, cfg.depth
    groups = cfg.groups

    def din(name, shape):
        return nc.dram_tensor(name, list(shape), F32, kind="ExternalInput").ap()

    xT = din("xT", [D, TP])
    cst = {}
    ch = host_consts(cfg)
    for k in CONST_NAMES:
        cst[k] = din("c_" + k, ch[k].shape)
    w_in = [din("w_in_%d" % i_, [D, W_IN]) for i_ in range(depth)]
    w_rot = [din("w_rot_%d" % i_, [D, 1024]) for i_ in range(depth)]
    w_qk = [din("w_qk_%d" % i_, [D, 1024]) for i_ in range(depth)]
    w_vres = [din("w_vres_%d" % i_, [D, 32]) for i_ in range(max(depth - 1, 1))]
    w_branch = [din("w_branch_%d" % i_, [4096, D]) for i_ in range(depth)]
    w_out = [din("w_out_%d" % i_, [D, D]) for i_ in range(depth)]
    w_up = [din("w_up_%d" % i_, [D, 2 * FH]) for i_ in range(depth)]
    w_down = [din("w_down_%d" % i_, [FH, D]) for i_ in range(depth)]
    rw2 = [din("rw2_%d" % i_, [128, D]) for i_ in range(depth)]
    rg2 = [din("rg2_%d" % i_, [128, D]) for i_ in range(depth)]
    rv2 = [din("rv2_%d" % i_, [32, D]) for i_ in range(max(depth - 1, 1))]
    NCOLP = 8 + 8 + 24 + 8 + 4 * 44 + 1
    colp = [din("colp_%d" % i_, [128, NCOLP]) for i_ in range(depth)]
    NROWA = 9 * 1024
    rowa = [din("rowa_%d" % i_, [128, NROWA]) for i_ in range(depth)]
    NROWB = 2048 + 32 * 3
    rowb = [din("rowb_%d" % i_, [128, NROWB]) for i_ in range(depth)]
    mu_all = [din("mu_all_%d" % i_, [128, 3328 + 32]) for i_ in range(depth)]
    cw_all = [din("cw_all_%d" % i_, [128, 4 * 3072]) for i_ in range(depth)]
    brow = [din("brow_%d" % i_, [1, 3072 + 32]) for i_ in range(depth)]
    nfin = din("nfin", [128, 8])
    outT = nc.dram_tensor("outT", [D, cfg.lx], F32, kind="ExternalOutput").ap()
    dbg_h = nc.dram_tensor("dbg_h", [depth * 3, D, TP], F32, kind="ExternalOutput").ap() if debug else None
    def dscr(name, shape, dt=F32):
        return nc.dram_tensor(name, list(shape), dt, kind="Internal").ap()

    s_rkv = dscr("s_rkv", [TP, 3072])
    s_vfirst = s_rkv
    s_vf = dscr("s_vf", [TP, 1024])
    s_mx = dscr("s_mx", [TP, 2048 + 2048 + 512 + 32])
    s_y = (nc.dram_tensor("s_y", [4096, TP], BF16, kind="ExternalOutput").ap() if debug else dscr("s_y", [4096, TP], BF16))
    s_lo = dscr("s_lo", [384, TP], BF16)
    b_rkv = [Buf("s_rkv%d" % i) for i in range(NT)]
    b_vf = [Buf("s_vf%d" % i) for i in range(NT)]
    b_mx = [Buf("s_mx%d" % i) for i in range(NT)]
    b_y = [[Buf("s_y%d_%d" % (m, i)) for i in range(NT)] for m in range(3)]

    A = Arena(nc, 208000)
    hT = A.alloc([128, 8, TP], F32)
    h_bytes = A.off
    b_h = [Buf("h%d" % g) for g in range(len(groups))]
    uT = A.alloc([128, 8, SHIFT + TP], BF16)
    b_u = Buf("uT")
    hu_bytes = A.off
    s_hT = dscr("s_hT", [128, 8 * TP])
    s_uT = dscr("s_uT", [128, 8 * (SHIFT + TP)], BF16)
    ident_f = A.alloc([128, 128], F32)
    ident_b = A.alloc([128, 128], BF16)
    ones_f = A.alloc([128, 128], F32)
    ones_b = A.alloc([128, 128], BF16)
    b_const = Buf("const")
    PS = nc.alloc_psum_tensor("ps", [128, 8, 512], F32).ap()
    b_ps = [Buf("ps%d" % i) for i in range(8)]
    ps_rr = [0]
    ps_lim = [0, 6]

    def psum(nb=1):
        lo, hi = ps_lim
        s = ps_rr[0]
        if s < lo or s >= hi:
            s = lo
        if (s - lo) % nb:
            s += nb - (s - lo) % nb
        if s + nb > hi:
            s = lo
        ps_rr[0] = s + nb
        ap = PS[:, s:s + nb, :].rearrange("p a b -> p (a b)") if nb > 1 else PS[:, s, :]
        return ap, b_ps[s:s + nb]

    pin_rr = [0]

    def psum_pin(k=None):
        if k is None:
            s = 6 + pin_rr[0] % 2
            pin_rr[0] += 1
        else:
            s = 6 + k
        return PS[:, s, :], b_ps[s:s + 1]

    def dma(q, out, in_, reads=(), writes=()):
        return R.op(q, lambda e: e.dma_start(out=out, in_=in_), reads, writes, dma=True)

    def mm(out, lhsT, rhs, start, stop, reads, writes):
        return R.op("pe", lambda e: e.matmul(out, lhsT, rhs, start=start, stop=stop), reads, writes)

    def tr(out, in_, ident, reads, writes):
        return R.op("pe", lambda e: e.transpose(out, in_, ident), reads, writes)

    def act(out, in_, func, reads, writes, bias=None, scale=None, eng="act", accum=None):
        kw = {}
        if bias is not None:
            kw["bias"] = bias
        if scale is not None:
            kw["scale"] = scale
        if accum is not None:
            kw["accum_out"] = accum
        return R.op(eng, lambda e: e.activation(out, in_, func, **kw), reads, writes)

    def tt(eng, out, in0, in1, op, reads, writes):
        return R.op(eng, lambda e: e.tensor_tensor(out, in0, in1, op), reads, writes)

    def ts(eng, out, in0, s1, s2, op0, op1, reads, writes):
        return R.op(eng, lambda e: e.tensor_scalar(out, in0, s1, s2, op0, op1), reads, writes)

    def stt(out, in0, scalar, in1, op0, op1, reads, writes):
        return R.op("dve", lambda e: e.scalar_tensor_tensor(out, in0, scalar, in1, op0, op1), reads, writes)

    def cp(eng, out, in_, reads, writes):
        if eng == "act":
            return R.op("act", lambda e: e.copy(out, in_), reads, writes)
        return R.op(eng, lambda e: e.tensor_copy(out, in_), reads, writes)

    def memset(eng, ap, val, writes):
        return R.op(eng, lambda e: e.memset(ap, val), (), writes)

    def rsqrt_inplace(ap, bufs, scale, eps):
        ts("dve", ap, ap, scale, eps, ALU.mult, ALU.add, bufs, bufs)
        act(ap, ap, AF.Ln, bufs, bufs)
        act(ap, ap, AF.Exp, bufs, bufs, scale=-0.5)

    dma("sp", ident_f, cst["ident"], (), [b_const])
    cp("dve", ident_b, ident_f, [b_const], [b_const])
    memset("dve", ones_f, 1.0, [b_const])
    memset("dve", ones_b, 1.0, [b_const])
    memset("pool", uT[:, :, 0:SHIFT], 0.0, [b_u])
    for gi, (gs, gn) in enumerate(groups):
        dma("sp", hT[:, :, gs:gs + gn], xT.rearrange("(a p) t -> p a t", p=128)[:, :, gs:gs + gn], (), [b_h[gi]])

    def load_cast(dst_bf, src_f32, bufs_w, reads=()):
        return dma("pool", dst_bf, src_f32, reads, bufs_w)

    def phase_norm(l, which, gsel=None, dst=None, dst_b=None, colp_t=None, b_colp=None):
        m = A.mark()
        sq = A.alloc([128, 512], F32)
        rs = A.alloc([128, 512], F32)
        b_sq, b_rs = Buf("sq"), Buf("rs")
        for gi, (gs, gn) in enumerate(groups):
            if gsel is not None and gi not in gsel:
                continue
            pa, pb = psum(1)
            for kc in range(8):
                act(sq[:, :gn], hT[:, kc, gs:gs + gn], AF.Square, [b_h[gi]], [b_sq])
                mm(pa[:, :gn], ones_f, sq[:, :gn], kc == 0, kc == 7, [b_sq, b_const], pb)
            ts("dve", rs[:, :gn], pa[:, :gn], 1.0 / D, EPS, ALU.mult, ALU.add, pb, [b_rs])
            act(rs[:, :gn], rs[:, :gn], AF.Ln, [b_rs], [b_rs])
            act(rs[:, :gn], rs[:, :gn], AF.Exp, [b_rs], [b_rs], scale=-0.5)
            for kc in range(8):
                o = (dst[:, kc, 0:gn] if dst is not None else uT[:, kc, SHIFT + gs:SHIFT + gs + gn])
                stt(o, hT[:, kc, gs:gs + gn], colp_t[:, which * 8 + kc:which * 8 + kc + 1], rs[:, :gn],
                    ALU.mult, ALU.mult, [b_h[gi], b_rs, b_colp], [dst_b if dst is not None else b_u])
        A.release(m)

    CP_NMIX, CP_NFFN, CP_GB, CP_BCB, CP_FW, CP_FB = 0, 1, 16, 40, 48, 48 + 3 * 44

    def phase_ffn(l, colp_t, b_colp):
        m = A.mark()
        u2 = A.alloc([128, 8, 512], BF16)
        b_u2 = Buf("u2")
        halo = A.alloc([128, 44, 2], F32)
        b_halo = Buf("halo")
        actT = A.alloc([128, NFC, 512], BF16)
        b_actT = Buf("actT")
        wup = [A.alloc([128, 8, 256], BF16) for _ in range(2)]
        b_wup = [Buf("wup0"), Buf("wup1")]
        wdn = [A.alloc([128, NFC, 128], BF16) for _ in range(2)]
        b_wdn = [Buf("wdn0"), Buf("wdn1")]
        X = [A.alloc([128, 514], F32) for _ in range(4)]
        b_X = [Buf("X%d" % i) for i in range(4)]
        cc_all = [A.alloc([128, 512], F32) for _ in range(4)]
        b_cc_all = [Buf("cc%d" % i) for i in range(4)]
        sg_all = [A.alloc([128, 512], F32) for _ in range(2)]
        b_sg_all = [Buf("sg0"), Buf("sg1")]
        memset("dve", halo, 0.0, [b_halo])
        wu = w_up[l].rearrange("(a p) f -> p a f", p=128)
        wd = w_down[l].rearrange("(a p) d -> p a d", p=128)
        for gi, (gs, gn) in enumerate(groups):
            phase_norm(l, 1, gsel=[gi], dst=u2, dst_b=b_u2, colp_t=colp_t, b_colp=b_colp)
            for fc in range(NFC):
                cc = cc_all[(fc % 2) * 2:(fc % 2) * 2 + 2]
                b_cc = b_cc_all[(fc % 2) * 2:(fc % 2) * 2 + 2]
                sg, b_sg = sg_all[fc % 2], b_sg_all[fc % 2]
                wbf = wup[fc % 2]
                load_cast(wbf[:, :, 0:128], wu[:, :, fc * 128:(fc + 1) * 128], [b_wup[fc % 2]])
                load_cast(wbf[:, :, 128:256], wu[:, :, FH + fc * 128:FH + (fc + 1) * 128], [b_wup[fc % 2]])
                for half in range(2):
                    ci = half * NFC + fc
                    pa, pb = psum(1)
                    for kc in range(8):
                        mm(pa[:, :gn], wbf[:, kc, half * 128:(half + 1) * 128], u2[:, kc, :gn], kc == 0, kc == 7,
                           [b_wup[fc % 2], b_u2], pb)
                    xi = (fc % 2) * 2 + half
                    Xh = X[xi]
                    cp("act", Xh[:, 2:2 + gn], pa[:, :gn], pb, [b_X[xi]])
                    cp("act", Xh[:, 0:2], halo[:, ci, :], [b_halo], [b_X[xi]])
                    c = cc[half]
                    w = lambda k: colp_t[:, CP_FW + k * 44 + ci:CP_FW + k * 44 + ci + 1]
                    bcol = colp_t[:, CP_FB + ci:CP_FB + ci + 1]
                    ts("dve", c[:, :gn], Xh[:, 2:2 + gn], w(2), bcol, ALU.mult, ALU.add, [b_X[xi], b_colp], [b_cc[half]])
                    stt(c[:, :gn], Xh[:, 1:1 + gn], w(1), c[:, :gn], ALU.mult, ALU.add, [b_X[xi], b_colp, b_cc[half]], [b_cc[half]])
                    stt(c[:, :gn], Xh[:, 0:gn], w(0), c[:, :gn], ALU.mult, ALU.add, [b_X[xi], b_colp, b_cc[half]], [b_cc[half]])
                    cp("act", halo[:, ci, :], Xh[:, gn:gn + 2], [b_X[xi]], [b_halo])
                act(sg[:, :gn], cc[0][:, :gn], AF.Silu, [b_cc[0]], [b_sg])
                tt("dve", actT[:, fc, :gn], sg[:, :gn], cc[1][:, :gn], ALU.mult, [b_sg, b_cc[1]], [b_actT])
            for dc in range(8):
                load_cast(wdn[dc % 2], wd[:, :, dc * 128:(dc + 1) * 128], [b_wdn[dc % 2]])
                pa, pb = psum(1)
                for fc in range(NFC):
                    mm(pa[:, :gn], wdn[dc % 2][:, fc, :], actT[:, fc, :gn], fc == 0, fc == NFC - 1,
                       [b_wdn[dc % 2], b_actT], pb)
                tt("dve", hT[:, dc, gs:gs + gn], hT[:, dc, gs:gs + gn], pa[:, :gn], ALU.add, [b_h[gi]] + pb, [b_h[gi]])
        A.release(m)

    def phase_merge(l, colp_t, b_colp):
        m = A.mark()
        yT = A.alloc([128, 32, 512], BF16)
        b_yT = Buf("yT")
        wbr = [A.alloc([128, 32, 128], BF16) for _ in range(2)]
        b_wbr = [Buf("wbr0"), Buf("wbr1")]
        wg = [A.alloc([128, 8, 384], BF16) for _ in range(2)]
        b_wg = [Buf("wg0"), Buf("wg1")]
        wo = A.alloc([128, 8, 1024], BF16)
        b_wo = Buf("wo")
        mT = A.alloc([128, 8, 512], BF16)
        b_mT = Buf("mT")
        sig = A.alloc([128, 512], F32)
        b_sig = Buf("sig")
        tmp = A.alloc([128, 512], F32)
        b_tmp = Buf("tmp")
        acc = A.alloc([128, 512], F32)
        b_acc = Buf("acc")
        load_cast(wo, w_out[l].rearrange("(a p) d -> p a d", p=128), [b_wo])
        wbd = w_branch[l].rearrange("(a p) d -> p a d", p=128)
        wi = w_in[l].rearrange("(a p) f -> p a f", p=128)
        syv = s_y.rearrange("(a p) t -> p a t", p=128)
        for gi, (gs, gn) in enumerate(groups):
            ybufs = [sy_buf(kc, gi) for kc in range(32)]
            dma("sp", yT[:, :, :gn], syv[:, :, gs:gs + gn], ybufs, [b_yT])
            for dc in range(8):
                load_cast(wbr[dc % 2], wbd[:, :, dc * 128:(dc + 1) * 128], [b_wbr[dc % 2]])
                for br in range(3):
                    c0 = OFF_G + br * 1024 + dc * 128
                    load_cast(wg[dc % 2][:, :, br * 128:(br + 1) * 128], wi[:, :, c0:c0 + 128], [b_wg[dc % 2]])
                for br, (k0, k1) in enumerate([(0, 8), (8, 24), (24, 32)]):
                    pa, pb = psum(1)
                    for kc in range(k0, k1):
                        mm(pa[:, :gn], wbr[dc % 2][:, kc, :], yT[:, kc, :gn], kc == k0, kc == k1 - 1,
                           [b_wbr[dc % 2], b_yT], pb)
                    pg, pgb = psum(1)
                    for kc in range(8):
                        mm(pg[:, :gn], wg[dc % 2][:, kc, br * 128:(br + 1) * 128],
                           uT[:, kc, SHIFT + gs:SHIFT + gs + gn], kc == 0, kc == 7, [b_wg[dc % 2], b_u], pgb)
                    gb = colp_t[:, CP_GB + br * 8 + dc:CP_GB + br * 8 + dc + 1]
                    act(sig[:, :gn], pg[:, :gn], AF.Sigmoid, pgb + [b_colp], [b_sig], bias=gb)
                    if br == 0:
                        tt("dve", acc[:, :gn], sig[:, :gn], pa[:, :gn], ALU.mult, [b_sig] + pb, [b_acc])
                    else:
                        tt("dve", tmp[:, :gn], sig[:, :gn], pa[:, :gn], ALU.mult, [b_sig] + pb, [b_tmp])
                        if br == 1:
                            tt("dve", acc[:, :gn], acc[:, :gn], tmp[:, :gn], ALU.add, [b_acc, b_tmp], [b_acc])
                        else:
                            tt("dve", mT[:, dc, :gn], acc[:, :gn], tmp[:, :gn], ALU.add, [b_acc, b_tmp], [b_mT])
            for dc2 in range(8):
                pa, pb = psum(1)
                for dc in range(8):
                    mm(pa[:, :gn], wo[:, dc, dc2 * 128:(dc2 + 1) * 128], mT[:, dc, :gn], dc == 0, dc == 7,
                       [b_wo, b_mT], pb)
                tt("dve", hT[:, dc2, gs:gs + gn], hT[:, dc2, gs:gs + gn], pa[:, :gn], ALU.add, [b_h[gi]] + pb, [b_h[gi]])
        A.release(m)


    sy_bufs = {}

    def sy_buf(kc, gi):
        if (kc, gi) not in sy_bufs:
            sy_bufs[(kc, gi)] = Buf("sy%d_%d" % (kc, gi))
        return sy_bufs[(kc, gi)]

    def phase_ret(l, colp_t, b_colp):
        m = A.mark()
        wq = A.alloc([128, 8, 512], BF16)
        wv = A.alloc([128, 8, 256], BF16)
        wgt = A.alloc([128, 8, 256], BF16)
        b_wq, b_wv, b_wgt = Buf("wq"), Buf("wv"), Buf("wgt")
        qT = A.alloc([128, TP], BF16)
        kT = A.alloc([128, TP], BF16)
        b_qT, b_kT = Buf("qT"), Buf("kT")
        vtok = A.alloc([128, NT, 256], BF16)
        b_vtok = Buf("vtok")
        gT = A.alloc([128, 2, TP], BF16)
        b_gT = Buf("gT")
        tabs = [A.alloc([128, 512], F32) for _ in range(4)]
        b_tabs = [Buf("tab%d" % i) for i in range(4)]
        dect = A.alloc([128, TP], F32)
        b_dect = Buf("dect")
        Pm = [A.alloc([128, 512], BF16) for _ in range(2)]
        b_Pm = [Buf("P0"), Buf("P1")]
        t1 = A.alloc([128, 512], F32)
        t2 = A.alloc([128, 512], F32)
        b_t1, b_t2 = Buf("t1"), Buf("t2")
        ysb = A.alloc([128, 512], F32)
        sq = A.alloc([128, 512], F32)
        msb = A.alloc([128, 512], F32)
        var = A.alloc([128, 512], F32)
        b_ysb, b_sq, b_msb, b_var = Buf("ysb"), Buf("sq"), Buf("msb"), Buf("var")
        yo = A.alloc([128, 512], BF16)
        b_yo = Buf("yo")
        wi = w_in[l].rearrange("(a p) f -> p a f", p=128)
        wqk = w_qk[l].rearrange("(a p) f -> p a f", p=128)
        wrt = w_rot[l].rearrange("(a p) f -> p a f", p=128)
        tabn = ["cosq", "sinq", "cosk", "sink"]
        pcount = 0
        for hp in range(4):
            load_cast(wq[:, :, 0:128], wqk[:, :, hp * 128:(hp + 1) * 128], [b_wq])
            load_cast(wq[:, :, 128:256], wrt[:, :, hp * 128:(hp + 1) * 128], [b_wq])
            load_cast(wq[:, :, 256:384], wqk[:, :, 512 + hp * 128:512 + (hp + 1) * 128], [b_wq])
            load_cast(wq[:, :, 384:512], wrt[:, :, 512 + hp * 128:512 + (hp + 1) * 128], [b_wq])
            load_cast(wv, wi[:, :, OFF_C + 1024 + hp * 256:OFF_C + 1024 + (hp + 1) * 256], [b_wv])
            load_cast(wgt, wi[:, :, OFF_C + 2048 + hp * 256:OFF_C + 2048 + (hp + 1) * 256], [b_wgt])
            for gi, (gs, gn) in enumerate(groups):
                for ti in range(4):
                    dma("sp", tabs[ti][:, :gn], cst[tabn[ti]][:, gs:gs + gn], (), [b_tabs[ti]])
                for qi in range(2):
                    pq, pqb = psum(1)
                    pr, prb = psum(1)
                    for kc in range(8):
                        mm(pq[:, :gn], wq[:, kc, qi * 256:qi * 256 + 128], uT[:, kc, SHIFT + gs:SHIFT + gs + gn],
                           kc == 0, kc == 7, [b_wq, b_u], pqb)
                    for kc in range(8):
                        mm(pr[:, :gn], wq[:, kc, qi * 256 + 128:qi * 256 + 256], uT[:, kc, SHIFT + gs:SHIFT + gs + gn],
                           kc == 0, kc == 7, [b_wq, b_u], prb)
                    tt("dve", t1[:, :gn], pq[:, :gn], tabs[qi * 2][:, :gn], ALU.mult, pqb + [b_tabs[qi * 2]], [b_t1])
                    tt("dve", t2[:, :gn], pr[:, :gn], tabs[qi * 2 + 1][:, :gn], ALU.mult, prb + [b_tabs[qi * 2 + 1]], [b_t2])
                    dst, dstb = (qT, b_qT) if qi == 0 else (kT, b_kT)
                    tt("pool", dst[:, gs:gs + gn], t1[:, :gn], t2[:, :gn], ALU.add, [b_t1, b_t2], [dstb])
                for hh in range(2):
                    pg, pgb = psum(1)
                    for kc in range(8):
                        mm(pg[:, :gn], wgt[:, kc, hh * 128:(hh + 1) * 128], uT[:, kc, SHIFT + gs:SHIFT + gs + gn],
                           kc == 0, kc == 7, [b_wgt, b_u], pgb)
                    act(gT[:, hh, gs:gs + gn], pg[:, :gn], AF.Silu, pgb, [b_gT])
            for i in range(NT):
                pv, pvb = psum(1)
                for kc in range(8):
                    mm(pv[:, :256], uT[:, kc, SHIFT + i * 128:SHIFT + (i + 1) * 128], wv[:, kc, :], kc == 0, kc == 7,
                       [b_wv, b_u], pvb)
                cp("act", vtok[:, i, :], pv[:, :256], pvb, [b_vtok])
            for hh in range(2):
                h = hp * 2 + hh
                base = hh * 64
                dma("sp", dect, cst["rdec"][h], (), [b_dect])
                for gi, (gs, gn) in enumerate(groups):
                    yps, ypb = psum_pin()
                    nst = (gs + gn) // 128
                    for si in range(nst):
                        s0 = si * 128
                        ls = max(gs, s0)
                        n = gs + gn - ls
                        sc, scb = psum(1)
                        mm(sc[:, :n], kT[base:base + 64, s0:s0 + 128], qT[base:base + 64, ls:ls + n], True, True,
                           [b_kT, b_qT], scb)
                        P = Pm[pcount % 2]
                        bP = b_Pm[pcount % 2]
                        pcount += 1
                        tt("dve", P[:, :n], sc[:, :n], dect[:, ls - s0:ls - s0 + n], ALU.mult, scb + [b_dect], [bP])
                        mm(yps[:, ls - gs:ls - gs + n], vtok[:, si, hh * 128:(hh + 1) * 128], P[:, :n], si == 0,
                           si == nst - 1, [b_vtok, bP], ypb)
                    cp("act", ysb[:, :gn], yps[:, :gn], ypb, [b_ysb])
                    act(sq[:, :gn], ysb[:, :gn], AF.Square, [b_ysb], [b_sq])
                    pm, pmb = psum(1)
                    mm(pm[:, :gn], ones_f, ysb[:, :gn], True, True, [b_ysb, b_const], pmb)
                    pv2, pv2b = psum(1)
                    mm(pv2[:, :gn], ones_f, sq[:, :gn], True, True, [b_sq, b_const], pv2b)
                    ts("dve", msb[:, :gn], pm[:, :gn], 1.0 / 128, None, ALU.mult, ALU.bypass, pmb, [b_msb])
                    tt("pool", sq[:, :gn], msb[:, :gn], msb[:, :gn], ALU.mult, [b_msb], [b_sq])
                    stt(var[:, :gn], pv2[:, :gn], 1.0 / 128, sq[:, :gn], ALU.mult, ALU.subtract, pv2b + [b_sq], [b_var])
                    ts("dve", var[:, :gn], var[:, :gn], 1.0, EPS, ALU.mult, ALU.add, [b_var], [b_var])
                    act(var[:, :gn], var[:, :gn], AF.Ln, [b_var], [b_var])
                    act(var[:, :gn], var[:, :gn], AF.Exp, [b_var], [b_var], scale=-0.5)
                    tt("pool", ysb[:, :gn], ysb[:, :gn], msb[:, :gn], ALU.subtract, [b_ysb, b_msb], [b_ysb])
                    tt("dve", ysb[:, :gn], ysb[:, :gn], var[:, :gn], ALU.mult, [b_ysb, b_var], [b_ysb])
                    tt("dve", yo[:, :gn], ysb[:, :gn], gT[:, hh, gs:gs + gn], ALU.mult, [b_ysb, b_gT], [b_yo])
                    dma("act", s_y[(24 + h) * 128:(25 + h) * 128, gs:gs + gn], yo[:, :gn], [b_yo], [sy_buf(24 + h, gi)])
        A.release(m)


    def phase_ssm(l, colp_t, b_colp):
        m0 = A.mark()
        BT = A.alloc([128, 4, TP], BF16)
        CT = A.alloc([128, 4, TP], BF16)
        b_BT, b_CT = Buf("BT"), Buf("CT")
        brow_t = A.alloc([1, 3072 + 32], F32)
        b_brow = Buf("brow")
        dma("sp", brow_t, brow[l], (), [b_brow])
        wi = w_in[l].rearrange("(a p) f -> p a f", p=128)
        smx_b = [Buf("smx%d" % i) for i in range(NT)]
        m1 = A.mark()
        CBW = 256
        stage = A.alloc([128, 8, CBW], F32)
        b_stage = Buf("stage")
        cwt = A.alloc([128, 4, CBW], F32)
        b_cwt = Buf("cwt")
        wt = A.alloc([128, 4, 8, CBW], BF16)
        b_wt = Buf("wt")
        ev = [A.alloc([128, 512], F32) for _ in range(2)]
        b_ev = [Buf("ev0"), Buf("ev1")]
        evc = 0
        wz = wt[:, 0, :, :]
        for cb in range(2048 // CBW):
            load_cast(wz, wi[:, :, OFF_B + cb * CBW:OFF_B + (cb + 1) * CBW], [b_wt])
            for i in range(NT):
                pa, pb = psum(1)
                for kc in range(8):
                    mm(pa[:, :CBW], uT[:, kc, SHIFT + i * 128:SHIFT + (i + 1) * 128], wz[:, kc, :], kc == 0, kc == 7,
                       [b_wt, b_u], pb)
                e = ev[evc % 2]
                be = b_ev[evc % 2]
                evc += 1
                act(e[:, :CBW], pa[:, :CBW], AF.Silu, pb, [be])
                dma("act", s_mx[i * 128:(i + 1) * 128, cb * CBW:(cb + 1) * CBW], e[:, :CBW], [be], [smx_b[i]])
        for cb in range(3072 // CBW):
            c0 = cb * CBW
            dma("sp", stage, wi[:, :, OFF_B + 2048 + c0:OFF_B + 2048 + c0 + CBW], (), [b_stage])
            for k in range(4):
                dma("sp", cwt[:, k, :], cw_all[l][:, k * 3072 + c0:k * 3072 + c0 + CBW], (), [b_cwt])
            for k in range(4):
                tt("pool" if k % 2 else "dve", wt[:, k, :, :], stage,
                   cwt[:, k, :].unsqueeze(1).to_broadcast([128, 8, CBW]), ALU.mult, [b_stage, b_cwt], [b_wt])
            if c0 < 2560:
                for i in range(NT):
                    pa, pb = psum(1)
                    for k in range(4):
                        for kc in range(8):
                            mm(pa[:, :CBW], uT[:, kc, i * 128 + k:(i + 1) * 128 + k], wt[:, k, kc, :],
                               k == 0 and kc == 0, False, [b_wt, b_u], pb)
                    mm(pa[:, :CBW], ones_f[0:1, :], brow_t[0:1, c0:c0 + CBW], False, True, [b_const, b_brow], pb)
                    e = ev[evc % 2]
                    be = b_ev[evc % 2]
                    evc += 1
                    act(e[:, :CBW], pa[:, :CBW], AF.Silu, pb, [be])
                    dma("act", s_mx[i * 128:(i + 1) * 128, 2048 + c0:2048 + c0 + CBW], e[:, :CBW], [be], [smx_b[i]])
            if c0 >= 2048:
                for sub in range(CBW // 128):
                    cc0 = c0 + sub * 128 - 2048
                    isC = cc0 >= 512
                    g = (cc0 % 512) // 128
                    dstT, dstb = (CT, b_CT) if isC else (BT, b_BT)
                    for gi, (gs, gn) in enumerate(groups):
                        pa, pb = psum(1)
                        for k in range(4):
                            for kc in range(8):
                                mm(pa[:, :gn], wt[:, k, kc, sub * 128:(sub + 1) * 128], uT[:, kc, gs + k:gs + k + gn],
                                   k == 0 and kc == 0, k == 3 and kc == 7, [b_wt, b_u], pb)
                        bc = colp_t[:, CP_BCB + cc0 // 128:CP_BCB + cc0 // 128 + 1]
                        act(dstT[:, g, gs:gs + gn], pa[:, :gn], AF.Silu, pb + [b_colp], [dstb], bias=bc)
        wdt = wt[:, 0, :, 0:32]
        load_cast(wdt, wi[:, :, OFF_B + 5120:OFF_B + 5152], [b_wt])
        for i in range(NT):
            pa, pb = psum(1)
            for kc in range(8):
                mm(pa[:, :32], uT[:, kc, SHIFT + i * 128:SHIFT + (i + 1) * 128], wdt[:, kc, :], kc == 0, False,
                   [b_wt, b_u], pb)
            mm(pa[:, :32], ones_f[0:1, :], brow_t[0:1, 3072:3104], False, True, [b_const, b_brow], pb)
            e = ev[evc % 2]
            be = b_ev[evc % 2]
            evc += 1
            act(e[:, :32], pa[:, :32], AF.Exp, pb, [be])
            ts("dve", e[:, :32], e[:, :32], 1.0, 1.0, ALU.mult, ALU.add, [be], [be])
            act(e[:, :32], e[:, :32], AF.Ln, [be], [be])
            dma("act", s_mx[i * 128:(i + 1) * 128, 4608:4640], e[:, :32], [be], [smx_b[i]])
        R.barrier()
        A.release(m1)
        hv = hT.rearrange("p a t -> p (a t)")
        dma("act", s_hT, hv, b_h, [Buf("s_hT")])
        R.barrier()
        A2 = A.sub(0, h_bytes) if h_bytes >= 40000 else A
        b_S32s, b_Sbfs, b_ybfs = [Buf("S32a"), Buf("S32b")], [Buf("Sbfa"), Buf("Sbfb")], [Buf("ybfa"), Buf("ybfb")]
        rb = A.alloc([128, NROWB], F32)
        b_rb = Buf("rb")
        dma("sp", rb, rowb[l], (), [b_rb])
        ng_bc = rb[:, 0:2048]
        D_bc = rb[:, 2048:2080]
        A_bc = rb[:, 2080:2112]
        act(A_bc, A_bc, AF.Exp, [b_rb], [b_rb])
        ts("dve", A_bc, A_bc, -1.0, None, ALU.mult, ALU.bypass, [b_rb], [b_rb])
        tri = A.alloc([128, 128], F32)
        sfx = A.alloc([128, 128], F32)
        negU = A.alloc([128, 512], F32)
        caus = A.alloc([128, 128], F32)
        b_k = Buf("ssmconst")
        dma("sp", tri, cst["tri128"], (), [b_k])
        dma("sp", sfx, cst["sfx128"], (), [b_k])
        dma("sp", negU, cst["negU"], (), [b_k])
        dma("sp", caus, cst["causal"], (), [b_k])
        S32 = A.alloc([128, 4, 512], F32)
        Sbf = A.alloc([128, 4, 512], BF16)
        b_S32, b_Sbf = Buf("S32"), Buf("Sbf")
        memset("pool", S32, 0.0, b_S32s)
        memset("pool", Sbf, 0.0, b_Sbfs)
        sm = A.alloc([128, 8, 32], F32)
        b_sm = Buf("sm")
        dtt, aa, acs, nacs, eacs, eend, cdb, dte = [sm[:, j, :] for j in range(8)]
        def m2_bufs(AA):
            ss = AA.alloc([128, 8], F32)
            b_ss = Buf("ss")
            Rt = AA.alloc([128, 4, 128], F32)
            b_Rt = Buf("Rt")
            xs = AA.alloc([128, 512], F32)
            zs = AA.alloc([128, 512], F32)
            Bt = AA.alloc([128, 128], F32)
            Btb = AA.alloc([128, 128], BF16)
            b_xs, b_zs, b_Bt, b_Btb = Buf("xs"), Buf("zs"), Buf("Bt"), Buf("Btb")
            xdt = AA.alloc([128, 8, 64], BF16)
            xde = AA.alloc([128, 8, 64], BF16)
            b_xdt, b_xde = Buf("xdt"), Buf("xde")
            CBm = AA.alloc([128, 128], F32)
            b_CBm = Buf("CBm")
            E = AA.alloc([128, 4, 128], F32)
            MT = AA.alloc([128, 4, 128], BF16)
            b_E, b_MT = Buf("E"), Buf("MT")
            yt = AA.alloc([128, 8, 64], F32)
            t3 = AA.alloc([128, 8, 64], F32)
            junk = AA.alloc([128, 512], F32)
            b_yt, b_t3, b_junk = Buf("yt"), Buf("t3"), Buf("junk")
            return dict(locals())
        SB = [m2_bufs(A), m2_bufs(A2)]
        ybf = A.alloc([128, 2048], BF16)
        b_ybf = Buf("ybf")
        ybT = A.alloc([128, 16, 128], BF16)
        b_ybT = Buf("ybT")
        for i in range(NT):
            tc0 = i * 128
            dma("sp", dtt, s_mx[tc0:tc0 + 128, 4608:4640], [smx_b[i]], [b_sm])
            tt("dve", aa, dtt, A_bc, ALU.mult, [b_sm, b_rb], [b_sm])
            p1, p1b = psum(1)
            mm(p1[:, 0:32], tri, aa, True, True, [b_k, b_sm], p1b)
            mm(p1[:, 32:64], sfx, aa, True, True, [b_k, b_sm], p1b)
            mm(p1[:, 64:96], ones_f, aa, True, True, [b_const, b_sm], p1b)
            cp("dve", acs, p1[:, 0:32], p1b, [b_sm])
            ts("dve", nacs, p1[:, 0:32], -1.0, None, ALU.mult, ALU.bypass, p1b, [b_sm])
            act(eacs, p1[:, 0:32], AF.Exp, p1b, [b_sm])
            act(eend, p1[:, 32:64], AF.Exp, p1b, [b_sm])
            act(cdb, p1[:, 64:96], AF.Exp, p1b, [b_sm])
            tt("dve", dte, dtt, eend, ALU.mult, [b_sm], [b_sm])
            def m2_group(g, sidx):
                L_ = SB[sidx]
                ss = L_["ss"]
                b_ss = L_["b_ss"]
                Rt = L_["Rt"]
                b_Rt = L_["b_Rt"]
                xs = L_["xs"]
                zs = L_["zs"]
                Bt = L_["Bt"]
                Btb = L_["Btb"]
                b_xs = L_["b_xs"]
                b_zs = L_["b_zs"]
                b_Bt = L_["b_Bt"]
                b_Btb = L_["b_Btb"]
                xdt = L_["xdt"]
                xde = L_["xde"]
                b_xdt = L_["b_xdt"]
                b_xde = L_["b_xde"]
                CBm = L_["CBm"]
                b_CBm = L_["b_CBm"]
                E = L_["E"]
                MT = L_["MT"]
                b_E = L_["b_E"]
                b_MT = L_["b_MT"]
                yt = L_["yt"]
                t3 = L_["t3"]
                junk = L_["junk"]
                b_yt = L_["b_yt"]
                b_t3 = L_["b_t3"]
                b_junk = L_["b_junk"]
                dma("sp", xs, s_mx[tc0:tc0 + 128, 2048 + g * 512:2048 + (g + 1) * 512], [smx_b[i]], [b_xs])
                dma("sp", zs, s_mx[tc0:tc0 + 128, g * 512:(g + 1) * 512], [smx_b[i]], [b_zs])
                dma("sp", Bt, s_mx[tc0:tc0 + 128, 4096 + g * 128:4096 + (g + 1) * 128], [smx_b[i]], [b_Bt])
                xs3 = xs.rearrange("p (r q) -> p r q", q=64)
                tt("dve", xdt, xs3, dtt[:, g * 8:(g + 1) * 8].unsqueeze(2).to_broadcast([128, 8, 64]), ALU.mult,
                   [b_xs, b_sm], [b_xdt])
                tt("pool", xde, xs3, dte[:, g * 8:(g + 1) * 8].unsqueeze(2).to_broadcast([128, 8, 64]), ALU.mult,
                   [b_xs, b_sm], [b_xde])
                cp("pool", Btb, Bt, [b_Bt], [b_Btb])
                pcb, pcbb = psum(1)
                mm(pcb[:, :128], BT[:, g, tc0:tc0 + 128], CT[:, g, tc0:tc0 + 128], True, True, [b_BT, b_CT], pcbb)
                tt("dve", CBm, pcb[:, :128], caus, ALU.mult, pcbb + [b_k], [b_CBm])
                yd, ydb = psum_pin(sidx)
                for blk in range(2):
                    r0 = g * 8 + blk * 4
                    tt("pool", Rt, tri.unsqueeze(1).to_broadcast([128, 4, 128]),
                       aa[:, r0:r0 + 4].unsqueeze(2).to_broadcast([128, 4, 128]), ALU.mult, [b_k, b_sm], [b_Rt])
                    pe_, peb = psum(1)
                    mm(pe_[:, :512], ones_f, Rt.rearrange("p a b -> p (a b)"), True, False, [b_const, b_Rt], peb)
                    mm(pe_[:, :512], ident_f, negU, False, True, [b_const, b_k], peb)
                    for r in range(4):
                        act(E[:, r, :], pe_[:, r * 128:(r + 1) * 128], AF.Exp, peb + [b_sm], [b_E],
                            bias=nacs[:, r0 + r:r0 + r + 1])
                    tt("dve", MT, E, CBm.unsqueeze(1).to_broadcast([128, 4, 128]), ALU.mult, [b_E, b_CBm], [b_MT])
                    for r in range(4):
                        hh = blk * 4 + r
                        mm(yd[:, hh * 64:(hh + 1) * 64], MT[:, r, :], xdt[:, hh, :], True, True, [b_MT, b_xdt], ydb)
                po, pob = psum(1)
                mm(po[:, :512], CT[:, g, tc0:tc0 + 128], Sbf[:, g, :], True, True, [b_CT, b_Sbfs[sidx]], pob)
                tt("dve", yt, po[:, :512].rearrange("p (r q) -> p r q", q=64),
                   eacs[:, g * 8:(g + 1) * 8].unsqueeze(2).to_broadcast([128, 8, 64]), ALU.mult, pob + [b_sm], [b_yt])
                yt2 = yt.rearrange("p r q -> p (r q)")
                tt("dve", yt2, yt2, yd[:, :512], ALU.add, [b_yt] + ydb, [b_yt])
                tt("pool", t3, xs3, D_bc[:, g * 8:(g + 1) * 8].unsqueeze(2).to_broadcast([128, 8, 64]), ALU.mult,
                   [b_xs, b_rb], [b_t3])
                tt("pool", yt, yt, t3, ALU.add, [b_yt, b_t3], [b_yt])
                tt("dve", yt2, yt2, zs, ALU.mult, [b_yt, b_zs], [b_yt])
                act(junk, yt2, AF.Square, [b_yt], [b_junk])
                R.op("dve", lambda e, g=g: e.tensor_reduce(ss[:, g:g + 1], junk, AX.X, ALU.add), [b_junk], [b_ss])
                ts("dve", ss[:, g:g + 1], ss[:, g:g + 1], 1.0 / 512, EPS, ALU.mult, ALU.add, [b_ss], [b_ss])
                act(ss[:, g:g + 1], ss[:, g:g + 1], AF.Ln, [b_ss], [b_ss])
                act(ss[:, g:g + 1], ss[:, g:g + 1], AF.Exp, [b_ss], [b_ss], scale=-0.5)
                stt(ybf[:, g * 512:(g + 1) * 512], yt2, ss[:, g:g + 1], ng_bc[:, g * 512:(g + 1) * 512], ALU.mult,
                    ALU.mult, [b_yt, b_ss, b_rb], [b_ybfs[sidx]])
                pst, pstb = psum(1)
                mm(pst[:, :512], Btb, xde.rearrange("p r q -> p (r q)"), True, True, [b_Btb, b_xde], pstb)
                S3 = S32[:, g, :].rearrange("p (r q) -> p r q", q=64)
                tt("dve", S3, S3, cdb[:, g * 8:(g + 1) * 8].unsqueeze(2).to_broadcast([128, 8, 64]), ALU.mult,
                   [b_S32s[sidx], b_sm], [b_S32s[sidx]])
                tt("dve", S32[:, g, :], S32[:, g, :], pst[:, :512], ALU.add, [b_S32s[sidx]] + pstb, [b_S32s[sidx]])
                cp("act", Sbf[:, g, :], S32[:, g, :], [b_S32s[sidx]], [b_Sbfs[sidx]])
            n0_ = len(R.ops)
            ps_lim[0], ps_lim[1] = 0, 3
            m2_group(0, 0)
            m2_group(1, 0)
            n1_ = len(R.ops)
            ps_lim[0], ps_lim[1] = 3, 6
            m2_group(2, 1)
            m2_group(3, 1)
            n2_ = len(R.ops)
            ps_lim[0], ps_lim[1] = 0, 6
            R.interleave(n0_, n1_, n2_)
            for half in range(2):
                pt, ptb = psum(1)
                ptv = pt.bitcast(BF16)
                for c in range(8):
                    cc = half * 8 + c
                    tr(ptv[:, c * 128:(c + 1) * 128], ybf[:, cc * 128:(cc + 1) * 128], ident_b, b_ybfs + [b_const], ptb)
                cp("act" if half else "dve", ybT[:, half * 8:(half + 1) * 8, :].rearrange("p a b -> p (a b)"),
                   ptv[:, :1024], ptb, [b_ybT])
            gi = [k for k, (gs, gn) in enumerate(groups) if gs <= tc0 < gs + gn][0]
            dma("act", s_y[1024:3072, tc0:tc0 + 128].rearrange("(a p) t -> p a t", p=128), ybT, [b_ybT],
                [sy_buf(kc, gi) for kc in range(8, 24)])
        R.barrier()
        dma("sp", hv, s_hT, (), b_h)
        R.barrier()
        A.release(m0)


    def phase_rwkv(l, colp_t, b_colp):
        m0 = A.mark()
        wi = w_in[l].rearrange("(a p) f -> p a f", p=128)
        rkv_b = [Buf("rkv%d" % i) for i in range(NT)]
        lo_b = [Buf("lo%d" % g) for g in range(len(groups))]
        CBW = 256
        stage = A.alloc([128, 8, CBW], F32)
        tmpw = A.alloc([128, 8, CBW], F32)
        mut = A.alloc([128, CBW], F32)
        wt = A.alloc([128, 2, 8, CBW], BF16)
        b_stage, b_tmpw, b_mut, b_wt = Buf("stage"), Buf("tmpw"), Buf("mut"), Buf("wt")
        ev = [A.alloc([128, 512], F32) for _ in range(2)]
        b_ev = [Buf("ev0"), Buf("ev1")]
        lo_sb = A.alloc([128, 512], BF16)
        b_lo_sb = Buf("lo_sb")
        evc = 0

        def make_w(src_ap, mu_ap, n):
            dma("sp", stage[:, :, :n], src_ap, (), [b_stage])
            dma("sp", mut[:, :n], mu_ap, (), [b_mut])
            tt("dve", tmpw[:, :, :n], stage[:, :, :n], mut[:, :n].unsqueeze(1).to_broadcast([128, 8, n]), ALU.mult,
               [b_stage, b_mut], [b_tmpw])
            tt("pool", wt[:, 0, :, :n], stage[:, :, :n], tmpw[:, :, :n], ALU.subtract, [b_stage, b_tmpw], [b_wt])
            cp("act", wt[:, 1, :, :n], tmpw[:, :, :n], [b_tmpw], [b_wt])

        for cb in range(3072 // CBW):
            c0 = cb * CBW
            make_w(wi[:, :, c0:c0 + CBW], mu_all[l][:, c0:c0 + CBW], CBW)
            for i in range(NT):
                pa, pb = psum(1)
                for kc in range(8):
                    mm(pa[:, :CBW], uT[:, kc, SHIFT + i * 128:SHIFT + (i + 1) * 128], wt[:, 0, kc, :], kc == 0, False,
                       [b_wt, b_u], pb)
                for kc in range(8):
                    mm(pa[:, :CBW], uT[:, kc, SHIFT - 1 + i * 128:SHIFT - 1 + (i + 1) * 128], wt[:, 1, kc, :], False,
                       kc == 7, [b_wt, b_u], pb)
                e = ev[evc % 2]
                be = b_ev[evc % 2]
                evc += 1
                cp("act" if evc % 2 else "dve", e[:, :CBW], pa[:, :CBW], pb, [be])
                dma("act", s_rkv[i * 128:(i + 1) * 128, c0:c0 + CBW], e[:, :CBW], [be], [rkv_b[i]])
                if l == 0 and c0 >= 2048:
                    dma("act", s_vf[i * 128:(i + 1) * 128, c0 - 2048:c0 - 2048 + CBW], e[:, :CBW], [be], [b_vf[i]])
        blocks = [("A", wi[:, :, 3072:3200], mu_all[l][:, 3072:3200], 128),
                  ("G", wi[:, :, 3200:3328], mu_all[l][:, 3200:3328], 128)]
        if l > 0:
            blocks.append(("V", w_vres[l - 1].rearrange("(a p) f -> p a f", p=128), mu_all[l][:, 3328:3360], 32))
        for bi, (nm, wsrc, musrc, n) in enumerate(blocks):
            make_w(wsrc, musrc, n)
            for gi, (gs, gn) in enumerate(groups):
                pa, pb = psum(1)
                for kc in range(8):
                    mm(pa[:n, :gn], wt[:, 0, kc, :n], uT[:, kc, SHIFT + gs:SHIFT + gs + gn], kc == 0, False,
                       [b_wt, b_u], pb)
                for kc in range(8):
                    mm(pa[:n, :gn], wt[:, 1, kc, :n], uT[:, kc, SHIFT - 1 + gs:SHIFT - 1 + gs + gn], False, kc == 7,
                       [b_wt, b_u], pb)
                if nm == "A":
                    act(lo_sb[0:64, :gn], pa[0:64, :gn], AF.Tanh, pb, [b_lo_sb])
                    cp("act", lo_sb[64:128, :gn], pa[64:128, :gn], pb, [b_lo_sb])
                elif nm == "G":
                    act(lo_sb[:, :gn], pa[:, :gn], AF.Sigmoid, pb, [b_lo_sb])
                else:
                    cp("act", lo_sb[0:32, :gn], pa[0:32, :gn], pb, [b_lo_sb])
                dma("act", s_lo[bi * 128:bi * 128 + n, gs:gs + gn], lo_sb[:n, :gn], [b_lo_sb], [lo_b[gi]])
        R.barrier()
        A.release(m0)
        hv = hT.rearrange("p a t -> p (a t)")
        uv = uT.rearrange("p a t -> p (a t)")
        dma("act", s_hT, hv, b_h, [Buf("s_hT")])
        dma("act", s_uT, uv, [b_u], [Buf("s_uT")])
        R.barrier()
        A2 = A.sub(0, hu_bytes) if hu_bytes >= 100000 else A
        g2t = A2.alloc([128, 1024], BF16)
        b_w2 = Buf("w2")
        wa2z = A2.alloc([128, 2, 1024], BF16)
        b_wa2z = Buf("wa2z")
        load_cast(g2t, rg2[l], [b_w2])
        memset("pool", wa2z, 0.0, [b_wa2z])
        load_cast(wa2z[0:64, 0, :], rw2[l][0:64, :], [b_wa2z])
        load_cast(wa2z[64:128, 1, :], rw2[l][64:128, :], [b_wa2z])
        v2z = A2.alloc([128, 1024], BF16)
        b_v2z = Buf("v2z")
        memset("pool", v2z, 0.0, [b_v2z])
        if l > 0:
            load_cast(v2z[0:32, :], rv2[l - 1], [b_v2z])
        kst = A2.alloc([128, 128 * 3 + 256], F32)
        b_kst = Buf("kst")
        tri, sfxm, nmask2 = kst[:, 0:128], kst[:, 128:256], kst[:, 256:384]
        mmask2 = kst[:, 384:640]
        dma("sp", tri, cst["tri64"], (), [b_kst])
        dma("sp", sfxm, cst["sfx64"], (), [b_kst])
        dma("sp", nmask2, cst["nmask2"], (), [b_kst])
        dma("sp", mmask2, cst["mmask2"], (), [b_kst])

        def r2_stream(hf, AA):
            ra = AA.alloc([128, 8, 512], F32)
            b_ra = Buf("ra")
            names = ["r", "k", "v", "vf", "s", "a", "kk", "kp", "b", "g", "e1", "e2", "e3", "e4", "t1", "t2"]
            alias = {"e4": "e1", "vf": "e2", "e3": "a"}
            T = {n: AA.alloc([128, 512], F32) for n in names if n not in alias}
            Bf = {n: Buf("T_" + n) for n in names if n not in alias}
            for n_, o_ in alias.items():
                T[n_] = T[o_]
                Bf[n_] = Bf[o_]
            T0 = {n: T[n] for n in ("r", "k", "v")}
            B0 = {n: Bf[n] for n in ("r", "k", "v")}
            T1 = {n: AA.alloc([128, 512], F32) for n in ("r", "k", "v")}
            B1 = {n: Buf("T1_" + n) for n in ("r", "k", "v")}
            bnames = ["at", "rt", "bt", "kt", "Vt", "Bz0", "Bz1", "Kz0", "Kz1"]
            TB = {n: AA.alloc([128, 512], BF16) for n in bnames}
            BB = {n: Buf("TB_" + n) for n in bnames}
            ARTz = [AA.alloc([128, 4, 2, 128], BF16) for _ in range(2)]
            b_ARTz = Buf("ARTz")
            BKT = AA.alloc([128, 4, 2, 128], BF16)
            b_BKT = Buf("BKT")
            Mb = AA.alloc([128, 8, 2, 128], BF16)
            Mk = AA.alloc([128, 8, 2, 128], BF16)
            b_Mb, b_Mk = Buf("Mb"), Buf("Mk")
            Nn = [AA.alloc([128, 8, 128], BF16) for _ in range(2)]
            NTt = [AA.alloc([128, 8, 128], BF16) for _ in range(2)]
            Pp = AA.alloc([128, 8, 128], BF16)
            b_Nn = [Buf("N0"), Buf("N1")]
            b_NTt = [Buf("NT0"), Buf("NT1")]
            b_Pp = Buf("Pp")
            Wsb = AA.alloc([128, 8, 64], BF16)
            Usb = AA.alloc([128, 8, 64], BF16)
            b_Wsb, b_Usb = Buf("Wsb"), Buf("Usb")
            S32 = AA.alloc([128, 4, 64], F32)
            Sbf = AA.alloc([128, 4, 64], BF16)
            b_S32, b_Sbf = Buf("S32"), Buf("Sbf")
            Ysb = AA.alloc([128, 8, 64], F32)
            b_Ysb = Buf("Ysb")
            ya = AA.alloc([128, 512], BF16)
            yaT = AA.alloc([128, 4, 128], BF16)
            b_ya, b_yaT = Buf("ya"), Buf("yaT")
            lo_t = AA.alloc([128, 3, 128], BF16)
            b_lo_t = Buf("lo_t")
            st = AA.alloc([128, 8, 8], F32)
            b_st = Buf("st")
            ssq, bon, s1, s2, mean, varr = [st[:, j, :] for j in range(6)]
            gC = AA.alloc([128, 8], F32)
            b_gC = Buf("gC")
            memset("pool", ARTz[0], 0.0, [b_ARTz])
            memset("pool", ARTz[1], 0.0, [b_ARTz])
            memset("pool", Wsb, 0.0, [b_Wsb])
            memset("pool", Usb, 0.0, [b_Usb])
            memset("pool", lo_t, 0.0, [b_lo_t])
            slo = s_lo.rearrange("(a p) t -> p a t", p=128)
            rav = rowa[l].rearrange("p (j f) -> p j f", f=1024)
            ind = [tri[:, 63:64], tri[:, 127:128]]

            def bc8(ap8):
                return ap8.unsqueeze(2).to_broadcast([128, 8, 64])

            def v3(ap):
                return ap.rearrange("p (h q) -> p h q", q=64)

            f0 = hf * 512
            dma("sp", ra, rav[:, 0:8, f0:f0 + 512], (), [b_ra])
            w0b, a0b, kkb, kab, rkb, lgb, lbb, v0b = [ra[:, j, :] for j in range(8)]
            memset("pool", S32, 0.0, [b_S32])
            memset("pool", Sbf, 0.0, [b_Sbf])
            for i in range(NT):
                tc0 = i * 128
                gi = [k for k, (gs, gn) in enumerate(groups) if gs <= tc0 < gs + gn][0]
                for n_ in ("r", "k", "v"):
                    T[n_] = (T1 if i % 2 else T0)[n_]
                    Bf[n_] = (B1 if i % 2 else B0)[n_]
                dma("sp", T["r"], s_rkv[tc0:tc0 + 128, f0:f0 + 512], [rkv_b[i]], [Bf["r"]])
                dma("sp", T["k"], s_rkv[tc0:tc0 + 128, 1024 + f0:1024 + f0 + 512], [rkv_b[i]], [Bf["k"]])
                dma("sp", T["v"], s_rkv[tc0:tc0 + 128, 2048 + f0:2048 + f0 + 512], [rkv_b[i]], [Bf["v"]])
                nlo = 3 if l > 0 else 2
                for j in range(nlo):
                    npart = 128 if j < 2 else 32
                    dma("sp", lo_t[0:npart, j, :], slo[0:npart, j, tc0:tc0 + 128], [lo_b[gi]], [b_lo_t])
                pw, pwb = psum(1)
                mm(pw[:, :512], lo_t[:, 0, :], wa2z[:, 0, f0:f0 + 512], True, True, [b_lo_t, b_wa2z], pwb)
                pa_, pab = psum(1)
                mm(pa_[:, :512], lo_t[:, 0, :], wa2z[:, 1, f0:f0 + 512], True, True, [b_lo_t, b_wa2z], pab)
                pg, pgb = psum(1)
                mm(pg[:, :512], lo_t[:, 1, :], g2t[:, f0:f0 + 512], True, True, [b_lo_t, b_w2], pgb)
                tt("dve", T["t1"], pw[:, :512], w0b, ALU.add, pwb + [b_ra], [Bf["t1"]])
                act(T["s"], T["t1"], AF.Sigmoid, [Bf["t1"]], [Bf["s"]])
                tt("dve", T["t2"], pa_[:, :512], a0b, ALU.add, pab + [b_ra], [Bf["t2"]])
                act(T["a"], T["t2"], AF.Sigmoid, [Bf["t2"]], [Bf["a"]])
                cp("act", T["g"], pg[:, :512], pgb, [Bf["g"]])
                if l > 0:
                    dma("sp", T["vf"], s_vf[tc0:tc0 + 128, f0:f0 + 512], [b_vf[i]], [Bf["vf"]])
                    pvr, pvrb = psum(1)
                    mm(pvr[:, :512], lo_t[:, 2, :], v2z[:, f0:f0 + 512], True, True, [b_lo_t, b_v2z], pvrb)
                    tt("dve", T["t1"], pvr[:, :512], v0b, ALU.add, pvrb + [b_ra], [Bf["t1"]])
                    act(T["t1"], T["t1"], AF.Sigmoid, [Bf["t1"]], [Bf["t1"]])
                    tt("pool", T["t2"], T["vf"], T["v"], ALU.subtract, [Bf["vf"], Bf["v"]], [Bf["t2"]])
                    tt("dve", T["t2"], T["t2"], T["t1"], ALU.mult, [Bf["t2"], Bf["t1"]], [Bf["t2"]])
                    tt("pool", T["v"], T["v"], T["t2"], ALU.add, [Bf["v"], Bf["t2"]], [Bf["v"]])
                tt("pool", T["kk"], T["k"], kkb, ALU.mult, [Bf["k"], b_ra], [Bf["kk"]])
                act(T["t1"], T["kk"], AF.Square, [Bf["kk"]], [Bf["t1"]])
                R.op("dve", lambda e: e.tensor_reduce(ssq, v3(T["t1"]), AX.X, ALU.add), [Bf["t1"]], [b_st])
                ts("dve", ssq, ssq, 1e-24, None, ALU.max, ALU.bypass, [b_st], [b_st])
                act(ssq, ssq, AF.Ln, [b_st], [b_st])
                act(ssq, ssq, AF.Exp, [b_st], [b_st], scale=-0.5)
                tt("dve", v3(T["kk"]), v3(T["kk"]), bc8(ssq), ALU.mult, [Bf["kk"], b_st], [Bf["kk"]])
                stt(T["t1"], T["a"], -1.0, kab, ALU.add, ALU.mult, [Bf["a"], b_ra], [Bf["t1"]])
                stt(T["kp"], T["t1"], 1.0, T["k"], ALU.add, ALU.mult, [Bf["t1"], Bf["k"]], [Bf["kp"]])
                tt("pool", T["b"], T["kk"], T["a"], ALU.mult, [Bf["kk"], Bf["a"]], [Bf["b"]])
                tt("pool", T["t2"], T["r"], rkb, ALU.mult, [Bf["r"], b_ra], [Bf["t2"]])
                tt("dve", T["t2"], T["t2"], T["kp"], ALU.mult, [Bf["t2"], Bf["kp"]], [Bf["t2"]])
                R.op("dve", lambda e: e.tensor_reduce(bon, v3(T["t2"]), AX.X, ALU.add), [Bf["t2"]], [b_st])
                pcs, pcsb = psum(1)
                mm(pcs[:, :512], tri, T["s"], True, True, [b_kst, Bf["s"]], pcsb)
                psf, psfb = psum(1)
                mm(psf[:, :512], sfxm, T["s"], True, True, [b_kst, Bf["s"]], psfb)
                pgc, pgcb = psum(1)
                for pr in range(4):
                    mm(pgc[:, pr * 2:pr * 2 + 2], T["s"][:, pr * 128:(pr + 1) * 128], tri[:, 63:128:64], True, True,
                       [Bf["s"], b_kst], pgcb)
                act(gC, pgc[:, 0:8], AF.Exp, pgcb, [b_gC], scale=-WDEC)
                act(T["e1"], pcs[:, :512], AF.Exp, pcsb, [Bf["e1"]], scale=-WDEC)
                tt("pool", TB["rt"], T["r"], T["e1"], ALU.mult, [Bf["r"], Bf["e1"]], [BB["rt"]])
                act(T["e2"], pcs[:, :512], AF.Exp, pcsb, [Bf["e2"]], scale=WDEC)
                tt("dve", T["t1"], pcs[:, :512], T["s"], ALU.subtract, pcsb + [Bf["s"]], [Bf["t1"]])
                act(T["e3"], T["t1"], AF.Exp, [Bf["t1"]], [Bf["e3"]], scale=-WDEC)
                act(T["e4"], psf[:, :512], AF.Exp, psfb, [Bf["e4"]], scale=-WDEC)
                stt(TB["at"], T["kk"], -1.0, T["e3"], ALU.mult, ALU.mult, [Bf["kk"], Bf["e3"]], [BB["at"]])
                tt("dve", TB["bt"], T["b"], T["e2"], ALU.mult, [Bf["b"], Bf["e2"]], [BB["bt"]])
                tt("pool", TB["kt"], T["kp"], T["e2"], ALU.mult, [Bf["kp"], Bf["e2"]], [BB["kt"]])
                tt("dve", T["t1"], T["b"], T["e4"], ALU.mult, [Bf["b"], Bf["e4"]], [Bf["t1"]])
                tt("pool", T["t2"], T["kp"], T["e4"], ALU.mult, [Bf["kp"], Bf["e4"]], [Bf["t2"]])
                for c in range(2):
                    ts("dve", TB["Bz%d" % c], T["t1"], ind[c], None, ALU.mult, ALU.bypass, [Bf["t1"], b_kst],
                       [BB["Bz%d" % c]])
                    ts("pool", TB["Kz%d" % c], T["t2"], ind[c], None, ALU.mult, ALU.bypass, [Bf["t2"], b_kst],
                       [BB["Kz%d" % c]])
                cp("act", TB["Vt"], T["v"], [Bf["v"]], [BB["Vt"]])
                pt, ptb = psum(1)
                ptv = pt.bitcast(BF16)
                for pr in range(4):
                    for q, nmq in enumerate(("at", "rt")):
                        tr(ptv[:, (pr * 2 + q) * 128:(pr * 2 + q + 1) * 128], TB[nmq][:, pr * 128:(pr + 1) * 128],
                           ident_b, [BB[nmq], b_const], ptb)
                cp("dve", ARTz[0][0:64].rearrange("p a b c -> p (a b c)"), ptv[0:64, :1024], ptb, [b_ARTz])
                cp("act", ARTz[1][64:128].rearrange("p a b c -> p (a b c)"), ptv[64:128, :1024], ptb, [b_ARTz])
                pt, ptb = psum(1)
                ptv = pt.bitcast(BF16)
                for pr in range(4):
                    for q, nmq in enumerate(("bt", "kt")):
                        tr(ptv[:, (pr * 2 + q) * 128:(pr * 2 + q + 1) * 128], TB[nmq][:, pr * 128:(pr + 1) * 128],
                           ident_b, [BB[nmq], b_const], ptb)
                cp("dve", BKT.rearrange("p a b c -> p (a b c)"), ptv[:, :1024], ptb, [b_BKT])
                mk4 = mmask2.rearrange("p (q t) -> p q t", t=128).unsqueeze(1).to_broadcast([128, 4, 2, 128])
                for hg in range(2):
                    pmb, pmbb = psum(2)
                    pmk, pmkb = psum(2)
                    for hl in range(4):
                        h = hg * 4 + hl
                        pr, hh = h // 2, h % 2
                        rhs = ARTz[hh][:, pr, :, :].rearrange("p q t -> p (q t)")
                        mm(pmb[:, hl * 256:(hl + 1) * 256], BKT[:, pr, 0, :], rhs, True, True, [b_BKT, b_ARTz], pmbb)
                        mm(pmk[:, hl * 256:(hl + 1) * 256], BKT[:, pr, 1, :], rhs, True, True, [b_BKT, b_ARTz], pmkb)
                    tt("dve", Mb[:, hg * 4:(hg + 1) * 4], pmb[:, :1024].rearrange("p (h q t) -> p h q t", q=2, t=128), mk4,
                       ALU.mult, pmbb + [b_kst], [b_Mb])
                    tt("dve", Mk[:, hg * 4:(hg + 1) * 4], pmk[:, :1024].rearrange("p (h q t) -> p h q t", q=2, t=128), mk4,
                       ALU.mult, pmkb + [b_kst], [b_Mk])
                pnt, pntb = psum(2)
                for h in range(8):
                    pr, hh = h // 2, h % 2
                    mm(pnt[:, h * 128:(h + 1) * 128], ARTz[hh][:, pr, 0, :], BKT[:, pr, 0, :], True, True,
                       [b_BKT, b_ARTz], pntb)
                tt("dve", NTt[0], pnt[:, :1024].rearrange("p (h t) -> p h t", t=128),
                   nmask2.unsqueeze(1).to_broadcast([128, 8, 128]), ALU.mult, pntb + [b_kst], [b_NTt[0]])
                cp("pool", Nn[0], Mb[:, :, 0, :], [b_Mb], [b_Nn[0]])
                tt("pool", Pp, Mb[:, :, 0, :], ident_b.unsqueeze(1).to_broadcast([128, 8, 128]), ALU.add,
                   [b_Mb, b_const], [b_Pp])
                cur = 0
                for lev in range(5):
                    nx = 1 - cur
                    pN, pNb = psum(2)
                    pNT, pNTb = psum(2)
                    for h in range(8):
                        hs2 = slice(h * 128, (h + 1) * 128)
                        mm(pN[:, hs2], NTt[cur][:, h, :], Nn[cur][:, h, :], True, True, [b_NTt[cur], b_Nn[cur]], pNb)
                        mm(pNT[:, hs2], Nn[cur][:, h, :], NTt[cur][:, h, :], True, True, [b_NTt[cur], b_Nn[cur]], pNTb)
                    cp("dve", Nn[nx].rearrange("p h t -> p (h t)"), pN[:, :1024], pNb, [b_Nn[nx]])
                    cp("act", NTt[nx].rearrange("p h t -> p (h t)"), pNT[:, :1024], pNTb, [b_NTt[nx]])
                    pP, pPb = psum(2)
                    for h in range(8):
                        hs2 = slice(h * 128, (h + 1) * 128)
                        mm(pP[:, hs2], NTt[nx][:, h, :], Pp[:, h, :], True, True, [b_NTt[nx], b_Pp], pPb)
                    tt("dve", Pp.rearrange("p h t -> p (h t)"), Pp.rearrange("p h t -> p (h t)"), pP[:, :1024],
                       ALU.add, [b_Pp] + pPb, [b_Pp])
                    cur = nx
                for c in range(2):
                    cs_ = slice(c * 64, (c + 1) * 64)
                    pW, pWb = psum(1)
                    for h in range(8):
                        pr, hh = h // 2, h % 2
                        hs = slice(h * 64, (h + 1) * 64)
                        mm(pW[:, hs], ARTz[hh][:, pr, 0, :], Sbf[:, pr, :], True, False, [b_ARTz, b_Sbf], pWb)
                        mm(pW[:, hs], Mk[:, h, 0, :], TB["Vt"][:, hs], False, True, [b_Mk, BB["Vt"]], pWb)
                    cp("dve", Wsb[cs_].rearrange("p h t -> p (h t)"), pW[cs_, :512], pWb, [b_Wsb])
                    pU, pUb = psum(1)
                    for h in range(8):
                        hs = slice(h * 64, (h + 1) * 64)
                        mm(pU[:, hs], Pp[:, h, :], Wsb[:, h, :], True, True, [b_Pp, b_Wsb], pUb)
                    cp("act", Usb[cs_].rearrange("p h t -> p (h t)"), pU[cs_, :512], pUb, [b_Usb])
                    pY, pYb = psum(1)
                    pS, pSb = psum(1)
                    for h in range(8):
                        pr, hh = h // 2, h % 2
                        hs = slice(h * 64, (h + 1) * 64)
                        mm(pY[:, hs], ARTz[hh][:, pr, 1, :], Sbf[:, pr, :], True, False, [b_ARTz, b_Sbf], pYb)
                        mm(pY[:, hs], Mb[:, h, 1, :], Usb[:, h, :], False, False, [b_Mb, b_Usb], pYb)
                        mm(pY[:, hs], Mk[:, h, 1, :], TB["Vt"][:, hs], False, True, [b_Mk, BB["Vt"]], pYb)
                        mm(pS[:, hs], TB["Bz%d" % c][:, pr * 128:(pr + 1) * 128], Usb[:, h, :], True, False,
                           [BB["Bz%d" % c], b_Usb], pSb)
                        mm(pS[:, hs], TB["Kz%d" % c][:, pr * 128:(pr + 1) * 128], TB["Vt"][:, hs], False, True,
                           [BB["Kz%d" % c], BB["Vt"]], pSb)
                    cp("act", Ysb[cs_].rearrange("p h t -> p (h t)"), pY[cs_, :512], pYb, [b_Ysb])
                    gcv = gC.rearrange("p (a c) -> p a c", c=2)[:, :, c:c + 1].to_broadcast([128, 4, 64])
                    tt("dve", S32, S32, gcv, ALU.mult, [b_S32, b_gC], [b_S32])
                    pS4 = pS[:, :512].rearrange("p (a hh v) -> p a hh v", hh=2, v=64)
                    for hh in range(2):
                        rs_ = slice(hh * 64, (hh + 1) * 64)
                        tt("dve", S32[rs_], S32[rs_], pS4[rs_, :, hh, :], ALU.add, [b_S32] + pSb, [b_S32])
                    cp("act", Sbf, S32, [b_S32], [b_Sbf])
                R.op("dve", lambda e: e.tensor_reduce(s1, Ysb, AX.X, ALU.add), [b_Ysb], [b_st])
                act(T["t1"], Ysb.rearrange("p h t -> p (h t)"), AF.Square, [b_Ysb], [Bf["t1"]])
                R.op("dve", lambda e: e.tensor_reduce(s2, v3(T["t1"]), AX.X, ALU.add), [Bf["t1"]], [b_st])
                ts("dve", mean, s1, 1.0 / 64, None, ALU.mult, ALU.bypass, [b_st], [b_st])
                tt("dve", varr, mean, mean, ALU.mult, [b_st], [b_st])
                stt(varr, s2, 1.0 / 64, varr, ALU.mult, ALU.subtract, [b_st], [b_st])
                ts("dve", varr, varr, 1.0, A_LN_EPS, ALU.mult, ALU.add, [b_st], [b_st])
                act(varr, varr, AF.Ln, [b_st], [b_st])
                act(varr, varr, AF.Exp, [b_st], [b_st], scale=-0.5)
                tt("dve", Ysb, Ysb, bc8(mean), ALU.subtract, [b_Ysb, b_st], [b_Ysb])
                tt("dve", Ysb, Ysb, bc8(varr), ALU.mult, [b_Ysb, b_st], [b_Ysb])
                y2 = Ysb.rearrange("p h t -> p (h t)")
                tt("pool", y2, y2, lgb, ALU.mult, [b_Ysb, b_ra], [b_Ysb])
                tt("pool", y2, y2, lbb, ALU.add, [b_Ysb, b_ra], [b_Ysb])
                tt("dve", v3(T["t2"]), v3(T["v"]), bc8(bon), ALU.mult, [Bf["v"], b_st], [Bf["t2"]])
                tt("pool", y2, y2, T["t2"], ALU.add, [b_Ysb, Bf["t2"]], [b_Ysb])
                tt("dve", ya, y2, T["g"], ALU.mult, [b_Ysb, Bf["g"]], [b_ya])
                pt, ptb = psum(1)
                ptv = pt.bitcast(BF16)
                for pr in range(4):
                    tr(ptv[:, pr * 128:(pr + 1) * 128], ya[:, pr * 128:(pr + 1) * 128], ident_b, [b_ya, b_const], ptb)
                cp("act", yaT.rearrange("p a t -> p (a t)"), ptv[:, :512], ptb, [b_yaT])
                dma("act", s_y[f0:f0 + 512, tc0:tc0 + 128].rearrange("(a p) t -> p a t", p=128), yaT, [b_yaT],
                    [sy_buf(hf * 4 + kc, gi) for kc in range(4)])

        n0 = len(R.ops)
        ps_lim[0], ps_lim[1] = 0, 4
        r2_stream(0, A)
        n1 = len(R.ops)
        ps_lim[0], ps_lim[1] = 4, 8
        r2_stream(1, A2)
        n2 = len(R.ops)
        ps_lim[0], ps_lim[1] = 0, 6
        R.interleave(n0, n1, n2)
        R.barrier()
        dma("sp", hv, s_hT, (), b_h)
        dma("sp", uv, s_uT, (), [b_u])
        R.barrier()
        A.release(m0)

    PHASES_PLACEHOLDER = None

    for l in range(depth):
        mL = A.mark()
        colp_t = A.alloc([128, NCOLP], F32)
        b_colp = Buf("colp")
        dma("sp", colp_t, colp[l], (), [b_colp])
        phase_norm(l, 0, colp_t=colp_t, b_colp=b_colp)
        R.barrier()
        if l == 0 and cfg.do_merge:
            zr = []
            if not cfg.do_rwkv:
                zr += list(range(0, 8))
            if not cfg.do_ssm:
                zr += list(range(8, 24))
            if not cfg.do_ret:
                zr += list(range(24, 32))
            if zr:
                mz = A.mark()
                zt = A.alloc([128, 512], BF16)
                b_zt = Buf("zt")
                memset("pool", zt, 0.0, [b_zt])
                for kc in zr:
                    for gi, (gs, gn) in enumerate(groups):
                        dma("act", s_y[kc * 128:(kc + 1) * 128, gs:gs + gn], zt[:, :gn], [b_zt], [sy_buf(kc, gi)])
                R.barrier()
                A.release(mz)
        if cfg.do_rwkv:
            R.phase = "L%d_rwkv" % l
            phase_rwkv(l, colp_t, b_colp)
            R.barrier()
        if cfg.do_ssm:
            R.phase = "L%d_ssm" % l
            phase_ssm(l, colp_t, b_colp)
            R.barrier()
        if cfg.do_ret:
            R.phase = "L%d_ret" % l
            phase_ret(l, colp_t, b_colp)
            R.barrier()
        def dump(slot):
            if debug:
                for gi, (gs, gn) in enumerate(groups):
                    dma("sp", dbg_h[slot].rearrange("(a p) t -> p a t", p=128)[:, :, gs:gs + gn], hT[:, :, gs:gs + gn],
                        [b_h[gi]], [Buf("dbg")])
                R.barrier()
        def zero_pad():
            if TP > L:
                memset("pool", hT[:, :, L:TP], 0.0, [b_h[len(groups) - 1]])
        if cfg.do_merge:
            R.phase = "L%d_merge" % l
            phase_merge(l, colp_t, b_colp)
            zero_pad()
            R.barrier()
        dump(l * 3)
        if cfg.do_ffn:
            R.phase = "L%d_ffn" % l
            phase_ffn(l, colp_t, b_colp)
            zero_pad()
            R.barrier()
        dump(l * 3 + 1)
        A.release(mL)

    m = A.mark()
    nf_t = A.alloc([128, 8], F32)
    b_nf = Buf("nf")
    dma("sp", nf_t, nfin, (), [b_nf])
    ot = A.alloc([128, 8, 512], F32)
    b_ot = Buf("ot")
    for gi, (gs, gn) in enumerate(groups):
        phase_norm(0, 0, gsel=[gi], dst=ot, dst_b=b_ot, colp_t=nf_t, b_colp=b_nf)
        lo = max(gs, NM)
        hi = min(gs + gn, L)
        if hi > lo:
            dma("act", outT.rearrange("(a p) t -> p a t", p=128)[:, :, lo - NM:hi - NM], ot[:, :, lo - gs:hi - gs],
                [b_ot], [Buf("out%d" % gi)])
    A.release(m)
    R.barrier()
    R.op("sp", lambda e: e.nop(), (), ())
    R.emit(nc, None)
    return nc


def _cols(v, n):
    return np.ascontiguousarray(np.asarray(v, np.float32).reshape(n, 128).T)


def _rep(v):
    return np.broadcast_to(np.asarray(v, np.float32).reshape(1, -1), (128, np.asarray(v).size))


_PERM = np.concatenate([np.arange(0, 64, 2), np.arange(1, 64, 2)])
_PART = np.concatenate([_PERM[32:], _PERM[:32]])


def prep_shared(cfg, inp):
    depth = cfg.depth
    f = lambda k: np.asarray(inp[k], np.float32)
    sh = {}
    ch = host_consts(cfg)
    for k in CONST_NAMES:
        sh["c_" + k] = np.ascontiguousarray(ch[k])
    w_in = f("w_in")[:depth]
    sh["w_in"] = np.ascontiguousarray(w_in)
    qk = np.empty((depth, D, 1024), np.float32)
    rot = np.empty((depth, D, 1024), np.float32)
    for part in range(2):
        for h in range(8):
            base = OFF_C + part * 512 + h * 64
            qk[:, :, part * 512 + h * 64:part * 512 + (h + 1) * 64] = w_in[:, :, base + _PERM]
            rot[:, :, part * 512 + h * 64:part * 512 + (h + 1) * 64] = w_in[:, :, base + _PART]
    sh["w_qk"] = qk
    sh["w_rot"] = rot
    nv = max(depth - 1, 1)
    wv = np.zeros((nv, D, 32), np.float32)
    rv2 = np.zeros((nv, 32, D), np.float32)
    if depth > 1:
        wv[:] = f("w_in_vres")[:depth - 1]
        rv2[:] = f("rwkv_v2")[:depth - 1]
    sh["w_vres"] = wv
    sh["rv2"] = rv2
    sh["w_branch"] = np.ascontiguousarray(f("w_branch")[:depth])
    sh["w_out"] = np.ascontiguousarray(f("w_out")[:depth])
    sh["w_up"] = np.ascontiguousarray(f("ffn_w_up")[:depth])
    sh["w_down"] = np.ascontiguousarray(f("ffn_w_down")[:depth])
    sh["rw2"] = np.ascontiguousarray(np.concatenate([f("rwkv_w2")[:depth], f("rwkv_a2")[:depth]], axis=1))
    sh["rg2"] = np.ascontiguousarray(f("rwkv_g2")[:depth])
    colp = []
    rowa = []
    rowb = []
    mu_all = []
    cw_all = []
    brow = []
    for l in range(depth):
        fw = f("ffn_conv_w")[l]
        colp.append(np.concatenate([
            _cols(f("norm_mix")[l], 8), _cols(f("norm_ffn")[l], 8), _cols(f("gate_bias")[l], 24),
            _cols(f("ssm_conv_b")[l][2048:3072], 8),
            _cols(fw[0], 44), _cols(fw[1], 44), _cols(fw[2], 44), _cols(f("ffn_conv_b")[l], 44),
            np.zeros((128, 1), np.float32)], axis=1))
        v0 = f("rwkv_v0")[l - 1] if l > 0 else np.zeros(1024, np.float32)
        rowa.append(np.concatenate([_rep(f("rwkv_w0")[l]), _rep(f("rwkv_a0")[l]), _rep(f("rwkv_k_k")[l]),
                                    _rep(f("rwkv_k_a")[l]), _rep(f("rwkv_r_k")[l].reshape(-1)),
                                    _rep(f("rwkv_ln_g")[l]), _rep(f("rwkv_ln_b")[l]), _rep(v0),
                                    np.zeros((128, 1024), np.float32)], axis=1))
        rowb.append(np.concatenate([_rep(f("ssm_norm_g")[l]), _rep(f("ssm_d")[l]), _rep(f("ssm_a_log")[l]),
                                    np.zeros((128, 32), np.float32)], axis=1))
        muv = f("rwkv_mu_vres")[l - 1] if l > 0 else np.zeros(32, np.float32)
        mu_all.append(_rep(np.concatenate([f("rwkv_mu")[l], muv])))
        cw_all.append(_rep(f("ssm_conv_w")[l].reshape(-1)))
        brow.append(np.concatenate([f("ssm_conv_b")[l], f("ssm_dt_bias")[l]]).reshape(1, -1))
    sh["colp"] = np.ascontiguousarray(np.stack(colp))
    sh["rowa"] = np.ascontiguousarray(np.stack(rowa))
    sh["rowb"] = np.ascontiguousarray(np.stack(rowb))
    sh["mu_all"] = np.ascontiguousarray(np.stack(mu_all))
    sh["cw_all"] = np.ascontiguousarray(np.stack(cw_all))
    sh["brow"] = np.ascontiguousarray(np.stack(brow))
    sh["nfin"] = _cols(f("norm_final"), 8)
    for n in STACKED:
        arr = sh.pop(n)
        for i in range(arr.shape[0]):
            sh["%s_%d" % (n, i)] = np.ascontiguousarray(arr[i])
    return sh


def run(cfg, inp, n_cores=8, debug=False):
    x = np.asarray(inp["x"], np.float32)
    meta = np.asarray(inp["meta"], np.float32)
    bsz = x.shape[0]
    sh = prep_shared(cfg, inp)
    nc = build(cfg, debug=debug)
    in_maps = []
    for b in range(bsz):
        xT = np.zeros((D, cfg.TP), np.float32)
        xT[:, :NM] = meta.T
        xT[:, NM:cfg.L] = x[b].T
        m = dict(sh)
        m["xT"] = xT
        in_maps.append(m)
    res = run_bass_kernel_spmd(nc, in_maps, core_ids=list(range(bsz)))
    out = np.stack([np.ascontiguousarray(r["outT"].T) for r in res.results], axis=0)
    if debug:
        return out.astype(np.float32), [(r["dbg_h"], r["s_y"]) for r in res.results]
    return out.astype(np.float32)


def kernel(**inputs):
    cfg = Cfg(inputs["x"].shape[1], 4)
    return run(cfg, inputs)
```
